# Optimizing a Trainium2 kernel written in Bass

```python
import jax, jax.numpy as jnp
from jax import lax
import numpy as np

D_MODEL = 1024
BATCH = 8
SEQ = 2048
DEPTH = 1

CHUNK = 64
N_META = 16
EPS = 1e-6

D_MIX = D_MODEL
M_HEADS = 4
M_HEAD_DIM = (D_MIX // 2) // M_HEADS
M_WIDTH = M_HEADS * M_HEAD_DIM
CONV_W = 4
F_HEADS = 8
F_HEAD_DIM = (D_MIX - M_WIDTH) // F_HEADS
F_WIDTH = F_HEADS * F_HEAD_DIM
Q_BLOCK = 128

PROJ_SIZES = (M_WIDTH, M_WIDTH, M_WIDTH, M_WIDTH, M_HEADS, M_HEADS, F_WIDTH, F_WIDTH, F_WIDTH, F_HEADS)
PROJ_DIM = 4 * M_WIDTH + 2 * M_HEADS + 3 * F_WIDTH + F_HEADS

PEER_HEADS = 8
N_KEYS = 128
N_EXPERTS = N_KEYS * N_KEYS
PEER_TOPK = 16
PEER_QDIM = 256
PEER_HALF = PEER_QDIM // 2
PEER_TOKEN_BLOCK = 128

kernel_name = "hybrid_mlstm_fox_peer_block"


def _rmsnorm(x, g):
    xf = x.astype(jnp.float32)
    y = xf * lax.rsqrt(jnp.mean(xf * xf, axis=-1, keepdims=True) + EPS)
    return (y * g.astype(jnp.float32)).astype(x.dtype)


def _proj_splits():
    out, acc = [], 0
    for s in PROJ_SIZES[:-1]:
        acc += s
        out.append(acc)
    return out


def _causal_dwconv(x, w):
    c = x.shape[-1]
    return lax.conv_general_dilated(
        x, w.astype(x.dtype)[:, None, :], window_strides=(1,),
        padding=[(CONV_W - 1, 0)], dimension_numbers=("NWC", "WIO", "NWC"),
        feature_group_count=c)


def _mlstm(q, k, v, o_pre, i_pre, f_pre, out_gain):
    f32 = jnp.float32
    B, L, _ = q.shape
    n_chunks = -(-L // CHUNK)
    pad = n_chunks * CHUNK - L

    def heads(t):
        t = jnp.pad(t.astype(f32), ((0, 0), (0, pad), (0, 0)))
        return t.reshape(B, n_chunks, CHUNK, M_HEADS, M_HEAD_DIM).transpose(1, 0, 3, 2, 4)

    def gates(t):
        t = jnp.pad(t.astype(f32), ((0, 0), (0, pad), (0, 0)))
        return t.reshape(B, n_chunks, CHUNK, M_HEADS).transpose(1, 0, 3, 2)

    qc = heads(q)
    kc = heads(k) * (M_HEAD_DIM ** -0.5)
    vc = heads(v)
    logi = gates(i_pre)
    logf = jax.nn.log_sigmoid(gates(f_pre))
    causal = jnp.asarray(np.tril(np.ones((CHUNK, CHUNK), dtype=bool)))

    def step(carry, inp):
        C, n, m = carry
        qb, kb, vb, li, lf = inp
        b = jnp.cumsum(lf, axis=-1)
        Dm = jnp.where(causal, b[..., :, None] - b[..., None, :] + li[..., None, :], -jnp.inf)
        m_inter = b + m[..., None]
        m_t = jnp.maximum(m_inter, jnp.max(Dm, axis=-1))
        S = jnp.einsum('bhtd,bhsd->bhts', qb, kb) * jnp.exp(Dm - m_t[..., None])
        w_inter = jnp.exp(m_inter - m_t)
        num = jnp.einsum('bhts,bhsd->bhtd', S, vb) + w_inter[..., None] * jnp.einsum('bhvk,bhtk->bhtv', C, qb)
        den = jnp.sum(S, axis=-1) + w_inter * jnp.einsum('bhk,bhtk->bht', n, qb)
        h = num / jnp.maximum(jnp.abs(den), jnp.exp(-m_t))[..., None]
        b_end = b[..., -1]
        g = b_end[..., None] - b + li
        m_new = jnp.maximum(b_end + m, jnp.max(g, axis=-1))
        decay = jnp.exp(b_end + m - m_new)
        wg = jnp.exp(g - m_new[..., None])
        C = decay[..., None, None] * C + jnp.einsum('bhs,bhsv,bhsk->bhvk', wg, vb, kb)
        n = decay[..., None] * n + jnp.einsum('bhs,bhsk->bhk', wg, kb)
        return (C, n, m_new), h

    init = (jnp.zeros((B, M_HEADS, M_HEAD_DIM, M_HEAD_DIM), f32),
            jnp.zeros((B, M_HEADS, M_HEAD_DIM), f32),
            jnp.zeros((B, M_HEADS), f32))
    _, h = lax.scan(step, init, (qc, kc, vc, logi, logf))
    h = h.transpose(1, 0, 3, 2, 4).reshape(B, n_chunks * CHUNK, M_HEADS, M_HEAD_DIM)[:, :L]
    h = _rmsnorm(h, out_gain.reshape(M_HEADS, M_HEAD_DIM)).reshape(B, L, M_WIDTH)
    h = h * jax.nn.sigmoid(o_pre.astype(f32))
    return h.astype(q.dtype)


def _forgetting_attention(q, k, v, f_pre, g_q, g_k):
    B, L, _ = q.shape

    def heads(t):
        return t.reshape(B, L, F_HEADS, F_HEAD_DIM).transpose(0, 2, 1, 3)

    qh = _rmsnorm(heads(q), g_q)
    kh = _rmsnorm(heads(k), g_k)
    vh = heads(v)
    c = jnp.cumsum(jax.nn.log_sigmoid(f_pre.astype(jnp.float32)), axis=1).transpose(0, 2, 1)
    scale = F_HEAD_DIM ** -0.5
    outs = []
    for start in range(0, L, Q_BLOCK):
        end = min(start + Q_BLOCK, L)
        logits = jnp.einsum('bhtd,bhsd->bhts', qh[:, :, start:end], kh[:, :, :end]).astype(jnp.float32) * scale
        logits = logits + c[:, :, start:end, None] - c[:, :, None, :end]
        mask = np.arange(start, end)[:, None] >= np.arange(end)[None, :]
        logits = jnp.where(mask, logits, -jnp.inf)
        p = jax.nn.softmax(logits, axis=-1).astype(vh.dtype)
        outs.append(jnp.einsum('bhts,bhsd->bhtd', p, vh[:, :, :end]))
    out = jnp.concatenate(outs, axis=2)
    return out.transpose(0, 2, 1, 3).reshape(B, L, F_WIDTH)


def _peer(x, w_query, sub_keys, u_emb, v_emb):
    B, L, D = x.shape
    T = B * L
    Tp = -(-T // PEER_TOKEN_BLOCK) * PEER_TOKEN_BLOCK
    xt = jnp.pad(x.reshape(T, D), ((0, Tp - T), (0, 0))).reshape(-1, PEER_TOKEN_BLOCK, D)
    keys32 = sub_keys.astype(jnp.float32)

    def block(xb):
        tb = xb.shape[0]
        qh = (xb @ w_query).astype(jnp.float32).reshape(tb, PEER_HEADS, 2, PEER_HALF)
        scores = jnp.einsum('thpc,hpnc->thpn', qh, keys32)
        val, idx = lax.top_k(scores, PEER_TOPK)
        cand = (val[:, :, 0, :, None] + val[:, :, 1, None, :]).reshape(tb, PEER_HEADS, PEER_TOPK * PEER_TOPK)
        cand_id = (idx[:, :, 0, :, None] * N_KEYS + idx[:, :, 1, None, :]).reshape(tb, PEER_HEADS, PEER_TOPK * PEER_TOPK)
        top_s, pos = lax.top_k(cand, PEER_TOPK)
        ids = jnp.take_along_axis(cand_id, pos, axis=-1).reshape(tb, PEER_HEADS * PEER_TOPK)
        gate = jax.nn.softmax(top_s, axis=-1).reshape(tb, PEER_HEADS * PEER_TOPK)
        u = u_emb[ids]
        act = jax.nn.gelu(jnp.einsum('ted,td->te', u, xb).astype(jnp.float32), approximate=False)
        w = (gate * act).astype(xb.dtype)
        return jnp.einsum('te,ted->td', w, v_emb[ids])

    out = lax.map(block, xt)
    return out.reshape(Tp, D)[:T].reshape(B, L, D)


def setup_inputs(seed: int = 0) -> dict:
    key = jax.random.key(seed)
    ks = jax.random.split(key, 20)
    f32 = jnp.float32
    nrm = lambda k, shape, s: jax.random.normal(k, shape, f32) * s
    fbias_m = jnp.broadcast_to(jnp.linspace(3.0, 6.0, M_HEADS, dtype=f32), (DEPTH, M_HEADS))
    fbias_f = jnp.broadcast_to(jnp.linspace(2.0, 6.0, F_HEADS, dtype=f32), (DEPTH, F_HEADS))
    return {
        "x": nrm(ks[0], (BATCH, SEQ, D_MODEL), 1.0),
        "meta_tokens": nrm(ks[1], (N_META, D_MODEL), 1.0),
        "norm_mix": 1.0 + nrm(ks[2], (DEPTH, D_MODEL), 0.02),
        "w_in": nrm(ks[3], (DEPTH, D_MODEL, PROJ_DIM), D_MODEL ** -0.5),
        "conv_qk": nrm(ks[4], (DEPTH, CONV_W, 2 * M_WIDTH), CONV_W ** -0.5),
        "b_igate": nrm(ks[5], (DEPTH, M_HEADS), 0.1),
        "b_fgate_m": fbias_m + nrm(ks[6], (DEPTH, M_HEADS), 0.1),
        "m_out_norm": 1.0 + nrm(ks[7], (DEPTH, M_WIDTH), 0.02),
        "b_fgate_f": fbias_f + nrm(ks[8], (DEPTH, F_HEADS), 0.1),
        "f_q_norm": 1.0 + nrm(ks[9], (DEPTH, F_HEAD_DIM), 0.02),
        "f_k_norm": 1.0 + nrm(ks[10], (DEPTH, F_HEAD_DIM), 0.02),
        "w_out": nrm(ks[11], (DEPTH, D_MIX, D_MODEL), D_MIX ** -0.5),
        "norm_ffn": 1.0 + nrm(ks[12], (DEPTH, D_MODEL), 0.02),
        "peer_query": nrm(ks[13], (DEPTH, D_MODEL, PEER_HEADS * PEER_QDIM), D_MODEL ** -0.5),
        "peer_sub_keys": nrm(ks[14], (DEPTH, PEER_HEADS, 2, N_KEYS, PEER_HALF), PEER_HALF ** -0.5),
        "peer_u": nrm(ks[15], (DEPTH, N_EXPERTS, D_MODEL), D_MODEL ** -0.5),
        "peer_v": nrm(ks[16], (DEPTH, N_EXPERTS, D_MODEL), 0.1),
    }


def reference(x, meta_tokens, norm_mix, w_in, conv_qk, b_igate, b_fgate_m, m_out_norm,
              b_fgate_f, f_q_norm, f_k_norm, w_out, norm_ffn, peer_query, peer_sub_keys,
              peer_u, peer_v):
    B = x.shape[0]
    meta = jnp.broadcast_to(meta_tokens.astype(x.dtype)[None], (B, N_META, D_MODEL))
    h_res = jnp.concatenate([meta, x], axis=1)
    splits = _proj_splits()
    for layer in range(DEPTH):
        hn = _rmsnorm(h_res, norm_mix[layer])
        z = hn @ w_in[layer]
        mq, mk, mv, mo, mi, mf, fq, fk, fv, ff = jnp.split(z, splits, axis=-1)
        qk = jax.nn.silu(_causal_dwconv(jnp.concatenate([mq, mk], axis=-1), conv_qk[layer]))
        mq, mk = jnp.split(qk, 2, axis=-1)
        y_m = _mlstm(mq, mk, mv, mo, mi + b_igate[layer], mf + b_fgate_m[layer], m_out_norm[layer])
        y_f = _forgetting_attention(fq, fk, fv, ff + b_fgate_f[layer], f_q_norm[layer], f_k_norm[layer])
        h_res = h_res + jnp.concatenate([y_m, y_f], axis=-1) @ w_out[layer]
        if layer == DEPTH - 1:
            h_res = h_res[:, N_META:]
        h_res = h_res + _peer(_rmsnorm(h_res, norm_ffn[layer]), peer_query[layer],
                              peer_sub_keys[layer], peer_u[layer], peer_v[layer])
    return h_res
```

```python
import numpy as np
import concourse.bass as bass
import concourse.mybir as mybir
from concourse.bass_utils import run_bass_kernel_spmd
from contextlib import ExitStack

F32 = mybir.dt.float32
BF16 = mybir.dt.bfloat16
AF = mybir.ActivationFunctionType
ALU = mybir.AluOpType
AX = mybir.AxisListType

ENGS = ("tensor", "vector", "scalar", "gpsimd", "sync")
import os
RELAX = os.environ.get("RELAX", "0") == "1"
VCLOCK = os.environ.get("VCLOCK", "1") == "1"

D = 1024
SEQ = 2048
NMETA = 16
L = SEQ + NMETA
LM = 2112
LP = 2176
NCH = 33
EPS = 1e-6
NEG = -1.0e30


class Prog:
    def __init__(self, nc, es):
        self.nc = nc
        self.es = es
        self.ops = {e: [] for e in ENGS}
        self.cnt = {e: 0 for e in ENGS}
        self.sem = {e: es.enter_context(nc.semaphore("s_" + e)) for e in ENGS}
        self.waited = {e: {} for e in ENGS}
        self.snap = {}
        self.dclock = {}
        self.last_w = {}
        self.readers = {}
        self.dsems = {}
        self.semobj = {("e", e): self.sem[e] for e in ENGS}
        self.nops = 0
        self.nwaits = 0

    @staticmethod
    def _merge(dst, src):
        for k, v in src.items():
            if dst.get(k, 0) < v:
                dst[k] = v

    def _deps(self, eng, reads, writes):
        deps = {}

        def add(d):
            if d is None:
                return
            k, v = d
            if deps.get(k, 0) < v:
                deps[k] = v
        me = ("e", eng)
        for k in reads:
            add(self.last_w.get(k))
        for k in writes:
            lw = self.last_w.get(k)
            if lw is not None and not (RELAX and lw[0] == me):
                add(lw)
            for sk, v in self.readers.get(k, {}).items():
                if RELAX and sk == me:
                    continue
                add((sk, v))
        out = []
        w = self.waited[eng]
        for sk, v in sorted(deps.items(), key=lambda kv: -kv[1]):
            if eng == "tensor" and sk == ("e", "tensor"):
                continue
            if sk[0] == "d":
                v = self.dsems[sk[1]][1]
            if w.get(sk, 0) < v:
                w[sk] = v
                out.append((sk, v))
                if VCLOCK:
                    if sk[0] == "d":
                        self._merge(w, self.dclock.get(sk[1], {}))
                    elif sk != me:
                        self._merge_snap(w, sk, v)
        self.nwaits += len(out)
        return out

    def _merge_snap(self, w, sk, v):
        s = self.snap.get((sk, v))
        if s is not None:
            own = w.get(("e", self._cur), 0)
            self._merge(w, s)
            if s.get(("e", self._cur), 0) > own:
                w[("e", self._cur)] = own

    def op(self, eng, meth, kw, reads=(), writes=()):
        fn = (meth, kw)
        self._cur = eng
        waits = self._deps(eng, reads, writes)
        self.cnt[eng] += 1
        n = self.cnt[eng]
        me = ("e", eng)
        self.ops[eng].append((waits, fn, (me, 1)))
        self.nops += 1
        if VCLOCK:
            s = dict(self.waited[eng])
            s[me] = n
            self.snap[(me, n)] = s
        for k in reads:
            self.readers.setdefault(k, {})[me] = n
        for k in writes:
            self.last_w[k] = (me, n)
            self.readers[k] = {}

    def dma(self, eng, slot, kw, reads=(), writes=()):
        fn = ("dma_start", kw)
        if slot not in self.dsems:
            s = self.es.enter_context(self.nc.semaphore("d_" + slot))
            self.dsems[slot] = [s, 0]
            self.semobj[("d", slot)] = s
        self._cur = eng
        waits = self._deps(eng, reads, writes)
        self.dsems[slot][1] += 16
        v = self.dsems[slot][1]
        me = ("d", slot)
        self.ops[eng].append((waits, fn, (me, 16)))
        self.nops += 1
        if VCLOCK:
            dc = self.dclock.setdefault(slot, {})
            own = dict(self.waited[eng])
            own.pop(("e", eng), None)
            self._merge(dc, own)
        for k in reads:
            self.readers.setdefault(k, {})[me] = v
        for k in writes:
            self.last_w[k] = (me, v)
            self.readers[k] = {}

    def barrier(self, exclude=()):
        cur = [(("e", e), self.cnt[e]) for e in ENGS if self.cnt[e] > 0]
        cur += [(("d", s), v[1]) for s, v in self.dsems.items() if v[1] > 0 and s not in exclude]
        for e in ENGS:
            w = self.waited[e]
            waits = []
            for sk, v in cur:
                if sk == ("e", e):
                    if e == "tensor":
                        continue
                if w.get(sk, 0) < v:
                    w[sk] = v
                    waits.append((sk, v))
            if waits:
                self.ops[e].append((waits, None, None))
        self.last_w = {k: v for k, v in self.last_w.items() if v[0][0] == "d" and v[0][1] in exclude}
        self.readers = {}
        self.snap = {}

    def emit(self):
        nc = self.nc
        with nc.Block() as block:
            for e in ENGS:
                ops = self.ops[e]
                if not ops:
                    continue
                semobj = self.semobj

                def body(eng, ops=ops):
                    for waits, fn, inc in ops:
                        for sk, v in waits:
                            eng.wait_ge(semobj[sk], v)
                        if fn is not None:
                            ins = getattr(eng, fn[0])(**fn[1])
                            ins.then_inc(semobj[inc[0]], inc[1])
                getattr(block, e)(body)
        self.ops = {e: [] for e in ENGS}


def build_nc(debug=None, stop=None):
    nc = bass.Bass("TRN2", target_bir_lowering=False)

    def din(name, shape):
        return nc.dram_tensor(name, list(shape), F32, kind="ExternalInput").ap()

    x = din("x", [SEQ, D])
    meta = din("meta_tokens", [NMETA, D])
    norm_mix = din("norm_mix", [1, D])
    w_in = din("w_in", [D, 3600])
    conv_qk = din("conv_qk", [4, 1024])
    b_igate = din("b_igate", [1, 4])
    b_fgate_m = din("b_fgate_m", [1, 4])
    m_out_norm = din("m_out_norm", [1, 512])
    b_fgate_f = din("b_fgate_f", [8, 1])
    f_q_norm = din("f_q_norm", [1, 64])
    f_k_norm = din("f_k_norm", [1, 64])
    w_out = din("w_out", [D, D])
    norm_ffn = din("norm_ffn", [1, D])
    peer_query = din("peer_query", [D, 2048])
    peer_keys = din("peer_sub_keys", [2048, 128])
    peer_u = din("peer_u", [16384, D])
    peer_v = din("peer_v", [16384, D])
    out = nc.dram_tensor("out", [SEQ, D], F32, kind="ExternalOutput").ap()
    ut_s = nc.dram_tensor("ut_s", [8, 128, 16384], BF16).ap()
    vb_s = nc.dram_tensor("vb_s", [16384, D], BF16).ap()

    dbg_out = {}
    VEX = ("vprep",)

    es = ExitStack()
    with es:
        P = Prog(nc, es)

        def sb(stack, name, shape, dt=F32):
            return stack.enter_context(nc.sbuf_tensor(name, list(shape), dt))

        def ps(stack, name, dt=F32):
            shape = [128, 512] if dt == F32 else [128, 1024]
            return stack.enter_context(nc.psum_tensor(name, shape, dt))

        def dump(name, ap, key, shape):
            if debug is None or name not in debug:
                return
            t = nc.dram_tensor("dbg_" + name, list(shape), ap.dtype, kind="ExternalOutput").ap()
            dbg_out[name] = t
            P.dma("sync", "dbg", dict(out=t, in_=ap), key if isinstance(key, list) else [key], ["dbg_" + name])

        V = lambda m, kw, r=(), w=(): P.op("vector", m, kw, r, w)
        A = lambda m, kw, r=(), w=(): P.op("scalar", m, kw, r, w)
        G = lambda m, kw, r=(), w=(): P.op("gpsimd", m, kw, r, w)
        T = lambda m, kw, r=(), w=(): P.op("tensor", m, kw, r, w)

        identf = sb(es, "identf", [128, 128])
        identb = sb(es, "identb", [128, 128], BF16)
        epsc = sb(es, "epsc", [128, 1])
        onec = sb(es, "onec", [128, 1])
        lnsc = sb(es, "lnsc", [128, 1])
        gffn_b = sb(es, "gffn_b", [128, D])

        G("iota", dict(out=identf[:], pattern=[[1, 128]], base=0, channel_multiplier=-1,
                           allow_small_or_imprecise_dtypes=True), w=["identf"])
        V("tensor_scalar", dict(out=identb[:], in0=identf[:], scalar1=0.0, scalar2=None, op0=ALU.is_equal), ["identf"], ["identb"])
        iotaf = identf
        G("memset", dict(ap=epsc[:], constant=EPS), w=["epsc"])
        G("memset", dict(ap=onec[:], constant=1.0), w=["onec"])
        G("memset", dict(ap=lnsc[:], constant=float(-0.5 * np.log(128.0))), w=["lnsc"])
        P.dma("sync", "c0", dict(out=gffn_b[:], in_=norm_ffn[0:1, :].partition_broadcast(128)), writes=["gffn_b"])

        def rstd_from_ss(ss_ap, dst_ap, n, r, keys_r, keys_w, tmp_ap):
            A("activation", dict(out=tmp_ap, in_=ss_ap, func=AF.Ln, bias=epsc[0:r, :], scale=1.0 / n), keys_r + ["epsc"], keys_w[:1])
            A("activation", dict(out=dst_ap, in_=tmp_ap, func=AF.Exp, scale=-0.5), keys_w[:1], keys_w[1:])


        import os as _os

        def gen_prep(ub, uts, ptu):
            k = 0
            for eg in range(32):
                b = eg % 2
                P.dma("gpsimd", f"ub{b}", dict(out=ub[b][:], in_=peer_u[512 * eg:512 * eg + 512, :].rearrange("(q p) d -> p q d", p=128)), writes=[f"ub{b}"])
                yield
                for q in range(4):
                    tb = ptu[k % 4]
                    kt = f"ptu{k % 4}"
                    for dc in range(8):
                        T("transpose", dict(out=tb[:, 128 * dc:128 * dc + 128], in_=ub[b][:, q, 128 * dc:128 * dc + 128], identity=identb[:, :]), [f"ub{b}", "identb"], [kt])
                        yield
                    if k % 2 == 0:
                        A("activation", dict(out=uts[b][:, :, 128 * q:128 * q + 128], in_=tb[:, :].rearrange("p (a e) -> p a e", a=8), func=AF.Copy), [kt], [f"uts{b}"])
                    else:
                        V("tensor_copy", dict(out=uts[b][:, :, 128 * q:128 * q + 128], in_=tb[:, :].rearrange("p (a e) -> p a e", a=8)), [kt], [f"uts{b}"])
                    yield
                    k += 1
                P.dma("sync", f"uts{b}", dict(out=ut_s[:, :, 512 * eg:512 * eg + 512].rearrange("dc p e -> p dc e"), in_=uts[b][:]), [f"uts{b}"], ["ut_s"])
                yield

        prep_gen = [None]

        def bg_step(n):
            g_ = prep_gen[0]
            if g_ is None:
                return
            for _ in range(n):
                try:
                    next(g_)
                except StopIteration:
                    prep_gen[0] = None
                    return

        yT = sb(es, "yT", [128, 8, LP], BF16)
        with ExitStack() as ms:
            causf = sb(ms, "causf", [128, 128])
            causb = sb(ms, "causb", [128, 128], BF16)
            negm = sb(ms, "negm", [128, 128])
            onesf = sb(ms, "onesf", [128, 128])
            gmix_b = sb(ms, "gmix_b", [128, D])
            V("tensor_scalar", dict(out=causf[:], in0=iotaf[:], scalar1=0.0, scalar2=None, op0=ALU.is_ge), ["identf"], ["causf"])
            V("tensor_copy", dict(out=causb[:], in_=causf[:]), ["causf"], ["causb"])
            V("tensor_scalar", dict(out=negm[:], in0=iotaf[:], scalar1=0.0, scalar2=NEG, op0=ALU.is_gt, op1=ALU.mult), ["identf"], ["negm"])
            V("tensor_scalar", dict(out=identf[:], in0=iotaf[:], scalar1=0.0, scalar2=None, op0=ALU.is_equal), ["identf", "identb", "causf", "negm"], ["identf"])
            G("memset", dict(ap=onesf[:], constant=1.0), w=["onesf"])
            P.dma("sync", "c0b", dict(out=gmix_b[:], in_=norm_mix[0:1, :].partition_broadcast(128)), writes=["gmix_b"])
            hnT = sb(ms, "hnT", [128, 8, LP], BF16)
            big1 = sb(ms, "big1", [128, 8, LP], BF16)
            wgate = sb(ms, "wgate", [128, 8, 16], BF16)
            P.dma("gpsimd", "c1", dict(out=wgate[:, :, 0:8], in_=w_in[:, 2048:2056].rearrange("(dc p) c -> p dc c", p=128)), writes=["wgate"])
            P.dma("gpsimd", "c1", dict(out=wgate[:, :, 8:16], in_=w_in[:, 3592:3600].rearrange("(dc p) c -> p dc c", p=128)), writes=["wgate"])

            def tile_cols(j):
                return (0, 16) if j == 0 else (16 + 128 * (j - 1), 128)

            pps = ExitStack()
            pps.__enter__()
            if not _os.environ.get("NOPREP"):
                ub_ = [sb(pps, f"ub{i}", [128, 4, D], BF16) for i in range(2)]
                uts_ = [sb(pps, f"uts{i}", [128, 8, 512], BF16) for i in range(2)]
                ptu_ = [ps(pps, f"ptu{i}", BF16) for i in range(4)]
                prep_gen[0] = gen_prep(ub_, uts_, ptu_)
            with ExitStack() as pa:
                xt = [sb(pa, f"xt{i}", [128, D]) for i in range(2)]
                junk = sb(pa, "junkA", [128, D], BF16)
                hn = [sb(pa, f"hn{i}", [128, D], BF16) for i in range(2)]
                ssA = [sb(pa, f"ssA{i}", [128, 3]) for i in range(2)]
                tpA = [ps(pa, f"tpA{i}", BF16) for i in range(2)]
                for j in range(17):
                    c0, r = tile_cols(j)
                    b = j % 2
                    xtb, hnb, ssb, tpb = xt[b], hn[b], ssA[b], tpA[b]
                    kx, kh, ks, kt = f"xt{b}", f"hn{b}", f"ssA{b}", f"tpA{b}"
                    if j == 0:
                        P.dma("sync", kx, dict(out=xtb[0:16, :], in_=meta[:, :]), writes=[kx])
                    else:
                        P.dma("sync", kx, dict(out=xtb[:, :], in_=x[128 * (j - 1):128 * j, :]), writes=[kx])
                    V("scalar_tensor_tensor", dict(out=junk[0:r, :], in0=xtb[0:r, :], scalar=1.0, in1=xtb[0:r, :], op0=ALU.mult, op1=ALU.mult, accum_out=ssb[0:r, 0:1]), [kx], ["junkA", ks + "a"])
                    rstd_from_ss(ssb[0:r, 0:1], ssb[0:r, 2:3], float(D), r, [ks + "a"], [ks + "b", ks + "c"], ssb[0:r, 1:2])
                    V("scalar_tensor_tensor", dict(out=hnb[0:r, :], in0=xtb[0:r, :], scalar=ssb[0:r, 2:3], in1=gmix_b[0:r, :], op0=ALU.mult, op1=ALU.mult), [kx, ks + "c", "gmix_b"], [kh])
                    for dc in range(8):
                        T("transpose", dict(out=tpb[:, dc * 128:dc * 128 + r], in_=hnb[0:r, dc * 128:(dc + 1) * 128], identity=identb[0:r, 0:r]), [kh, "identb"], [kt])
                    tpv = tpb[:, :].rearrange("p (a b) -> p a b", a=8)
                    A("activation", dict(out=hnT[:, :, c0:c0 + r], in_=tpv[:, :, 0:r], func=AF.Copy), [kt], [("hnT", j)])
                    bg_step(24)
                G("memset", dict(ap=hnT[:, :, L:LP], constant=0.0), w=[("hnT", 17)])
                P.barrier(exclude=VEX)
                P.emit()
            HN_ALL = [("hnT", j) for j in range(18)]
            dump("hnT", hnT[:, :, 0:L], HN_ALL, [128, 8, L])

            if stop == "A":
                P.barrier(exclude=VEX)
                P.emit()
                return nc, dbg_out

            CB = [(0, 512), (512, 512), (1024, 512), (1536, 512), (2048, 64)]
            with ExitStack() as pb:
                wg = [sb(pb, f"wg{i}", [128, 8, 512], BF16) for i in range(2)]
                cw = sb(pb, "cw", [128, 4, 8])
                zc = [sb(pb, f"zc{i}", [128, LM]) for i in range(2)]
                ac = [sb(pb, f"ac{i}", [128, LM]) for i in range(2)]
                pz = [ps(pb, f"pz{i}") for i in range(2)]
                for j in range(4):
                    P.dma("sync", "cw", dict(out=cw[:, j, :], in_=conv_qk[j:j + 1, :].rearrange("o (k c) -> c (o k)", c=128), allow_slow_non_contiguous=True), writes=["cw"])
                it = 0
                for g in range(2):
                    wgb = wg[g]
                    P.dma("gpsimd", f"wg{g}", dict(out=wgb[:], in_=w_in[:, 512 * g:512 * (g + 1)].rearrange("(dc p) c -> p dc c", p=128)), writes=[f"wg{g}"])
                    for ck in range(4):
                        idx = g * 4 + ck
                        b = idx % 2
                        zcb, acb = zc[b], ac[b]
                        for (c0, wdt) in CB:
                            pzb = pz[it % 2]
                            kp = f"pz{it % 2}"
                            it += 1
                            for dc in range(8):
                                T("matmul", dict(out=pzb[:, 0:wdt], lhsT=wgb[:, dc, ck * 128:(ck + 1) * 128], rhs=hnT[:, dc, c0:c0 + wdt], start=(dc == 0), stop=(dc == 7)), [f"wg{g}"] + HN_ALL, [kp])
                            A("activation", dict(out=zcb[:, c0:c0 + wdt], in_=pzb[:, 0:wdt], func=AF.Copy), [kp], [f"zc{b}"])
                            bg_step(22)
                        V("tensor_scalar", dict(out=acb[:, :], in0=zcb[:, :], scalar1=cw[:, 3, idx:idx + 1], scalar2=None, op0=ALU.mult), [f"zc{b}", "cw"], [f"ac{b}"])
                        for sh in (1, 2, 3):
                            V("scalar_tensor_tensor", dict(out=acb[:, sh:LM], in0=zcb[:, 0:LM - sh], scalar=cw[:, 3 - sh, idx:idx + 1], in1=acb[:, sh:LM], op0=ALU.mult, op1=ALU.add), [f"zc{b}", "cw", f"ac{b}"], [f"ac{b}"])
                        A("activation", dict(out=big1[:, idx, 0:LM], in_=acb[:, :], func=AF.Silu), [f"ac{b}"], [("qk", idx)])
                bg_step(100000)
                P.barrier(exclude=VEX)
                P.emit()
            pps.close()
            QK_ALL = [("qk", i) for i in range(8)]
            dump("qk", big1[:, :, 0:L], QK_ALL, [128, 8, L])
            if stop == "B":
                P.barrier(exclude=VEX)
                P.emit()
                return nc, dbg_out

            with ExitStack() as pc:
                wv = sb(pc, "wv", [128, 8, 512], BF16)
                wo = sb(pc, "wo", [128, 8, 512], BF16)
                P.dma("gpsimd", "wv", dict(out=wv[:], in_=w_in[:, 1024:1536].rearrange("(dc p) c -> p dc c", p=128)), writes=["wv"])
                P.dma("gpsimd", "wo", dict(out=wo[:], in_=w_in[:, 1536:2048].rearrange("(dc p) c -> p dc c", p=128)), writes=["wo"])
                bm_b = sb(pc, "bm_b", [128, 8])
                P.dma("sync", "c2", dict(out=bm_b[:, 0:4], in_=b_igate[0:1, :].partition_broadcast(128)), writes=["bm_b"])
                P.dma("sync", "c2", dict(out=bm_b[:, 4:8], in_=b_fgate_m[0:1, :].partition_broadcast(128)), writes=["bm_b"])
                mg_b = sb(pc, "mg_b", [128, 512])
                P.dma("sync", "c2", dict(out=mg_b[:], in_=m_out_norm[0:1, :].partition_broadcast(128)), writes=["mg_b"])
                CT = sb(pc, "CT", [128, 4, 129])
                CTb = sb(pc, "CTb", [128, 4, 129], BF16)
                mst = sb(pc, "mst", [128, 4])
                G("memset", dict(ap=CT[:], constant=0.0), w=[("CT", h) for h in range(4)])
                G("memset", dict(ap=CTb[:], constant=0.0), w=[("CTb", h) for h in range(4)])
                G("memset", dict(ap=mst[:], constant=0.0), w=["mst"])
                NB = 2
                vx = [sb(pc, f"vx{i}", [64, 4, 129], BF16) for i in range(NB)]
                for i in range(NB):
                    G("memset", dict(ap=vx[i][:], constant=1.0), w=[f"vx{i}"])
                og = [sb(pc, f"og{i}", [64, 512]) for i in range(NB)]
                sm = [sb(pc, f"sm{i}", [128, 64]) for i in range(NB)]
                dg = [sb(pc, f"dg{i}", [64, 4, 64]) for i in range(NB)]
                mk = [sb(pc, f"mk{i}", [64, 4, 64]) for i in range(NB)]
                spT = [sb(pc, f"spT{i}", [64, 64], BF16) for i in range(4)]
                tmpB = [sb(pc, f"tmpB{i}", [64, 129]) for i in range(2)]
                num = [sb(pc, f"num{i}", [64, 129]) for i in range(2)]
                junkc = sb(pc, "junkc", [64, 128], BF16)
                hs = [sb(pc, f"hs{i}", [64, 8]) for i in range(4)]
                ytm = [sb(pc, f"ytm{i}", [64, 4, 128], BF16) for i in range(NB)]
                kw = [sb(pc, f"kw{i}", [64, 128], BF16) for i in range(2)]
                pg = ps(pc, "pg")
                pv = ps(pc, "pv")
                po = ps(pc, "po")
                pqk = ps(pc, "pqk")
                pAB = [ps(pc, f"pAB{i}") for i in range(2)]
                pU = ps(pc, "pU")
                ptr = ps(pc, "ptr", BF16)

                SM = dict(gt=0, e1=8, lfn=12, a=16, ea=20, cm=24, amax=28, M=32, Mend=36, eM=40, w=44, wg=48, dec=52, emt=56, t0=60)
                import os
                CLIM = int(os.environ.get("CLIM", NCH))
                LVL = int(os.environ.get("LVL", 99))
                def chunk_ctx(c):
                    Tn = 64
                    c0 = 64 * c
                    b = c % NB
                    s_ = sm[b]
                    ksm = lambda nm, b=b: (f"sm{b}", nm)
                    col = lambda nm, w=4, s_=s_, r=None: s_[0:(Tn if r is None else r), SM[nm]:SM[nm] + w]
                    hk = HN_ALL
                    return Tn, c0, b, s_, ksm, col, hk

                def gen_pre(c):
                    Tn, c0, b, s_, ksm, col, hk = chunk_ctx(c)
                    if c < 32 and not _os.environ.get("NOPREP"):
                        for r in (2 * c, 2 * c + 1):
                            P.dma("gpsimd", "vprep", dict(out=vb_s[256 * r:256 * r + 256, :], in_=peer_v[256 * r:256 * r + 256, :]), writes=["vb_s"])
                            yield
                    for dc in range(8):
                        T("matmul", dict(out=pg[0:Tn, 0:8], lhsT=hnT[:, dc, c0:c0 + Tn], rhs=wgate[:, dc, 0:8], start=(dc == 0), stop=(dc == 7)), hk + ["wgate"], ["pg"])
                        yield
                    V("tensor_tensor", dict(out=col("gt", 8), in0=pg[0:Tn, 0:8], in1=bm_b[0:Tn, :], op=ALU.add), ["pg", "bm_b"], [ksm("gt")])
                    yield
                    A("activation", dict(out=col("e1"), in_=s_[0:Tn, 4:8], func=AF.Exp, scale=-1.0), [ksm("gt")], [ksm("e1")])
                    yield
                    A("activation", dict(out=col("lfn"), in_=col("e1"), func=AF.Ln, bias=onec[0:Tn, :], scale=1.0), [ksm("e1"), "onec"], [ksm("lfn")])
                    yield
                    T("matmul", dict(out=pg[0:Tn, 8:12], lhsT=causf[0:Tn, 0:Tn], rhs=col("lfn"), start=True, stop=True), [ksm("lfn"), "causf"], ["pg"])
                    yield
                    T("matmul", dict(out=pg[:, 16:20], lhsT=onesf[0:Tn, :], rhs=col("lfn"), start=True, stop=True), [ksm("lfn"), "onesf"], ["pg"])
                    yield
                    V("tensor_tensor", dict(out=col("a"), in0=s_[0:Tn, 0:4], in1=pg[0:Tn, 8:12], op=ALU.add), [ksm("gt"), "pg"], [ksm("a")])
                    yield
                    dgb, mkb = dg[b], mk[b]
                    V("tensor_tensor", dict(out=dgb[0:Tn, :, 0:Tn], in0=identf[0:Tn, 0:Tn].unsqueeze(1).broadcast_to([Tn, 4, Tn]), in1=col("a").unsqueeze(2).broadcast_to([Tn, 4, Tn]), op=ALU.mult), [ksm("a"), "identf"], [f"dg{b}"])
                    yield
                    abc = pg[:, 256:512].rearrange("p (h s) -> p h s", h=4)
                    for h in range(4):
                        T("matmul", dict(out=pg[:, 256 + 64 * h:256 + 64 * h + Tn], lhsT=onesf[0:Tn, :], rhs=dgb[0:Tn, h, 0:Tn], start=True, stop=True), [f"dg{b}", "onesf"], ["pg"])
                        yield
                    V("tensor_tensor", dict(out=mkb[0:Tn, :, 0:Tn], in0=abc[0:Tn, :, 0:Tn], in1=negm[0:Tn, 0:Tn].unsqueeze(1).broadcast_to([Tn, 4, Tn]), op=ALU.add), ["pg", "negm"], [f"mk{b}"])
                    yield
                    V("tensor_reduce", dict(out=col("cm"), in_=mkb[0:Tn, :, 0:Tn], axis=AX.X, op=ALU.max), [f"mk{b}"], [ksm("cm")])
                    yield
                    V("tensor_reduce", dict(out=s_[:, SM["amax"]:SM["amax"] + 4], in_=abc[:, :, 0:Tn], axis=AX.X, op=ALU.max), ["pg"], [ksm("amax")])
                    yield
                    A("activation", dict(out=col("ea"), in_=col("a"), func=AF.Exp, bias=lnsc[0:Tn, :], scale=1.0), [ksm("a"), "lnsc"], [ksm("ea")])
                    yield
                    V("tensor_tensor", dict(out=col("M"), in0=col("cm"), in1=mst[0:Tn, :], op=ALU.max), [ksm("cm"), "mst"], [ksm("M")])
                    yield
                    V("tensor_tensor", dict(out=s_[:, SM["Mend"]:SM["Mend"] + 4], in0=s_[:, SM["amax"]:SM["amax"] + 4], in1=mst[:, :], op=ALU.max), [ksm("amax"), "mst"], [ksm("Mend")])
                    yield
                    A("activation", dict(out=col("eM"), in_=col("M"), func=AF.Exp, scale=-1.0), [ksm("M")], [ksm("eM")])
                    yield
                    V("tensor_tensor", dict(out=col("w"), in0=mst[0:Tn, :], in1=col("M"), op=ALU.subtract), [ksm("M"), "mst"], [ksm("w")])
                    yield
                    A("activation", dict(out=col("w"), in_=col("w"), func=AF.Exp), [ksm("w")], [ksm("w")])
                    yield
                    V("tensor_tensor", dict(out=col("wg"), in0=col("a"), in1=col("Mend"), op=ALU.subtract), [ksm("a"), ksm("Mend")], [ksm("wg")])
                    yield
                    A("activation", dict(out=col("wg"), in_=col("wg"), func=AF.Exp, bias=lnsc[0:Tn, :], scale=1.0), [ksm("wg"), "lnsc"], [ksm("wg")])
                    yield
                    V("tensor_tensor", dict(out=s_[:, SM["dec"]:SM["dec"] + 4], in0=mst[:, :], in1=s_[:, SM["Mend"]:SM["Mend"] + 4], op=ALU.subtract), [ksm("Mend"), "mst"], [ksm("dec")])
                    yield
                    A("activation", dict(out=s_[:, SM["dec"]:SM["dec"] + 4], in_=s_[:, SM["dec"]:SM["dec"] + 4], func=AF.Exp), [ksm("dec")], [ksm("dec")])
                    yield
                    V("tensor_tensor", dict(out=col("emt"), in0=pg[0:Tn, 8:12], in1=col("M"), op=ALU.subtract), [ksm("M"), "pg"], [ksm("emt")])
                    yield
                    A("activation", dict(out=col("emt"), in_=col("emt"), func=AF.Exp), [ksm("emt")], [ksm("emt")])
                    yield
                    V("tensor_tensor", dict(out=mst[:, :], in0=s_[:, SM["Mend"]:SM["Mend"] + 4], in1=pg[:, 16:20], op=ALU.subtract), [ksm("Mend"), "pg", "mst"], ["mst"])
                    yield
                    vxb, ogb = vx[b], og[b]
                    for dc in range(8):
                        T("matmul", dict(out=pv[0:Tn, :], lhsT=hnT[:, dc, c0:c0 + Tn], rhs=wv[:, dc, :], start=(dc == 0), stop=(dc == 7)), hk + ["wv"], ["pv"])
                        yield
                    A("activation", dict(out=vxb[0:Tn, :, 0:128], in_=pv[0:Tn, :].rearrange("p (h d) -> p h d", h=4), func=AF.Copy), ["pv"], [f"vx{b}"])
                    yield
                    for dc in range(8):
                        T("matmul", dict(out=po[0:Tn, :], lhsT=hnT[:, dc, c0:c0 + Tn], rhs=wo[:, dc, :], start=(dc == 0), stop=(dc == 7)), hk + ["wo"], ["po"])
                        yield
                    A("activation", dict(out=ogb[0:Tn, :], in_=po[0:Tn, :], func=AF.Exp, scale=-1.0), ["po"], [f"og{b}"])
                    yield
                    V("tensor_scalar", dict(out=ogb[0:Tn, :], in0=ogb[0:Tn, :], scalar1=1.0, scalar2=None, op0=ALU.add), [f"og{b}"], [f"og{b}"])
                    yield
                    V("reciprocal", dict(out=ogb[0:Tn, :], in_=ogb[0:Tn, :]), [f"og{b}"], [f"og{b}"])
                    yield
                    V("tensor_tensor", dict(out=ogb[0:Tn, :], in0=ogb[0:Tn, :], in1=mg_b[0:Tn, :], op=ALU.mult), [f"og{b}", "mg_b"], [f"og{b}"])
                    yield
                    ytb = ytm[b]

                def gen_head(c, h):
                    Tn, c0, b, s_, ksm, col, hk = chunk_ctx(c)
                    vxb, ogb, ytb = vx[b], og[b], ytm[b]
                    hb = h % 2
                    qTc = big1[:, h, c0:c0 + Tn]
                    kTc = big1[:, 4 + h, c0:c0 + Tn]
                    sp = spT[h]
                    pab = pAB[hb]
                    hsb = hs[h]
                    khs = lambda nm, h=h: (f"hs{h}", nm)
                    T("matmul", dict(out=pqk[0:Tn, 64 * h:64 * h + Tn], lhsT=kTc, rhs=qTc, start=True, stop=True), [("qk", h), ("qk", 4 + h)], ["pqk"])
                    yield
                    V("scalar_tensor_tensor", dict(out=sp[0:Tn, 0:Tn], in0=pqk[0:Tn, 64 * h:64 * h + Tn], scalar=col("ea")[:, h:h + 1], in1=causf[0:Tn, 0:Tn], op0=ALU.mult, op1=ALU.mult), ["pqk", ksm("ea"), "causf"], [f"spT{h}"])
                    yield
                    T("matmul", dict(out=pab[0:Tn, 0:129], lhsT=sp[0:Tn, 0:Tn], rhs=vxb[0:Tn, h, :], start=True, stop=True), [f"spT{h}", f"vx{b}"], [f"pAB{hb}"])
                    yield
                    T("matmul", dict(out=pab[0:Tn, 256:385], lhsT=qTc, rhs=CTb[:, h, :], start=True, stop=True), [("qk", h), ("CTb", h)], [f"pAB{hb}"])
                    yield
                    tB, nm = tmpB[hb], num[hb]
                    A("activation", dict(out=tB[0:Tn, :], in_=pab[0:Tn, 256:385], func=AF.Copy, scale=col("w")[:, h:h + 1]), [f"pAB{hb}", ksm("w")], [f"tmpB{hb}"])
                    yield
                    V("scalar_tensor_tensor", dict(out=nm[0:Tn, :], in0=pab[0:Tn, 0:129], scalar=col("eM")[:, h:h + 1], in1=tB[0:Tn, :], op0=ALU.mult, op1=ALU.add), [f"pAB{hb}", f"tmpB{hb}", ksm("eM")], [f"num{hb}"])
                    yield
                    A("activation", dict(out=hsb[0:Tn, 7:8], in_=nm[0:Tn, 128:129], func=AF.Abs), [f"num{hb}"], [khs("abs")])
                    yield
                    V("tensor_scalar", dict(out=hsb[0:Tn, 0:1], in0=hsb[0:Tn, 7:8], scalar1=col("emt")[:, h:h + 1], scalar2=None, op0=ALU.max), [khs("abs"), ksm("emt")], [khs("den")])
                    yield
                    V("reciprocal", dict(out=hsb[0:Tn, 1:2], in_=hsb[0:Tn, 0:1]), [khs("den")], [khs("rden")])
                    yield
                    V("scalar_tensor_tensor", dict(out=junkc[0:Tn, :], in0=nm[0:Tn, 0:128], scalar=1.0, in1=nm[0:Tn, 0:128], op0=ALU.mult, op1=ALU.mult, accum_out=hsb[0:Tn, 2:3]), [f"num{hb}"], ["junkc", khs("ss")])
                    yield
                    V("tensor_scalar", dict(out=hsb[0:Tn, 3:4], in0=hsb[0:Tn, 2:3], scalar1=hsb[0:Tn, 1:2], scalar2=hsb[0:Tn, 1:2], op0=ALU.mult, op1=ALU.mult), [khs("ss"), khs("rden")], [khs("t1")])
                    yield
                    rstd_from_ss(hsb[0:Tn, 3:4], hsb[0:Tn, 5:6], 128.0, Tn, [khs("t1")], [khs("ln"), khs("rstd")], hsb[0:Tn, 4:5])
                    yield
                    V("tensor_tensor", dict(out=hsb[0:Tn, 6:7], in0=hsb[0:Tn, 5:6], in1=hsb[0:Tn, 1:2], op=ALU.mult), [khs("rstd"), khs("rden")], [khs("sc")])
                    yield
                    V("scalar_tensor_tensor", dict(out=ytb[0:Tn, h, :], in0=nm[0:Tn, 0:128], scalar=hsb[0:Tn, 6:7], in1=ogb[0:Tn, 128 * h:128 * (h + 1)], op0=ALU.mult, op1=ALU.mult), [f"num{hb}", khs("sc"), f"og{b}"], [(f"ytm{b}", h)])
                    yield
                    T("transpose", dict(out=ptr[:, 64 * h:64 * h + Tn], in_=ytb[0:Tn, h, :], identity=identb[0:Tn, 0:Tn]), [(f"ytm{b}", h), "identb"], ["ptr"])
                    yield
                    A("activation", dict(out=yT[:, h, c0:c0 + Tn], in_=ptr[:, 64 * h:64 * h + Tn], func=AF.Copy), ["ptr"], [("yT", h, c)])
                    yield
                    kwb = kw[hb]
                    T("transpose", dict(out=ptr[0:Tn, 512 + 128 * hb:512 + 128 * hb + 128], in_=kTc, identity=identb[:, :]), [("qk", 4 + h), "identb"], ["ptr"])
                    yield
                    A("activation", dict(out=kwb[0:Tn, :], in_=ptr[0:Tn, 512 + 128 * hb:512 + 128 * hb + 128], func=AF.Copy, scale=col("wg")[:, h:h + 1]), ["ptr", ksm("wg")], [f"kw{hb}"])
                    yield
                    T("matmul", dict(out=pU[:, 256 * hb:256 * hb + 129], lhsT=kwb[0:Tn, :], rhs=vxb[0:Tn, h, :], start=True, stop=True), [f"kw{hb}", f"vx{b}"], ["pU"])
                    yield
                    V("scalar_tensor_tensor", dict(out=CT[:, h, :], in0=CT[:, h, :], scalar=s_[:, SM["dec"] + h:SM["dec"] + h + 1], in1=pU[:, 256 * hb:256 * hb + 129], op0=ALU.mult, op1=ALU.add), [("CT", h), ksm("dec"), "pU"], [("CT", h)])
                    yield
                    A("activation", dict(out=CTb[:, h, :], in_=CT[:, h, :], func=AF.Copy), [("CT", h)], [("CTb", h)])
                    yield


                def run_rr(gens, bg=()):
                    gens = list(gens)
                    bg = list(bg)
                    while gens:
                        alive = []
                        for g_ in gens:
                            try:
                                next(g_)
                                alive.append(g_)
                            except StopIteration:
                                pass
                        gens = alive
                        for g_ in list(bg):
                            try:
                                next(g_)
                            except StopIteration:
                                bg.remove(g_)
                    return bg

                NCC = min(NCH, CLIM)
                run_rr([gen_pre(0)])
                for c in range(NCC):
                    bg = [gen_pre(c + 1)] if c + 1 < NCC else []
                    bg = run_rr([gen_head(c, 0), gen_head(c, 1)], bg)
                    bg = run_rr([gen_head(c, 2), gen_head(c, 3)], bg)
                    run_rr(bg)
                P.barrier(exclude=VEX)
                P.emit()
            dump("yTm", yT[:, 0:4, 0:L], [], [128, 4, L])
            if stop == "C":
                P.barrier(exclude=VEX)
                P.emit()
                return nc, dbg_out

            NBLK = 17
            kTe = sb(ms, "kTe", [70, 8, LP], BF16)
            fvx = sb(ms, "fvx", [128, NBLK, 8, 65], BF16)
            prow = sb(ms, "prow", [48, LP], BF16)
            qTe = big1
            with ExitStack() as pd:
                gq_b = sb(pd, "gq_b", [128, 64])
                gk_b = sb(pd, "gk_b", [128, 64])
                nbff = sb(pd, "nbff", [8, 1])
                P.dma("sync", "c3", dict(out=gq_b[:], in_=f_q_norm[0:1, :].partition_broadcast(128)), writes=["gq_b"])
                P.dma("sync", "c3", dict(out=gk_b[:], in_=f_k_norm[0:1, :].partition_broadcast(128)), writes=["gk_b"])
                P.dma("sync", "c3", dict(out=nbff[:], in_=b_fgate_f[:, :]), writes=["nbff"])
                bffb = sb(pd, "bffb", [128, 8])
                P.dma("sync", "c3", dict(out=bffb[:], in_=b_fgate_f.rearrange("h o -> o h").partition_broadcast(128)), writes=["bffb"])
                V("tensor_scalar", dict(out=gq_b[:], in0=gq_b[:], scalar1=0.125, scalar2=None, op0=ALU.mult), ["gq_b"], ["gq_b"])
                V("tensor_scalar", dict(out=nbff[:], in0=nbff[:], scalar1=-1.0, scalar2=None, op0=ALU.mult), ["nbff"], ["nbff"])
                G("memset", dict(ap=fvx[:], constant=1.0), w=["fvx"])
                G("memset", dict(ap=qTe[64:70, :, :], constant=1.0), w=["qTe_ext"])
                G("memset", dict(ap=kTe[64:70, :, :], constant=1.0), w=["kTe_ext"])
                wf = [sb(pd, f"wf{i}", [128, 8, 512], BF16) for i in range(2)]
                qs = [sb(pd, f"qs{i}", [128, 512]) for i in range(2)]
                sq = sb(pd, "sqD", [128, 512])
                qtm = [sb(pd, f"qtm{i}", [128, 512], BF16) for i in range(2)]
                ssd = [sb(pd, f"ssd{i}", [128, 24]) for i in range(2)]
                pq = [ps(pd, f"pq{i}") for i in range(2)]
                ptq = [ps(pd, f"ptq{i}", BF16) for i in range(2)]
                pgf = ps(pd, "pgf")
                lfd = sb(pd, "lfd", [128, NBLK, 8])
                cnd = sb(pd, "cnd", [128, NBLK, 8])
                tot = sb(pd, "tot", [128, NBLK + 1, 8])
                for i in range(NBLK):
                    c0 = 128 * i
                    for dc in range(8):
                        T("matmul", dict(out=pgf[:, 8 * i:8 * i + 8], lhsT=hnT[:, dc, c0:c0 + 128], rhs=wgate[:, dc, 8:16], start=(dc == 0), stop=(dc == 7)), HN_ALL + ["wgate"], ["pgf"])
                pgv = pgf[:, 0:8 * NBLK].rearrange("p (b h) -> p b h", h=8)
                V("tensor_tensor", dict(out=lfd[:], in0=pgv, in1=bffb[:, :].unsqueeze(1).broadcast_to([128, NBLK, 8]), op=ALU.add), ["pgf", "bffb"], ["lfd"])
                A("activation", dict(out=lfd[:], in_=lfd[:], func=AF.Exp, scale=-1.0), ["lfd"], ["lfd"])
                A("activation", dict(out=lfd[:], in_=lfd[:], func=AF.Ln, bias=onec[:, :], scale=1.0), ["lfd", "onec"], ["lfd"])
                lf2 = lfd[:, :, :].rearrange("p b h -> p (b h)")
                T("matmul", dict(out=pgf[:, 256:256 + 8 * NBLK], lhsT=causf[:, :], rhs=lf2, start=True, stop=True), ["lfd", "causf"], ["pgf"])
                V("tensor_copy", dict(out=cnd[:, :, :].rearrange("p b h -> p (b h)"), in_=pgf[:, 256:256 + 8 * NBLK]), ["pgf"], ["cnd"])
                T("matmul", dict(out=pgf[:, 256:256 + 8 * NBLK], lhsT=onesf[:, :], rhs=lf2, start=True, stop=True), ["lfd", "onesf", "cnd"], ["pgf"])
                G("memset", dict(ap=tot[:, 0, :], constant=0.0), w=["tot"])
                for i in range(NBLK):
                    V("tensor_tensor", dict(out=tot[:, i + 1, :], in0=tot[:, i, :], in1=pgf[:, 256 + 8 * i:256 + 8 * i + 8], op=ALU.add), ["tot", "pgf"], ["tot"])
                V("tensor_tensor", dict(out=cnd[:], in0=cnd[:], in1=tot[:, 0:NBLK, :], op=ALU.add), ["cnd", "tot"], ["cnd"])
                parts = sb(pd, "parts", [128, NBLK, 48], BF16)
                r1 = sb(pd, "r1d", [128, NBLK, 8])
                pv3 = lambda a, b_: parts[:, :, a:b_]
                V("tensor_copy", dict(out=pv3(0, 8), in_=cnd[:]), ["cnd"], ["parts0"])
                V("tensor_tensor", dict(out=r1[:], in0=cnd[:], in1=pv3(0, 8), op=ALU.subtract), ["cnd", "parts0"], ["r1d"])
                V("tensor_copy", dict(out=pv3(8, 16), in_=r1[:]), ["r1d"], ["parts1"])
                V("tensor_tensor", dict(out=r1[:], in0=r1[:], in1=pv3(8, 16), op=ALU.subtract), ["r1d", "parts1"], ["r1d"])
                V("tensor_copy", dict(out=pv3(16, 24), in_=r1[:]), ["r1d"], ["parts2"])
                V("tensor_scalar", dict(out=pv3(24, 48), in0=pv3(0, 24), scalar1=-1.0, scalar2=None, op0=ALU.mult), ["parts0", "parts1", "parts2"], ["parts3"])
                PK = ["parts0", "parts1", "parts2", "parts3"]
                for i in range(NBLK):
                    tpb = ptq[i % 2]
                    T("transpose", dict(out=tpb[0:48, 0:128], in_=parts[:, i, :], identity=identb[:, :]), PK + ["identb"], [f"ptq{i % 2}"])
                    A("activation", dict(out=prow[:, 128 * i:128 * i + 128], in_=tpb[0:48, 0:128], func=AF.Copy), [f"ptq{i % 2}"], ["prow"])
                for h in range(8):
                    for j in range(3):
                        P.dma("sync", "ext", dict(out=kTe[67 + j:68 + j, h, :], in_=prow[8 * j + h:8 * j + h + 1, :]), ["prow", "kTe_ext"], [("kTe_ext", h, j)])
                        P.dma("sync", "ext", dict(out=qTe[64 + j:65 + j, h, :], in_=prow[24 + 8 * j + h:24 + 8 * j + h + 1, :]), ["prow", "qTe_ext"], [("qTe_ext", h, j)])
                wsrc = [(2056, "q"), (2568, "k"), (3080, "v")]
                sq2 = [sq, sb(pd, "sqD2", [128, 512])]

                def gen_blk(wfb, kw_, kind, i, b):
                    c0 = 128 * i
                    pqb, qsb, qtb, ssb, tpb, sqb = pq[b], qs[b], qtm[b], ssd[b], ptq[b], sq2[b]
                    for dc in range(8):
                        T("matmul", dict(out=pqb[:, :], lhsT=hnT[:, dc, c0:c0 + 128], rhs=wfb[:, dc, :], start=(dc == 0), stop=(dc == 7)), HN_ALL + [kw_], [f"pq{b}"])
                        yield
                    if kind == "v":
                        A("activation", dict(out=fvx[:, i, :, 0:64], in_=pqb[:, :].rearrange("p (h d) -> p h d", h=8), func=AF.Copy), [f"pq{b}", "fvx"], [("fvx", i)])
                        yield
                        return
                    gb_ = gq_b if kind == "q" else gk_b
                    A("activation", dict(out=qsb[:, :], in_=pqb[:, :], func=AF.Copy), [f"pq{b}"], [f"qs{b}"])
                    yield
                    V("tensor_tensor", dict(out=sqb[:, :], in0=qsb[:, :], in1=qsb[:, :], op=ALU.mult), [f"qs{b}"], [f"sqD{b}"])
                    yield
                    V("tensor_reduce", dict(out=ssb[:, 0:8], in_=sqb[:, :].rearrange("p (h d) -> p h d", h=8), axis=AX.X, op=ALU.add), [f"sqD{b}"], [f"ssd{b}a"])
                    yield
                    rstd_from_ss(ssb[:, 0:8], ssb[:, 16:24], 64.0, 128, [f"ssd{b}a"], [f"ssd{b}b", f"ssd{b}c"], ssb[:, 8:16])
                    yield
                    q3 = qsb[:, :].rearrange("p (h d) -> p h d", h=8)
                    V("tensor_tensor", dict(out=q3, in0=q3, in1=ssb[:, 16:24].unsqueeze(2).broadcast_to([128, 8, 64]), op=ALU.mult), [f"qs{b}", f"ssd{b}c"], [f"qs{b}"])
                    yield
                    V("tensor_tensor", dict(out=qtb[:, :].rearrange("p (h d) -> p h d", h=8), in0=q3, in1=gb_[:, :].unsqueeze(1).broadcast_to([128, 8, 64]), op=ALU.mult), [f"qs{b}", "gq_b", "gk_b"], [f"qtm{b}"])
                    yield
                    for h in range(8):
                        T("transpose", dict(out=tpb[0:64, 128 * h:128 * h + 128], in_=qtb[:, 64 * h:64 * h + 64], identity=identb[:, :]), [f"qtm{b}", "identb"], [f"ptq{b}"])
                        yield
                    dst = qTe if kind == "q" else kTe
                    A("activation", dict(out=dst[0:64, :, c0:c0 + 128], in_=tpb[0:64, :].rearrange("p (h t) -> p h t", h=8), func=AF.Copy), [f"ptq{b}"], [(kind + "T", i)])
                    yield

                def run_rr2(gens):
                    gens = list(gens)
                    while gens:
                        alive = []
                        for g_ in gens:
                            try:
                                next(g_)
                                alive.append(g_)
                            except StopIteration:
                                pass
                        gens = alive

                for wi, (wc0, kind) in enumerate(wsrc):
                    wfb = wf[wi % 2]
                    kw_ = f"wf{wi % 2}"
                    P.dma("gpsimd", kw_, dict(out=wfb[:], in_=w_in[:, wc0:wc0 + 512].rearrange("(dc p) c -> p dc c", p=128)), writes=[kw_])
                    for i in range(0, NBLK, 2):
                        gl = [gen_blk(wfb, kw_, kind, i, 0)]
                        if i + 1 < NBLK:
                            gl.append(gen_blk(wfb, kw_, kind, i + 1, 1))
                        run_rr2(gl)
                P.barrier(exclude=VEX)
                P.emit()
            dump("qTe", qTe[0:70, :, 0:L], [], [70, 8, L])
            dump("kTe", kTe[0:70, :, 0:L], [], [70, 8, L])
            if stop == "D1":
                P.barrier(exclude=VEX)
                P.emit()
                return nc, dbg_out
            with ExitStack() as pd2:
                pT = [sb(pd2, f"pT{i}", [128, 512], BF16) for i in range(3)]
                ytf = [sb(pd2, f"ytf{i}", [128, 4, 512], BF16) for i in range(2)]
                rinv = [sb(pd2, f"rinv{i}", [128, 4]) for i in range(2)]
                lg = [ps(pd2, f"lg{i}") for i in range(2)]
                pacc = [ps(pd2, f"pacc{i}") for i in range(4)]
                ptr2 = [ps(pd2, f"ptr2{i}", BF16) for i in range(2)]
                it = 0
                for TS in range(5):
                    jl = [j for j in range(4 * TS, min(4 * TS + 4, NBLK))]
                    nj = len(jl)
                    t0 = 512 * TS
                    WT = 128 * nj
                    ytb = ytf[TS % 2]
                    for h in range(8):
                        rb = rinv[h % 2]
                        nI = jl[-1] + 1

                        def qk(i, itv):
                            b2 = itv % 2
                            ts_ = max(t0, 128 * i)
                            Wd = t0 + WT - ts_
                            T("matmul", dict(out=lg[b2][:, 0:Wd], lhsT=kTe[0:70, h, 128 * i:128 * i + 128], rhs=qTe[0:70, h, ts_:ts_ + Wd], start=True, stop=True), [], [f"lg{b2}"])

                        qk(0, it)
                        for i in range(nI):
                            b3 = it % 3
                            b2 = it % 2
                            ts_ = max(t0, 128 * i)
                            Wd = t0 + WT - ts_
                            A("activation", dict(out=pT[b3][:, 0:Wd], in_=lg[b2][:, 0:Wd], func=AF.Exp), [f"lg{b2}"], [f"pT{b3}"])
                            if 128 * i >= t0:
                                G("tensor_tensor", dict(out=pT[b3][:, 0:128], in0=pT[b3][:, 0:128], in1=causb[:, :], op=ALU.mult), [f"pT{b3}", "causb"], [f"pT{b3}"])
                            if i + 1 < nI:
                                qk(i + 1, it + 1)
                            for j in jl:
                                if j < i:
                                    continue
                                jj = j - 4 * TS
                                co = 128 * j - ts_
                                T("matmul", dict(out=pacc[jj][:, 0:65], lhsT=pT[b3][:, co:co + 128], rhs=fvx[:, i, h, :], start=(i == 0), stop=(i == j)), [f"pT{b3}"], [f"pacc{jj}"])
                            it += 1
                        for jj in range(nj):
                            V("reciprocal", dict(out=rb[:, jj:jj + 1], in_=pacc[jj][:, 64:65]), [f"pacc{jj}"], [(f"rinv{h % 2}", jj)])
                            V("tensor_scalar", dict(out=ytb[:, jj, 64 * h:64 * h + 64], in0=pacc[jj][:, 0:64], scalar1=rb[:, jj:jj + 1], scalar2=None, op0=ALU.mult), [f"pacc{jj}", (f"rinv{h % 2}", jj)], [(f"ytf{TS % 2}", h)])
                    for jj, j in enumerate(jl):
                        tb = ptr2[jj % 2]
                        for pr in range(4):
                            T("transpose", dict(out=tb[:, 128 * pr:128 * pr + 128], in_=ytb[:, jj, 128 * pr:128 * pr + 128], identity=identb[:, :]), [(f"ytf{TS % 2}", hh) for hh in range(8)] + ["identb"], [f"ptr2{jj % 2}"])
                        A("activation", dict(out=yT[:, 4:8, 128 * j:128 * j + 128], in_=tb[:, 0:512].rearrange("p (a t) -> p a t", a=4), func=AF.Copy), [f"ptr2{jj % 2}"], [("yTf", j)])
                P.barrier(exclude=VEX)
                P.emit()
            dump("yT", yT[:, :, 0:L], [], [128, 8, L])
            if stop == "D":
                P.barrier(exclude=VEX)
                P.emit()
                return nc, dbg_out


        with ExitStack() as pf:
            wo16 = sb(pf, "wo16", [128, 8, D], BF16)
            wq16 = sb(pf, "wq16", [128, 8, 2048], BF16)
            keysT = sb(pf, "keysT", [128, 16, 128])
            P.dma("gpsimd", "wo16", dict(out=wo16[:], in_=w_out.rearrange("(cc p) d -> p cc d", p=128)), writes=["wo16"])
            for hf in range(2):
                P.dma("gpsimd", "wq16", dict(out=wq16[:, :, 1024 * hf:1024 * hf + 1024], in_=peer_query[:, 1024 * hf:1024 * hf + 1024].rearrange("(dc p) c -> p dc c", p=128)), writes=["wq16"])
            pacc = [ps(pf, f"paccF{i}") for i in range(2)]
            ptx = ps(pf, "ptx", BF16)
            pS = ps(pf, "pS")
            pa = [ps(pf, f"pa{i}") for i in range(2)]
            pgt = [ps(pf, f"pgt{i}") for i in range(2)]
            with ExitStack() as pk:
                knat = sb(pk, "knat", [128, 16, 128])
                P.dma("sync", "knat", dict(out=knat[:], in_=peer_keys.rearrange("(hp n) c -> n hp c", n=128)), writes=["knat"])
                for hp in range(16):
                    T("transpose", dict(out=pS[:, 128 * (hp % 4):128 * (hp % 4) + 128], in_=knat[:, hp, :], identity=identf[:, :]), ["knat", "identf"], ["pS"])
                    if hp % 4 == 3:
                        A("activation", dict(out=keysT[:, hp - 3:hp + 1, :], in_=pS[:, :].rearrange("p (a n) -> p a n", a=4), func=AF.Copy), ["pS"], ["keysT"])
                P.barrier(exclude=VEX)
                P.emit()
            xt2 = sb(pf, "xt2", [128, D])
            h2t = sb(pf, "h2t", [128, D])
            xn16 = sb(pf, "xn16", [128, D], BF16)
            xnT = sb(pf, "xnT", [128, 8, 128], BF16)
            scr16 = sb(pf, "scr16", [128, 2048])
            qT_sb = scr16[:, 0:2048].rearrange("p (a t) -> p a t", a=16)
            apre = scr16[:, :].rearrange("p (s e) -> p s e", s=4)
            S_sb = sb(pf, "S_sb", [128, 16, 128])
            v1 = sb(pf, "v1", [128, 16, 16])
            wk = [sb(pf, f"wk{i}", [128, 128]) for i in range(4)]
            cand = [sb(pf, f"cand{i}", [128, 256]) for i in range(2)]
            wk2 = [sb(pf, f"wk2{i}", [128, 256]) for i in range(2)]
            ct = sb(pf, "ct", [128, 8, 16])
            sv = sb(pf, "sv", [128, 64])
            dd = sb(pf, "dd", [128, 8, 16])
            S1t = sb(pf, "S1t", [128, 8, 128])
            UT = [sb(pf, f"UT{i}", [128, 8, 512], BF16) for i in range(2)]
            Vt = [sb(pf, f"Vt{i}", [128, 4, D], BF16) for i in range(2)]
            gA = sb(pf, "gA", [128, 4, 512], BF16)
            tmps = [[sb(pf, f"tmp{p_}_{i}", [128, 512], BF16) for i in range(8)] for p_ in range(2)]
            zt = [sb(pf, f"zt{i}", [128, 4, 128]) for i in range(4)]
            ez = [sb(pf, f"ez{i}", [128, 512], BF16) for i in range(4)]
            wT = [sb(pf, f"wT{i}", [128, 4, 128], BF16) for i in range(2)]
            ADDENG = os.environ.get("ADDENG", "PPPAPPPA")
            FH = [h for h in range(8) if ADDENG[h] == "F"]
            S2e = sb(pf, "S2e", [128, max(1, len(FH)), 128 if FH else 1])
            EH = [h for h in range(8) if ADDENG[h] == "E"]
            SBe = sb(pf, "SBe", [128, max(1, len(EH)), 128 if EH else 1])
            ezf = [sb(pf, f"ezf{i}", [128, 512 if EH else 1]) for i in range(2)]
            DH = [h for h in range(8) if ADDENG[h] == "D"]
            E1d = sb(pf, "E1d", [128, max(1, len(DH)), 128 if DH else 1])
            E2d = sb(pf, "E2d", [128, max(1, len(DH)), 128 if DH else 1])
            Rn = sb(pf, "Rn", [128, max(1, len(FH)), 128 if FH else 1])
            import os
            NT = int(os.environ.get("NTILE", 16))
            zi = 0
            ei = [0]
            for j in range(NT):
                c0 = 16 + 128 * j
                if j == 0:
                    P.dma("sync", "xt2", dict(out=xt2[:, :], in_=x[0:128, :]), writes=["xt2"])
                for hf in range(2):
                    for cc in range(8):
                        T("matmul", dict(out=pacc[hf][:, :], lhsT=yT[:, cc, c0:c0 + 128], rhs=wo16[:, cc, 512 * hf:512 * hf + 512], start=(cc == 0), stop=(cc == 7)), ["wo16"], [f"paccF{hf}"])
                    V("tensor_tensor", dict(out=h2t[:, 512 * hf:512 * hf + 512], in0=pacc[hf][:, :], in1=xt2[:, 512 * hf:512 * hf + 512], op=ALU.add), [f"paccF{hf}", "xt2"], [("h2t", hf)])
                H2 = [("h2t", 0), ("h2t", 1)]
                if j + 1 < NT:
                    P.dma("sync", "xt2", dict(out=xt2[:, :], in_=x[128 * (j + 1):128 * (j + 2), :]), ["xt2"], ["xt2"])
                if debug is not None and "h2" in debug:
                    if j == 0:
                        dbg_h2 = nc.dram_tensor("dbg_h2", [128, 16, D], F32, kind="ExternalOutput").ap()
                        dbg_out["h2"] = dbg_h2
                    P.dma("sync", "dbgh2", dict(out=dbg_h2[:, j, :], in_=h2t[:, :]), H2, ["dbgh2"])
                V("scalar_tensor_tensor", dict(out=xn16[:, :], in0=h2t[:, :], scalar=1.0, in1=h2t[:, :], op0=ALU.mult, op1=ALU.mult, accum_out=sv[:, 0:1]), H2, ["xn16", ("sv", "ss")])
                rstd_from_ss(sv[:, 0:1], sv[:, 2:3], float(D), 128, [("sv", "ss")], [("sv", "ln"), ("sv", "rstd")], sv[:, 1:2])
                V("scalar_tensor_tensor", dict(out=xn16[:, :], in0=h2t[:, :], scalar=sv[:, 2:3], in1=gffn_b[:, :], op0=ALU.mult, op1=ALU.mult), H2 + [("sv", "rstd"), "gffn_b"], ["xn16"])
                for dc in range(8):
                    T("transpose", dict(out=ptx[:, 128 * dc:128 * dc + 128], in_=xn16[:, 128 * dc:128 * dc + 128], identity=identb[:, :]), ["xn16", "identb"], ["ptx"])
                A("activation", dict(out=xnT[:, :, :], in_=ptx[:, :].rearrange("p (a t) -> p a t", a=8), func=AF.Copy), ["ptx"], ["xnT"])
                rot = [(pS, "pS"), (pa[0], "pa0"), (pa[1], "pa1")]
                for g4 in range(4):
                    pb_, pk_ = rot[g4 % 3]
                    for cq in range(4):
                        cc = 4 * g4 + cq
                        for dc in range(8):
                            T("matmul", dict(out=pb_[:, 128 * cq:128 * cq + 128], lhsT=wq16[:, dc, 128 * cc:128 * cc + 128], rhs=xnT[:, dc, :], start=(dc == 0), stop=(dc == 7)), ["wq16", "xnT"], [pk_])
                    A("activation", dict(out=qT_sb[:, 4 * g4:4 * g4 + 4, :], in_=pb_[:, :].rearrange("p (a t) -> p a t", a=4), func=AF.Copy), [pk_], [("scr", g4)])
                for g4 in range(4):
                    pb_, pk_ = rot[(g4 + 1) % 3]
                    for hq in range(4):
                        hp = 4 * g4 + hq
                        T("matmul", dict(out=pb_[:, 128 * hq:128 * hq + 128], lhsT=qT_sb[:, hp, :], rhs=keysT[:, hp, :], start=True, stop=True), [("scr", g4), "keysT"], [pk_])
                    A("activation", dict(out=S_sb[:, 4 * g4:4 * g4 + 4, :], in_=pb_[:, :].rearrange("p (a t) -> p a t", a=4), func=AF.Copy), [pk_], [("S_sb", g4)])
                SK = [("S_sb", g) for g in range(4)]
                for hp0 in range(0, 16, 4):
                    grp = range(hp0, hp0 + 4)
                    for hp in grp:
                        V("max", dict(out=v1[:, hp, 0:8], in_=S_sb[:, hp, :]), SK, [("v1", hp, 0)])
                    for hp in grp:
                        V("match_replace", dict(out=wk[hp % 4][:, :], in_to_replace=v1[:, hp, 0:8], in_values=S_sb[:, hp, :], imm_value=NEG), SK + [("v1", hp, 0)], [f"wk{hp % 4}"])
                    for hp in grp:
                        V("max", dict(out=v1[:, hp, 8:16], in_=wk[hp % 4][:, :]), [f"wk{hp % 4}"], [("v1", hp, 1)])
                for h0 in range(0, 8, 2):
                    grp = range(h0, h0 + 2)
                    for h in grp:
                        V("tensor_tensor", dict(out=cand[h % 2][:, :].rearrange("p (a b) -> p a b", a=16), in0=v1[:, 2 * h, :].unsqueeze(2).broadcast_to([128, 16, 16]), in1=v1[:, 2 * h + 1, :].unsqueeze(1).broadcast_to([128, 16, 16]), op=ALU.add),
                          [("v1", 2 * h, 0), ("v1", 2 * h, 1), ("v1", 2 * h + 1, 0), ("v1", 2 * h + 1, 1)], [f"cand{h % 2}"])
                    for h in grp:
                        V("max", dict(out=ct[:, h, 0:8], in_=cand[h % 2][:, :]), [f"cand{h % 2}"], [("ct", h, 0)])
                    for h in grp:
                        V("match_replace", dict(out=wk2[h % 2][:, :], in_to_replace=ct[:, h, 0:8], in_values=cand[h % 2][:, :], imm_value=NEG), [f"cand{h % 2}", ("ct", h, 0)], [f"wk2{h % 2}"])
                    for h in grp:
                        V("max", dict(out=ct[:, h, 8:16], in_=wk2[h % 2][:, :]), [f"wk2{h % 2}"], [("ct", h, 1)])
                CTK = [("ct", h, k_) for h in range(8) for k_ in range(2)]
                tau = ct[:, :, 15]
                mx_ = ct[:, :, 0]
                V("tensor_tensor", dict(out=dd[:, :, :], in0=ct[:, :, :], in1=ct[:, :, 0:1].broadcast_to([128, 8, 16]), op=ALU.subtract), CTK, ["dd"])
                A("activation", dict(out=dd[:, :, :], in_=dd[:, :, :], func=AF.Exp), ["dd"], ["dd"])
                V("tensor_reduce", dict(out=sv[:, 8:16], in_=dd[:, :, :], axis=AX.X, op=ALU.add), ["dd"], [("sv", "Z")])
                A("activation", dict(out=sv[:, 16:24], in_=sv[:, 8:16], func=AF.Ln), [("sv", "Z")], [("sv", "lnZ")])
                V("tensor_tensor", dict(out=sv[:, 24:32], in0=sv[:, 16:24], in1=mx_, op=ALU.add), [("sv", "lnZ")] + CTK, [("sv", "cst")])
                V("tensor_tensor", dict(out=sv[:, 32:40], in0=tau, in1=sv[:, 24:32], op=ALU.subtract), [("sv", "cst")] + CTK, [("sv", "bias")])
                S4 = S_sb[:, :, :].rearrange("p (h two) n -> p h two n", two=2)
                V("tensor_tensor", dict(out=S1t[:, :, :], in0=S4[:, :, 0, :], in1=ct[:, :, 15:16].broadcast_to([128, 8, 128]), op=ALU.subtract), SK + CTK, ["S1t"])
                V("tensor_scalar", dict(out=S1t[:, :, :], in0=S1t[:, :, :], scalar1=8.0e-6, scalar2=None, op0=ALU.add), ["S1t"], ["S1t"])
                for k_, h in enumerate(FH):
                    V("tensor_scalar", dict(out=S2e[:, k_, :], in0=S_sb[:, 2 * h + 1, :], scalar1=sv[:, 32 + h:33 + h], scalar2=None, op0=ALU.add), SK + [("sv", "bias")], ["S2e"])
                    V("tensor_scalar", dict(out=Rn[:, k_, :], in0=S1t[:, h, :], scalar1=-1.0, scalar2=None, op0=ALU.mult), ["S1t"], ["Rn"])
                if EH:
                    A("activation", dict(out=sv[:, 40:48], in_=sv[:, 32:40], func=AF.Exp), [("sv", "bias")], [("sv", "thr")])
                for k_, h in enumerate(EH):
                    V("tensor_scalar", dict(out=SBe[:, k_, :], in0=S1t[:, h, :], scalar1=sv[:, 32 + h:33 + h], scalar2=None, op0=ALU.add), ["S1t", ("sv", "bias")], ["SBe"])
                for k_, h in enumerate(DH):
                    A("activation", dict(out=E1d[:, k_, :], in_=S1t[:, h, :], func=AF.Exp), ["S1t"], ["E1d"])
                    A("activation", dict(out=E2d[:, k_, :], in_=S_sb[:, 2 * h + 1, :], func=AF.Exp, bias=sv[:, 32 + h:33 + h], scale=1.0), SK + [("sv", "bias")], ["E2d"])
                NIB = 32
                NDUM = int(os.environ.get("NDUM", 0))

                def loadUT(ibp):
                    b_ = ibp % 2
                    P.dma("sync", f"UT{b_}", dict(out=UT[b_][:], in_=ut_s[:, :, 512 * ibp:512 * ibp + 512].rearrange("dc p e -> p dc e")), writes=[f"UT{b_}"])

                def loadVt(ib):
                    b_ = ib % 2
                    P.dma("sync", f"Vt{b_}", dict(out=Vt[b_][:], in_=vb_s[512 * ib:512 * ib + 512, :].rearrange("(q p) d -> p q d", p=128)), ["vb_s"], [f"Vt{b_}"])

                def computeA(ibp):
                    b_ = ibp % 2
                    for q in range(4):
                        for dc in range(8):
                            T("matmul", dict(out=pa[b_][:, 128 * q:128 * q + 128], lhsT=UT[b_][:, dc, 128 * q:128 * q + 128], rhs=xnT[:, dc, :], start=(dc == 0), stop=(dc == 7)), [f"UT{b_}", "xnT"], [f"pa{b_}"])

                def evacA(ibp):
                    b_ = ibp % 2
                    A("activation", dict(out=apre[:, ibp % 4, :], in_=pa[b_][:, :], func=AF.Copy), [f"pa{b_}"], [("scr", ibp % 4)])

                def gelu_burst(ibs):
                    for ibp in ibs:
                        A("activation", dict(out=gA[:, ibp % 4, :], in_=apre[:, ibp % 4, :], func=AF.Gelu), [("scr", ibp % 4)], [("gA", ibp % 4)])

                def stageG(ib):
                    nonlocal zi
                    p_ = ib % 2
                    for h in range(8):
                        zb = zi % 4
                        zi += 1
                        ae = ADDENG[h]
                        if ae == "E":
                            k_ = EH.index(h)
                            eb = ei[0] % 2
                            ei[0] += 1
                            for q in range(4):
                                A("activation", dict(out=ezf[eb][:, 128 * q:128 * q + 128], in_=S_sb[:, 2 * h + 1, :], func=AF.Exp, bias=SBe[:, k_, 4 * ib + q:4 * ib + q + 1], scale=1.0), SK + ["SBe"], [f"ezf{eb}"])
                            V("scalar_tensor_tensor", dict(out=tmps[p_][h][:, :], in0=ezf[eb][:, :], scalar=sv[:, 40 + h:41 + h], in1=ezf[eb][:, :], op0=ALU.is_ge, op1=ALU.mult), [f"ezf{eb}", ("sv", "thr")], [f"tmp{p_}_{h}"])
                            continue
                        if ae == "F":
                            k_ = FH.index(h)
                            for q in range(4):
                                A("activation", dict(out=ez[zb][:, 128 * q:128 * q + 128], in_=S2e[:, k_, :], func=AF.Exp, bias=S1t[:, h, 4 * ib + q:4 * ib + q + 1], scale=1.0), ["S2e", "S1t"], [f"ez{zb}"])
                            for q in range(4):
                                V("scalar_tensor_tensor", dict(out=tmps[p_][h][:, 128 * q:128 * q + 128], in0=S_sb[:, 2 * h + 1, :], scalar=Rn[:, k_, 4 * ib + q:4 * ib + q + 1], in1=ez[zb][:, 128 * q:128 * q + 128], op0=ALU.is_ge, op1=ALU.mult), SK + ["Rn", f"ez{zb}"], [f"tmp{p_}_{h}"])
                            continue
                        if ae == "A":
                            for q in range(4):
                                A("activation", dict(out=zt[zb][:, q, :], in_=S_sb[:, 2 * h + 1, :], func=AF.Identity, bias=S1t[:, h, 4 * ib + q:4 * ib + q + 1], scale=1.0), SK + ["S1t"], [f"zt{zb}"])
                        else:
                            addeng = G if ae in ("P", "D") else V
                            addeng("tensor_tensor", dict(out=zt[zb][:, :, :], in0=S_sb[:, 2 * h + 1, :].unsqueeze(1).broadcast_to([128, 4, 128]), in1=S1t[:, h, 4 * ib:4 * ib + 4].unsqueeze(2).broadcast_to([128, 4, 128]), op=ALU.add), SK + ["S1t"], [f"zt{zb}"])
                        if ae == "D":
                            k_ = DH.index(h)
                            V("tensor_tensor", dict(out=ez[zb][:, :].rearrange("p (a n) -> p a n", a=4), in0=E2d[:, k_, :].unsqueeze(1).broadcast_to([128, 4, 128]), in1=E1d[:, k_, 4 * ib:4 * ib + 4].unsqueeze(2).broadcast_to([128, 4, 128]), op=ALU.mult), ["E1d", "E2d"], [f"ez{zb}"])
                        else:
                            A("activation", dict(out=ez[zb][:, :], in_=zt[zb][:, :, :].rearrange("p a n -> p (a n)"), func=AF.Exp, bias=sv[:, 32 + h:33 + h], scale=1.0), [f"zt{zb}", ("sv", "bias")], [f"ez{zb}"])
                        V("scalar_tensor_tensor", dict(out=tmps[p_][h][:, :], in0=zt[zb][:, :, :].rearrange("p a n -> p (a n)"), scalar=0.0, in1=ez[zb][:, :], op0=ALU.is_ge, op1=ALU.mult), [f"zt{zb}", f"ez{zb}"], [f"tmp{p_}_{h}"])

                def stageT(ib):
                    p_ = ib % 2
                    for _d in range(NDUM):
                        T("matmul", dict(out=pS[:, :], lhsT=identb[:, :], rhs=gA[:, _d % 4, :], start=True, stop=True), [], ["pSdummy"])
                    for q in range(4):
                        for h in range(8):
                            T("matmul", dict(out=pgt[p_][:, 128 * q:128 * q + 128], lhsT=tmps[p_][h][:, 128 * q:128 * q + 128], rhs=identb[:, :], start=(h == 0), stop=(h == 7)), [f"tmp{p_}_{h}", "identb"], [f"pgt{p_}"])

                def stageW(ib):
                    p_ = ib % 2
                    V("tensor_tensor", dict(out=wT[p_][:, :, :].rearrange("p a t -> p (a t)"), in0=pgt[p_][:, :], in1=gA[:, ib % 4, :], op=ALU.mult), [f"pgt{p_}", ("gA", ib % 4)], [f"wT{p_}"])
                    for q in range(4):
                        for hf in range(2):
                            T("matmul", dict(out=pacc[hf][:, :], lhsT=wT[p_][:, q, :], rhs=Vt[p_][:, q, 512 * hf:512 * hf + 512], start=(ib == 0 and q == 0), stop=(ib == NIB - 1 and q == 3)), [f"wT{p_}", f"Vt{p_}"], [f"paccF{hf}"])

                loadUT(0)
                loadUT(1)
                computeA(0)
                evacA(0)
                loadUT(2)
                computeA(1)
                evacA(1)
                for k in range(NIB + 2):
                    if k + 3 < NIB:
                        loadUT(k + 3)
                    if 0 <= k - 1 < NIB:
                        loadVt(k - 1)
                    if k % 4 == 2:
                        gelu_burst(range(k - 2, min(k + 2, NIB)))
                    if 0 <= k - 2 < NIB:
                        stageW(k - 2)
                    if 0 <= k - 1 < NIB:
                        stageT(k - 1)
                    if k + 2 < NIB:
                        computeA(k + 2)
                    if k < NIB:
                        stageG(k)
                    if k + 2 < NIB:
                        evacA(k + 2)
                SCR = [("scr", s_) for s_ in range(4)]
                for hf in range(2):
                    V("tensor_tensor", dict(out=scr16[:, 512 * hf:512 * hf + 512], in0=pacc[hf][:, :], in1=h2t[:, 512 * hf:512 * hf + 512], op=ALU.add), [f"paccF{hf}"] + H2, SCR)
                P.dma("sync", "ost", dict(out=out[128 * j:128 * j + 128, :], in_=scr16[:, 0:1024]), SCR, SCR + [("out", j)])
            P.barrier()
            P.emit()
        P.barrier()
        P.emit()
    return nc, dbg_out

_CACHE = {}


def kernel(**inputs):
    n = 8
    if "nc" not in _CACHE:
        _CACHE["nc"] = build_nc()[0]
    nc = _CACHE["nc"]
    f = lambda a: np.ascontiguousarray(np.asarray(a, dtype=np.float32))
    x = f(inputs["x"])
    shared = {
        "meta_tokens": f(inputs["meta_tokens"]),
        "norm_mix": f(inputs["norm_mix"]).reshape(1, D),
        "w_in": f(inputs["w_in"]).reshape(D, 3600),
        "conv_qk": f(inputs["conv_qk"]).reshape(4, 1024),
        "b_igate": f(inputs["b_igate"]).reshape(1, 4),
        "b_fgate_m": f(inputs["b_fgate_m"]).reshape(1, 4),
        "m_out_norm": f(inputs["m_out_norm"]).reshape(1, 512),
        "b_fgate_f": f(inputs["b_fgate_f"]).reshape(8, 1),
        "f_q_norm": f(inputs["f_q_norm"]).reshape(1, 64),
        "f_k_norm": f(inputs["f_k_norm"]).reshape(1, 64),
        "w_out": f(inputs["w_out"]).reshape(D, D),
        "norm_ffn": f(inputs["norm_ffn"]).reshape(1, D),
        "peer_query": f(inputs["peer_query"]).reshape(D, 2048),
        "peer_sub_keys": f(inputs["peer_sub_keys"]).reshape(2048, 128),
        "peer_u": f(inputs["peer_u"]).reshape(16384, D),
        "peer_v": f(inputs["peer_v"]).reshape(16384, D),
    }
    in_maps = [dict(shared, x=x[b]) for b in range(n)]
    res = run_bass_kernel_spmd(nc, in_maps, core_ids=list(range(n)))
    return np.stack([np.asarray(res.results[b]["out"], dtype=np.float32) for b in range(n)], axis=0)
```

```python
import numpy as np
import concourse.bass as bass
import concourse.mybir as mybir
from concourse.bass_utils import run_bass_kernel_spmd
from contextlib import ExitStack

F32 = mybir.dt.float32
BF16 = mybir.dt.bfloat16
AF = mybir.ActivationFunctionType
ALU = mybir.AluOpType
AX = mybir.AxisListType

ENGS = ("tensor", "vector", "scalar", "gpsimd", "sync")
import os
RELAX = os.environ.get("RELAX", "0") == "1"
VCLOCK = os.environ.get("VCLOCK", "1") == "1"

D = 1024
SEQ = 2048
NMETA = 16
L = SEQ + NMETA
LM = 2112
LP = 2176
NCH = 33
EPS = 1e-6
NEG = -1.0e30


class Prog:
    def __init__(self, nc, es):
        self.nc = nc
        self.es = es
        self.ops = {e: [] for e in ENGS}
        self.cnt = {e: 0 for e in ENGS}
        self.sem = {e: es.enter_context(nc.semaphore("s_" + e)) for e in ENGS}
        self.waited = {e: {} for e in ENGS}
        self.snap = {}
        self.dclock = {}
        self.last_w = {}
        self.readers = {}
        self.dsems = {}
        self.semobj = {("e", e): self.sem[e] for e in ENGS}
        self.nops = 0
        self.nwaits = 0

    @staticmethod
    def _merge(dst, src):
        for k, v in src.items():
            if dst.get(k, 0) < v:
                dst[k] = v

    def _deps(self, eng, reads, writes):
        deps = {}

        def add(d):
            if d is None:
                return
            k, v = d
            if deps.get(k, 0) < v:
                deps[k] = v
        me = ("e", eng)
        for k in reads:
            add(self.last_w.get(k))
        for k in writes:
            lw = self.last_w.get(k)
            if lw is not None and not (RELAX and lw[0] == me):
                add(lw)
            for sk, v in self.readers.get(k, {}).items():
                if RELAX and sk == me:
                    continue
                add((sk, v))
        out = []
        w = self.waited[eng]
        for sk, v in sorted(deps.items(), key=lambda kv: -kv[1]):
            if eng == "tensor" and sk == ("e", "tensor"):
                continue
            if sk[0] == "d":
                v = self.dsems[sk[1]][1]
            if w.get(sk, 0) < v:
                w[sk] = v
                out.append((sk, v))
                if VCLOCK:
                    if sk[0] == "d":
                        self._merge(w, self.dclock.get(sk[1], {}))
                    elif sk != me:
                        self._merge_snap(w, sk, v)
        self.nwaits += len(out)
        return out

    def _merge_snap(self, w, sk, v):
        s = self.snap.get((sk, v))
        if s is not None:
            own = w.get(("e", self._cur), 0)
            self._merge(w, s)
            if s.get(("e", self._cur), 0) > own:
                w[("e", self._cur)] = own

    def op(self, eng, meth, kw, reads=(), writes=()):
        fn = (meth, kw)
        self._cur = eng
        waits = self._deps(eng, reads, writes)
        self.cnt[eng] += 1
        n = self.cnt[eng]
        me = ("e", eng)
        self.ops[eng].append((waits, fn, (me, 1)))
        self.nops += 1
        if VCLOCK:
            s = dict(self.waited[eng])
            s[me] = n
            self.snap[(me, n)] = s
        for k in reads:
            self.readers.setdefault(k, {})[me] = n
        for k in writes:
            self.last_w[k] = (me, n)
            self.readers[k] = {}

    def dma(self, eng, slot, kw, reads=(), writes=()):
        fn = ("dma_start", kw)
        if slot not in self.dsems:
            s = self.es.enter_context(self.nc.semaphore("d_" + slot))
            self.dsems[slot] = [s, 0]
            self.semobj[("d", slot)] = s
        self._cur = eng
        waits = self._deps(eng, reads, writes)
        self.dsems[slot][1] += 16
        v = self.dsems[slot][1]
        me = ("d", slot)
        self.ops[eng].append((waits, fn, (me, 16)))
        self.nops += 1
        if VCLOCK:
            dc = self.dclock.setdefault(slot, {})
            own = dict(self.waited[eng])
            own.pop(("e", eng), None)
            self._merge(dc, own)
        for k in reads:
            self.readers.setdefault(k, {})[me] = v
        for k in writes:
            self.last_w[k] = (me, v)
            self.readers[k] = {}

    def barrier(self, exclude=()):
        cur = [(("e", e), self.cnt[e]) for e in ENGS if self.cnt[e] > 0]
        cur += [(("d", s), v[1]) for s, v in self.dsems.items() if v[1] > 0 and s not in exclude]
        for e in ENGS:
            w = self.waited[e]
            waits = []
            for sk, v in cur:
                if sk == ("e", e):
                    if e == "tensor":
                        continue
                if w.get(sk, 0) < v:
                    w[sk] = v
                    waits.append((sk, v))
            if waits:
                self.ops[e].append((waits, None, None))
        self.last_w = {k: v for k, v in self.last_w.items() if v[0][0] == "d" and v[0][1] in exclude}
        self.readers = {}
        self.snap = {}

    def emit(self):
        nc = self.nc
        with nc.Block() as block:
            for e in ENGS:
                ops = self.ops[e]
                if not ops:
                    continue
                semobj = self.semobj

                def body(eng, ops=ops):
                    for waits, fn, inc in ops:
                        for sk, v in waits:
                            eng.wait_ge(semobj[sk], v)
                        if fn is not None:
                            ins = getattr(eng, fn[0])(**fn[1])
                            ins.then_inc(semobj[inc[0]], inc[1])
                getattr(block, e)(body)
        self.ops = {e: [] for e in ENGS}


def build_nc(debug=None, stop=None):
    nc = bass.Bass("TRN2", target_bir_lowering=False)

    def din(name, shape):
        return nc.dram_tensor(name, list(shape), F32, kind="ExternalInput").ap()

    x = din("x", [SEQ, D])
    meta = din("meta_tokens", [NMETA, D])
    norm_mix = din("norm_mix", [1, D])
    w_in = din("w_in", [D, 3600])
    conv_qk = din("conv_qk", [4, 1024])
    b_igate = din("b_igate", [1, 4])
    b_fgate_m = din("b_fgate_m", [1, 4])
    m_out_norm = din("m_out_norm", [1, 512])
    b_fgate_f = din("b_fgate_f", [8, 1])
    f_q_norm = din("f_q_norm", [1, 64])
    f_k_norm = din("f_k_norm", [1, 64])
    w_out = din("w_out", [D, D])
    norm_ffn = din("norm_ffn", [1, D])
    peer_query = din("peer_query", [D, 2048])
    peer_keys = din("peer_sub_keys", [2048, 128])
    peer_u = din("peer_u", [16384, D])
    peer_v = din("peer_v", [16384, D])
    out = nc.dram_tensor("out", [SEQ, D], F32, kind="ExternalOutput").ap()
    ut_s = nc.dram_tensor("ut_s", [8, 128, 16384], BF16).ap()
    vb_s = nc.dram_tensor("vb_s", [16384, D], BF16).ap()

    dbg_out = {}
    VEX = ("vprep",)

    es = ExitStack()
    with es:
        P = Prog(nc, es)

        def sb(stack, name, shape, dt=F32):
            return stack.enter_context(nc.sbuf_tensor(name, list(shape), dt))

        def ps(stack, name, dt=F32):
            shape = [128, 512] if dt == F32 else [128, 1024]
            return stack.enter_context(nc.psum_tensor(name, shape, dt))

        def dump(name, ap, key, shape):
            if debug is None or name not in debug:
                return
            t = nc.dram_tensor("dbg_" + name, list(shape), ap.dtype, kind="ExternalOutput").ap()
            dbg_out[name] = t
            P.dma("sync", "dbg", dict(out=t, in_=ap), key if isinstance(key, list) else [key], ["dbg_" + name])

        V = lambda m, kw, r=(), w=(): P.op("vector", m, kw, r, w)
        A = lambda m, kw, r=(), w=(): P.op("scalar", m, kw, r, w)
        G = lambda m, kw, r=(), w=(): P.op("gpsimd", m, kw, r, w)
        T = lambda m, kw, r=(), w=(): P.op("tensor", m, kw, r, w)

        identf = sb(es, "identf", [128, 128])
        identb = sb(es, "identb", [128, 128], BF16)
        epsc = sb(es, "epsc", [128, 1])
        onec = sb(es, "onec", [128, 1])
        lnsc = sb(es, "lnsc", [128, 1])
        gffn_b = sb(es, "gffn_b", [128, D])

        G("iota", dict(out=identf[:], pattern=[[1, 128]], base=0, channel_multiplier=-1,
                           allow_small_or_imprecise_dtypes=True), w=["identf"])
        V("tensor_scalar", dict(out=identb[:], in0=identf[:], scalar1=0.0, scalar2=None, op0=ALU.is_equal), ["identf"], ["identb"])
        iotaf = identf
        G("memset", dict(ap=epsc[:], constant=EPS), w=["epsc"])
        G("memset", dict(ap=onec[:], constant=1.0), w=["onec"])
        G("memset", dict(ap=lnsc[:], constant=float(-0.5 * np.log(128.0))), w=["lnsc"])
        P.dma("sync", "c0", dict(out=gffn_b[:], in_=norm_ffn[0:1, :].partition_broadcast(128)), writes=["gffn_b"])

        def rstd_from_ss(ss_ap, dst_ap, n, r, keys_r, keys_w, tmp_ap):
            A("activation", dict(out=tmp_ap, in_=ss_ap, func=AF.Ln, bias=epsc[0:r, :], scale=1.0 / n), keys_r + ["epsc"], keys_w[:1])
            A("activation", dict(out=dst_ap, in_=tmp_ap, func=AF.Exp, scale=-0.5), keys_w[:1], keys_w[1:])


        import os as _os

        def gen_prep(ub, uts, ptu):
            k = 0
            for eg in range(32):
                b = eg % 2
                P.dma("gpsimd", f"ub{b}", dict(out=ub[b][:], in_=peer_u[512 * eg:512 * eg + 512, :].rearrange("(q p) d -> p q d", p=128)), writes=[f"ub{b}"])
                yield
                for q in range(4):
                    tb = ptu[k % 4]
                    kt = f"ptu{k % 4}"
                    for dc in range(8):
                        T("transpose", dict(out=tb[:, 128 * dc:128 * dc + 128], in_=ub[b][:, q, 128 * dc:128 * dc + 128], identity=identb[:, :]), [f"ub{b}", "identb"], [kt])
                        yield
                    if k % 2 == 0:
                        A("activation", dict(out=uts[b][:, :, 128 * q:128 * q + 128], in_=tb[:, :].rearrange("p (a e) -> p a e", a=8), func=AF.Copy), [kt], [f"uts{b}"])
                    else:
                        V("tensor_copy", dict(out=uts[b][:, :, 128 * q:128 * q + 128], in_=tb[:, :].rearrange("p (a e) -> p a e", a=8)), [kt], [f"uts{b}"])
                    yield
                    k += 1
                P.dma("sync", f"uts{b}", dict(out=ut_s[:, :, 512 * eg:512 * eg + 512].rearrange("dc p e -> p dc e"), in_=uts[b][:]), [f"uts{b}"], ["ut_s"])
                yield

        prep_gen = [None]

        def bg_step(n):
            g_ = prep_gen[0]
            if g_ is None:
                return
            for _ in range(n):
                try:
                    next(g_)
                except StopIteration:
                    prep_gen[0] = None
                    return

        yT = sb(es, "yT", [128, 8, LP], BF16)
        with ExitStack() as ms:
            causf = sb(ms, "causf", [128, 128])
            causb = sb(ms, "causb", [128, 128], BF16)
            negm = sb(ms, "negm", [128, 128])
            onesf = sb(ms, "onesf", [128, 128])
            gmix_b = sb(ms, "gmix_b", [128, D])
            V("tensor_scalar", dict(out=causf[:], in0=iotaf[:], scalar1=0.0, scalar2=None, op0=ALU.is_ge), ["identf"], ["causf"])
            V("tensor_copy", dict(out=causb[:], in_=causf[:]), ["causf"], ["causb"])
            V("tensor_scalar", dict(out=negm[:], in0=iotaf[:], scalar1=0.0, scalar2=NEG, op0=ALU.is_gt, op1=ALU.mult), ["identf"], ["negm"])
            V("tensor_scalar", dict(out=identf[:], in0=iotaf[:], scalar1=0.0, scalar2=None, op0=ALU.is_equal), ["identf", "identb", "causf", "negm"], ["identf"])
            G("memset", dict(ap=onesf[:], constant=1.0), w=["onesf"])
            P.dma("sync", "c0b", dict(out=gmix_b[:], in_=norm_mix[0:1, :].partition_broadcast(128)), writes=["gmix_b"])
            hnT = sb(ms, "hnT", [128, 8, LP], BF16)
            big1 = sb(ms, "big1", [128, 8, LP], BF16)
            wgate = sb(ms, "wgate", [128, 8, 16], BF16)
            P.dma("gpsimd", "c1", dict(out=wgate[:, :, 0:8], in_=w_in[:, 2048:2056].rearrange("(dc p) c -> p dc c", p=128)), writes=["wgate"])
            P.dma("gpsimd", "c1", dict(out=wgate[:, :, 8:16], in_=w_in[:, 3592:3600].rearrange("(dc p) c -> p dc c", p=128)), writes=["wgate"])

            def tile_cols(j):
                return (0, 16) if j == 0 else (16 + 128 * (j - 1), 128)

            pps = ExitStack()
            pps.__enter__()
            if not _os.environ.get("NOPREP"):
                ub_ = [sb(pps, f"ub{i}", [128, 4, D], BF16) for i in range(2)]
                uts_ = [sb(pps, f"uts{i}", [128, 8, 512], BF16) for i in range(2)]
                ptu_ = [ps(pps, f"ptu{i}", BF16) for i in range(4)]
                prep_gen[0] = gen_prep(ub_, uts_, ptu_)
            with ExitStack() as pa:
                xt = [sb(pa, f"xt{i}", [128, D]) for i in range(2)]
                junk = sb(pa, "junkA", [128, D], BF16)
                hn = [sb(pa, f"hn{i}", [128, D], BF16) for i in range(2)]
                ssA = [sb(pa, f"ssA{i}", [128, 3]) for i in range(2)]
                tpA = [ps(pa, f"tpA{i}", BF16) for i in range(2)]
                for j in range(17):
                    c0, r = tile_cols(j)
                    b = j % 2
                    xtb, hnb, ssb, tpb = xt[b], hn[b], ssA[b], tpA[b]
                    kx, kh, ks, kt = f"xt{b}", f"hn{b}", f"ssA{b}", f"tpA{b}"
                    if j == 0:
                        P.dma("sync", kx, dict(out=xtb[0:16, :], in_=meta[:, :]), writes=[kx])
                    else:
                        P.dma("sync", kx, dict(out=xtb[:, :], in_=x[128 * (j - 1):128 * j, :]), writes=[kx])
                    V("scalar_tensor_tensor", dict(out=junk[0:r, :], in0=xtb[0:r, :], scalar=1.0, in1=xtb[0:r, :], op0=ALU.mult, op1=ALU.mult, accum_out=ssb[0:r, 0:1]), [kx], ["junkA", ks + "a"])
                    rstd_from_ss(ssb[0:r, 0:1], ssb[0:r, 2:3], float(D), r, [ks + "a"], [ks + "b", ks + "c"], ssb[0:r, 1:2])
                    V("scalar_tensor_tensor", dict(out=hnb[0:r, :], in0=xtb[0:r, :], scalar=ssb[0:r, 2:3], in1=gmix_b[0:r, :], op0=ALU.mult, op1=ALU.mult), [kx, ks + "c", "gmix_b"], [kh])
                    for dc in range(8):
                        T("transpose", dict(out=tpb[:, dc * 128:dc * 128 + r], in_=hnb[0:r, dc * 128:(dc + 1) * 128], identity=identb[0:r, 0:r]), [kh, "identb"], [kt])
                    tpv = tpb[:, :].rearrange("p (a b) -> p a b", a=8)
                    A("activation", dict(out=hnT[:, :, c0:c0 + r], in_=tpv[:, :, 0:r], func=AF.Copy), [kt], [("hnT", j)])
                    bg_step(24)
                G("memset", dict(ap=hnT[:, :, L:LP], constant=0.0), w=[("hnT", 17)])
                P.barrier(exclude=VEX)
                P.emit()
            HN_ALL = [("hnT", j) for j in range(18)]
            dump("hnT", hnT[:, :, 0:L], HN_ALL, [128, 8, L])

            if stop == "A":
                P.barrier(exclude=VEX)
                P.emit()
                return nc, dbg_out

            CB = [(0, 512), (512, 512), (1024, 512), (1536, 512), (2048, 64)]
            with ExitStack() as pb:
                wg = [sb(pb, f"wg{i}", [128, 8, 512], BF16) for i in range(2)]
                cw = sb(pb, "cw", [128, 4, 8])
                zc = [sb(pb, f"zc{i}", [128, LM]) for i in range(2)]
                ac = [sb(pb, f"ac{i}", [128, LM]) for i in range(2)]
                pz = [ps(pb, f"pz{i}") for i in range(2)]
                for j in range(4):
                    P.dma("sync", "cw", dict(out=cw[:, j, :], in_=conv_qk[j:j + 1, :].rearrange("o (k c) -> c (o k)", c=128), allow_slow_non_contiguous=True), writes=["cw"])
                it = 0
                for g in range(2):
                    wgb = wg[g]
                    P.dma("gpsimd", f"wg{g}", dict(out=wgb[:], in_=w_in[:, 512 * g:512 * (g + 1)].rearrange("(dc p) c -> p dc c", p=128)), writes=[f"wg{g}"])
                    for ck in range(4):
                        idx = g * 4 + ck
                        b = idx % 2
                        zcb, acb = zc[b], ac[b]
                        for (c0, wdt) in CB:
                            pzb = pz[it % 2]
                            kp = f"pz{it % 2}"
                            it += 1
                            for dc in range(8):
                                T("matmul", dict(out=pzb[:, 0:wdt], lhsT=wgb[:, dc, ck * 128:(ck + 1) * 128], rhs=hnT[:, dc, c0:c0 + wdt], start=(dc == 0), stop=(dc == 7)), [f"wg{g}"] + HN_ALL, [kp])
                            A("activation", dict(out=zcb[:, c0:c0 + wdt], in_=pzb[:, 0:wdt], func=AF.Copy), [kp], [f"zc{b}"])
                            bg_step(22)
                        V("tensor_scalar", dict(out=acb[:, :], in0=zcb[:, :], scalar1=cw[:, 3, idx:idx + 1], scalar2=None, op0=ALU.mult), [f"zc{b}", "cw"], [f"ac{b}"])
                        for sh in (1, 2, 3):
                            V("scalar_tensor_tensor", dict(out=acb[:, sh:LM], in0=zcb[:, 0:LM - sh], scalar=cw[:, 3 - sh, idx:idx + 1], in1=acb[:, sh:LM], op0=ALU.mult, op1=ALU.add), [f"zc{b}", "cw", f"ac{b}"], [f"ac{b}"])
                        A("activation", dict(out=big1[:, idx, 0:LM], in_=acb[:, :], func=AF.Silu), [f"ac{b}"], [("qk", idx)])
                bg_step(100000)
                P.barrier(exclude=VEX)
                P.emit()
            pps.close()
            QK_ALL = [("qk", i) for i in range(8)]
            dump("qk", big1[:, :, 0:L], QK_ALL, [128, 8, L])
            if stop == "B":
                P.barrier(exclude=VEX)
                P.emit()
                return nc, dbg_out

            with ExitStack() as pc:
                wv = sb(pc, "wv", [128, 8, 512], BF16)
                wo = sb(pc, "wo", [128, 8, 512], BF16)
                P.dma("gpsimd", "wv", dict(out=wv[:], in_=w_in[:, 1024:1536].rearrange("(dc p) c -> p dc c", p=128)), writes=["wv"])
                P.dma("gpsimd", "wo", dict(out=wo[:], in_=w_in[:, 1536:2048].rearrange("(dc p) c -> p dc c", p=128)), writes=["wo"])
                bm_b = sb(pc, "bm_b", [128, 8])
                P.dma("sync", "c2", dict(out=bm_b[:, 0:4], in_=b_igate[0:1, :].partition_broadcast(128)), writes=["bm_b"])
                P.dma("sync", "c2", dict(out=bm_b[:, 4:8], in_=b_fgate_m[0:1, :].partition_broadcast(128)), writes=["bm_b"])
                mg_b = sb(pc, "mg_b", [128, 512])
                P.dma("sync", "c2", dict(out=mg_b[:], in_=m_out_norm[0:1, :].partition_broadcast(128)), writes=["mg_b"])
                CT = sb(pc, "CT", [128, 4, 129])
                CTb = sb(pc, "CTb", [128, 4, 129], BF16)
                mst = sb(pc, "mst", [128, 4])
                G("memset", dict(ap=CT[:], constant=0.0), w=[("CT", h) for h in range(4)])
                G("memset", dict(ap=CTb[:], constant=0.0), w=[("CTb", h) for h in range(4)])
                G("memset", dict(ap=mst[:], constant=0.0), w=["mst"])
                NB = 2
                vx = [sb(pc, f"vx{i}", [64, 4, 129], BF16) for i in range(NB)]
                for i in range(NB):
                    G("memset", dict(ap=vx[i][:], constant=1.0), w=[f"vx{i}"])
                og = [sb(pc, f"og{i}", [64, 512]) for i in range(NB)]
                sm = [sb(pc, f"sm{i}", [128, 64]) for i in range(NB)]
                dg = [sb(pc, f"dg{i}", [64, 4, 64]) for i in range(NB)]
                mk = [sb(pc, f"mk{i}", [64, 4, 64]) for i in range(NB)]
                spT = [sb(pc, f"spT{i}", [64, 64], BF16) for i in range(4)]
                tmpB = [sb(pc, f"tmpB{i}", [64, 129]) for i in range(2)]
                num = [sb(pc, f"num{i}", [64, 129]) for i in range(2)]
                junkc = sb(pc, "junkc", [64, 128], BF16)
                hs = [sb(pc, f"hs{i}", [64, 8]) for i in range(4)]
                ytm = [sb(pc, f"ytm{i}", [64, 4, 128], BF16) for i in range(NB)]
                kw = [sb(pc, f"kw{i}", [64, 128], BF16) for i in range(2)]
                pg = ps(pc, "pg")
                pv = ps(pc, "pv")
                po = ps(pc, "po")
                pqk = ps(pc, "pqk")
                pAB = [ps(pc, f"pAB{i}") for i in range(2)]
                pU = ps(pc, "pU")
                ptr = ps(pc, "ptr", BF16)

                SM = dict(gt=0, e1=8, lfn=12, a=16, ea=20, cm=24, amax=28, M=32, Mend=36, eM=40, w=44, wg=48, dec=52, emt=56, t0=60)
                import os
                CLIM = int(os.environ.get("CLIM", NCH))
                LVL = int(os.environ.get("LVL", 99))
                def chunk_ctx(c):
                    Tn = 64
                    c0 = 64 * c
                    b = c % NB
                    s_ = sm[b]
                    ksm = lambda nm, b=b: (f"sm{b}", nm)
                    col = lambda nm, w=4, s_=s_, r=None: s_[0:(Tn if r is None else r), SM[nm]:SM[nm] + w]
                    hk = HN_ALL
                    return Tn, c0, b, s_, ksm, col, hk

                def gen_pre(c):
                    Tn, c0, b, s_, ksm, col, hk = chunk_ctx(c)
                    if c < 32 and not _os.environ.get("NOPREP"):
                        for r in (2 * c, 2 * c + 1):
                            P.dma("gpsimd", "vprep", dict(out=vb_s[256 * r:256 * r + 256, :], in_=peer_v[256 * r:256 * r + 256, :]), writes=["vb_s"])
                            yield
                    for dc in range(8):
                        T("matmul", dict(out=pg[0:Tn, 0:8], lhsT=hnT[:, dc, c0:c0 + Tn], rhs=wgate[:, dc, 0:8], start=(dc == 0), stop=(dc == 7)), hk + ["wgate"], ["pg"])
                        yield
                    V("tensor_tensor", dict(out=col("gt", 8), in0=pg[0:Tn, 0:8], in1=bm_b[0:Tn, :], op=ALU.add), ["pg", "bm_b"], [ksm("gt")])
                    yield
                    A("activation", dict(out=col("e1"), in_=s_[0:Tn, 4:8], func=AF.Exp, scale=-1.0), [ksm("gt")], [ksm("e1")])
                    yield
                    A("activation", dict(out=col("lfn"), in_=col("e1"), func=AF.Ln, bias=onec[0:Tn, :], scale=1.0), [ksm("e1"), "onec"], [ksm("lfn")])
                    yield
                    T("matmul", dict(out=pg[0:Tn, 8:12], lhsT=causf[0:Tn, 0:Tn], rhs=col("lfn"), start=True, stop=True), [ksm("lfn"), "causf"], ["pg"])
                    yield
                    T("matmul", dict(out=pg[:, 16:20], lhsT=onesf[0:Tn, :], rhs=col("lfn"), start=True, stop=True), [ksm("lfn"), "onesf"], ["pg"])
                    yield
                    V("tensor_tensor", dict(out=col("a"), in0=s_[0:Tn, 0:4], in1=pg[0:Tn, 8:12], op=ALU.add), [ksm("gt"), "pg"], [ksm("a")])
                    yield
                    dgb, mkb = dg[b], mk[b]
                    V("tensor_tensor", dict(out=dgb[0:Tn, :, 0:Tn], in0=identf[0:Tn, 0:Tn].unsqueeze(1).broadcast_to([Tn, 4, Tn]), in1=col("a").unsqueeze(2).broadcast_to([Tn, 4, Tn]), op=ALU.mult), [ksm("a"), "identf"], [f"dg{b}"])
                    yield
                    abc = pg[:, 256:512].rearrange("p (h s) -> p h s", h=4)
                    for h in range(4):
                        T("matmul", dict(out=pg[:, 256 + 64 * h:256 + 64 * h + Tn], lhsT=onesf[0:Tn, :], rhs=dgb[0:Tn, h, 0:Tn], start=True, stop=True), [f"dg{b}", "onesf"], ["pg"])
                        yield
                    V("tensor_tensor", dict(out=mkb[0:Tn, :, 0:Tn], in0=abc[0:Tn, :, 0:Tn], in1=negm[0:Tn, 0:Tn].unsqueeze(1).broadcast_to([Tn, 4, Tn]), op=ALU.add), ["pg", "negm"], [f"mk{b}"])
                    yield
                    V("tensor_reduce", dict(out=col("cm"), in_=mkb[0:Tn, :, 0:Tn], axis=AX.X, op=ALU.max), [f"mk{b}"], [ksm("cm")])
                    yield
                    V("tensor_reduce", dict(out=s_[:, SM["amax"]:SM["amax"] + 4], in_=abc[:, :, 0:Tn], axis=AX.X, op=ALU.max), ["pg"], [ksm("amax")])
                    yield
                    A("activation", dict(out=col("ea"), in_=col("a"), func=AF.Exp, bias=lnsc[0:Tn, :], scale=1.0), [ksm("a"), "lnsc"], [ksm("ea")])
                    yield
                    V("tensor_tensor", dict(out=col("M"), in0=col("cm"), in1=mst[0:Tn, :], op=ALU.max), [ksm("cm"), "mst"], [ksm("M")])
                    yield
                    V("tensor_tensor", dict(out=s_[:, SM["Mend"]:SM["Mend"] + 4], in0=s_[:, SM["amax"]:SM["amax"] + 4], in1=mst[:, :], op=ALU.max), [ksm("amax"), "mst"], [ksm("Mend")])
                    yield
                    A("activation", dict(out=col("eM"), in_=col("M"), func=AF.Exp, scale=-1.0), [ksm("M")], [ksm("eM")])
                    yield
                    V("tensor_tensor", dict(out=col("w"), in0=mst[0:Tn, :], in1=col("M"), op=ALU.subtract), [ksm("M"), "mst"], [ksm("w")])
                    yield
                    A("activation", dict(out=col("w"), in_=col("w"), func=AF.Exp), [ksm("w")], [ksm("w")])
                    yield
                    V("tensor_tensor", dict(out=col("wg"), in0=col("a"), in1=col("Mend"), op=ALU.subtract), [ksm("a"), ksm("Mend")], [ksm("wg")])
                    yield
                    A("activation", dict(out=col("wg"), in_=col("wg"), func=AF.Exp, bias=lnsc[0:Tn, :], scale=1.0), [ksm("wg"), "lnsc"], [ksm("wg")])
                    yield
                    V("tensor_tensor", dict(out=s_[:, SM["dec"]:SM["dec"] + 4], in0=mst[:, :], in1=s_[:, SM["Mend"]:SM["Mend"] + 4], op=ALU.subtract), [ksm("Mend"), "mst"], [ksm("dec")])
                    yield
                    A("activation", dict(out=s_[:, SM["dec"]:SM["dec"] + 4], in_=s_[:, SM["dec"]:SM["dec"] + 4], func=AF.Exp), [ksm("dec")], [ksm("dec")])
                    yield
                    V("tensor_tensor", dict(out=col("emt"), in0=pg[0:Tn, 8:12], in1=col("M"), op=ALU.subtract), [ksm("M"), "pg"], [ksm("emt")])
                    yield
                    A("activation", dict(out=col("emt"), in_=col("emt"), func=AF.Exp), [ksm("emt")], [ksm("emt")])
                    yield
                    V("tensor_tensor", dict(out=mst[:, :], in0=s_[:, SM["Mend"]:SM["Mend"] + 4], in1=pg[:, 16:20], op=ALU.subtract), [ksm("Mend"), "pg", "mst"], ["mst"])
                    yield
                    vxb, ogb = vx[b], og[b]
                    for dc in range(8):
                        T("matmul", dict(out=pv[0:Tn, :], lhsT=hnT[:, dc, c0:c0 + Tn], rhs=wv[:, dc, :], start=(dc == 0), stop=(dc == 7)), hk + ["wv"], ["pv"])
                        yield
                    A("activation", dict(out=vxb[0:Tn, :, 0:128], in_=pv[0:Tn, :].rearrange("p (h d) -> p h d", h=4), func=AF.Copy), ["pv"], [f"vx{b}"])
                    yield
                    for dc in range(8):
                        T("matmul", dict(out=po[0:Tn, :], lhsT=hnT[:, dc, c0:c0 + Tn], rhs=wo[:, dc, :], start=(dc == 0), stop=(dc == 7)), hk + ["wo"], ["po"])
                        yield
                    A("activation", dict(out=ogb[0:Tn, :], in_=po[0:Tn, :], func=AF.Exp, scale=-1.0), ["po"], [f"og{b}"])
                    yield
                    V("tensor_scalar", dict(out=ogb[0:Tn, :], in0=ogb[0:Tn, :], scalar1=1.0, scalar2=None, op0=ALU.add), [f"og{b}"], [f"og{b}"])
                    yield
                    V("reciprocal", dict(out=ogb[0:Tn, :], in_=ogb[0:Tn, :]), [f"og{b}"], [f"og{b}"])
                    yield
                    V("tensor_tensor", dict(out=ogb[0:Tn, :], in0=ogb[0:Tn, :], in1=mg_b[0:Tn, :], op=ALU.mult), [f"og{b}", "mg_b"], [f"og{b}"])
                    yield
                    ytb = ytm[b]

                def gen_head(c, h):
                    Tn, c0, b, s_, ksm, col, hk = chunk_ctx(c)
                    vxb, ogb, ytb = vx[b], og[b], ytm[b]
                    hb = h % 2
                    qTc = big1[:, h, c0:c0 + Tn]
                    kTc = big1[:, 4 + h, c0:c0 + Tn]
                    sp = spT[h]
                    pab = pAB[hb]
                    hsb = hs[h]
                    khs = lambda nm, h=h: (f"hs{h}", nm)
                    T("matmul", dict(out=pqk[0:Tn, 64 * h:64 * h + Tn], lhsT=kTc, rhs=qTc, start=True, stop=True), [("qk", h), ("qk", 4 + h)], ["pqk"])
                    yield
                    V("scalar_tensor_tensor", dict(out=sp[0:Tn, 0:Tn], in0=pqk[0:Tn, 64 * h:64 * h + Tn], scalar=col("ea")[:, h:h + 1], in1=causf[0:Tn, 0:Tn], op0=ALU.mult, op1=ALU.mult), ["pqk", ksm("ea"), "causf"], [f"spT{h}"])
                    yield
                    T("matmul", dict(out=pab[0:Tn, 0:129], lhsT=sp[0:Tn, 0:Tn], rhs=vxb[0:Tn, h, :], start=True, stop=True), [f"spT{h}", f"vx{b}"], [f"pAB{hb}"])
                    yield
                    T("matmul", dict(out=pab[0:Tn, 256:385], lhsT=qTc, rhs=CTb[:, h, :], start=True, stop=True), [("qk", h), ("CTb", h)], [f"pAB{hb}"])
                    yield
                    tB, nm = tmpB[hb], num[hb]
                    A("activation", dict(out=tB[0:Tn, :], in_=pab[0:Tn, 256:385], func=AF.Copy, scale=col("w")[:, h:h + 1]), [f"pAB{hb}", ksm("w")], [f"tmpB{hb}"])
                    yield
                    V("scalar_tensor_tensor", dict(out=nm[0:Tn, :], in0=pab[0:Tn, 0:129], scalar=col("eM")[:, h:h + 1], in1=tB[0:Tn, :], op0=ALU.mult, op1=ALU.add), [f"pAB{hb}", f"tmpB{hb}", ksm("eM")], [f"num{hb}"])
                    yield
                    A("activation", dict(out=hsb[0:Tn, 7:8], in_=nm[0:Tn, 128:129], func=AF.Abs), [f"num{hb}"], [khs("abs")])
                    yield
                    V("tensor_scalar", dict(out=hsb[0:Tn, 0:1], in0=hsb[0:Tn, 7:8], scalar1=col("emt")[:, h:h + 1], scalar2=None, op0=ALU.max), [khs("abs"), ksm("emt")], [khs("den")])
                    yield
                    V("reciprocal", dict(out=hsb[0:Tn, 1:2], in_=hsb[0:Tn, 0:1]), [khs("den")], [khs("rden")])
                    yield
                    V("scalar_tensor_tensor", dict(out=junkc[0:Tn, :], in0=nm[0:Tn, 0:128], scalar=1.0, in1=nm[0:Tn, 0:128], op0=ALU.mult, op1=ALU.mult, accum_out=hsb[0:Tn, 2:3]), [f"num{hb}"], ["junkc", khs("ss")])
                    yield
                    V("tensor_scalar", dict(out=hsb[0:Tn, 3:4], in0=hsb[0:Tn, 2:3], scalar1=hsb[0:Tn, 1:2], scalar2=hsb[0:Tn, 1:2], op0=ALU.mult, op1=ALU.mult), [khs("ss"), khs("rden")], [khs("t1")])
                    yield
                    rstd_from_ss(hsb[0:Tn, 3:4], hsb[0:Tn, 5:6], 128.0, Tn, [khs("t1")], [khs("ln"), khs("rstd")], hsb[0:Tn, 4:5])
                    yield
                    V("tensor_tensor", dict(out=hsb[0:Tn, 6:7], in0=hsb[0:Tn, 5:6], in1=hsb[0:Tn, 1:2], op=ALU.mult), [khs("rstd"), khs("rden")], [khs("sc")])
                    yield
                    V("scalar_tensor_tensor", dict(out=ytb[0:Tn, h, :], in0=nm[0:Tn, 0:128], scalar=hsb[0:Tn, 6:7], in1=ogb[0:Tn, 128 * h:128 * (h + 1)], op0=ALU.mult, op1=ALU.mult), [f"num{hb}", khs("sc"), f"og{b}"], [(f"ytm{b}", h)])
                    yield
                    T("transpose", dict(out=ptr[:, 64 * h:64 * h + Tn], in_=ytb[0:Tn, h, :], identity=identb[0:Tn, 0:Tn]), [(f"ytm{b}", h), "identb"], ["ptr"])
                    yield
                    A("activation", dict(out=yT[:, h, c0:c0 + Tn], in_=ptr[:, 64 * h:64 * h + Tn], func=AF.Copy), ["ptr"], [("yT", h, c)])
                    yield
                    kwb = kw[hb]
                    T("transpose", dict(out=ptr[0:Tn, 512 + 128 * hb:512 + 128 * hb + 128], in_=kTc, identity=identb[:, :]), [("qk", 4 + h), "identb"], ["ptr"])
                    yield
                    A("activation", dict(out=kwb[0:Tn, :], in_=ptr[0:Tn, 512 + 128 * hb:512 + 128 * hb + 128], func=AF.Copy, scale=col("wg")[:, h:h + 1]), ["ptr", ksm("wg")], [f"kw{hb}"])
                    yield
                    T("matmul", dict(out=pU[:, 256 * hb:256 * hb + 129], lhsT=kwb[0:Tn, :], rhs=vxb[0:Tn, h, :], start=True, stop=True), [f"kw{hb}", f"vx{b}"], ["pU"])
                    yield
                    V("scalar_tensor_tensor", dict(out=CT[:, h, :], in0=CT[:, h, :], scalar=s_[:, SM["dec"] + h:SM["dec"] + h + 1], in1=pU[:, 256 * hb:256 * hb + 129], op0=ALU.mult, op1=ALU.add), [("CT", h), ksm("dec"), "pU"], [("CT", h)])
                    yield
                    A("activation", dict(out=CTb[:, h, :], in_=CT[:, h, :], func=AF.Copy), [("CT", h)], [("CTb", h)])
                    yield


                def run_rr(gens, bg=()):
                    gens = list(gens)
                    bg = list(bg)
                    while gens:
                        alive = []
                        for g_ in gens:
                            try:
                                next(g_)
                                alive.append(g_)
                            except StopIteration:
                                pass
                        gens = alive
                        for g_ in list(bg):
                            try:
                                next(g_)
                            except StopIteration:
                                bg.remove(g_)
                    return bg

                NCC = min(NCH, CLIM)
                run_rr([gen_pre(0)])
                for c in range(NCC):
                    bg = [gen_pre(c + 1)] if c + 1 < NCC else []
                    bg = run_rr([gen_head(c, 0), gen_head(c, 1)], bg)
                    bg = run_rr([gen_head(c, 2), gen_head(c, 3)], bg)
                    run_rr(bg)
                P.barrier(exclude=VEX)
                P.emit()
            dump("yTm", yT[:, 0:4, 0:L], [], [128, 4, L])
            if stop == "C":
                P.barrier(exclude=VEX)
                P.emit()
                return nc, dbg_out

            NBLK = 17
            kTe = sb(ms, "kTe", [70, 8, LP], BF16)
            fvx = sb(ms, "fvx", [128, NBLK, 8, 65], BF16)
            prow = sb(ms, "prow", [48, LP], BF16)
            qTe = big1
            with ExitStack() as pd:
                gq_b = sb(pd, "gq_b", [128, 64])
                gk_b = sb(pd, "gk_b", [128, 64])
                nbff = sb(pd, "nbff", [8, 1])
                P.dma("sync", "c3", dict(out=gq_b[:], in_=f_q_norm[0:1, :].partition_broadcast(128)), writes=["gq_b"])
                P.dma("sync", "c3", dict(out=gk_b[:], in_=f_k_norm[0:1, :].partition_broadcast(128)), writes=["gk_b"])
                P.dma("sync", "c3", dict(out=nbff[:], in_=b_fgate_f[:, :]), writes=["nbff"])
                bffb = sb(pd, "bffb", [128, 8])
                P.dma("sync", "c3", dict(out=bffb[:], in_=b_fgate_f.rearrange("h o -> o h").partition_broadcast(128)), writes=["bffb"])
                V("tensor_scalar", dict(out=gq_b[:], in0=gq_b[:], scalar1=0.125, scalar2=None, op0=ALU.mult), ["gq_b"], ["gq_b"])
                V("tensor_scalar", dict(out=nbff[:], in0=nbff[:], scalar1=-1.0, scalar2=None, op0=ALU.mult), ["nbff"], ["nbff"])
                G("memset", dict(ap=fvx[:], constant=1.0), w=["fvx"])
                G("memset", dict(ap=qTe[64:70, :, :], constant=1.0), w=["qTe_ext"])
                G("memset", dict(ap=kTe[64:70, :, :], constant=1.0), w=["kTe_ext"])
                wf = [sb(pd, f"wf{i}", [128, 8, 512], BF16) for i in range(2)]
                qs = [sb(pd, f"qs{i}", [128, 512]) for i in range(2)]
                sq = sb(pd, "sqD", [128, 512])
                qtm = [sb(pd, f"qtm{i}", [128, 512], BF16) for i in range(2)]
                ssd = [sb(pd, f"ssd{i}", [128, 24]) for i in range(2)]
                pq = [ps(pd, f"pq{i}") for i in range(2)]
                ptq = [ps(pd, f"ptq{i}", BF16) for i in range(2)]
                pgf = ps(pd, "pgf")
                lfd = sb(pd, "lfd", [128, NBLK, 8])
                cnd = sb(pd, "cnd", [128, NBLK, 8])
                tot = sb(pd, "tot", [128, NBLK + 1, 8])
                for i in range(NBLK):
                    c0 = 128 * i
                    for dc in range(8):
                        T("matmul", dict(out=pgf[:, 8 * i:8 * i + 8], lhsT=hnT[:, dc, c0:c0 + 128], rhs=wgate[:, dc, 8:16], start=(dc == 0), stop=(dc == 7)), HN_ALL + ["wgate"], ["pgf"])
                pgv = pgf[:, 0:8 * NBLK].rearrange("p (b h) -> p b h", h=8)
                V("tensor_tensor", dict(out=lfd[:], in0=pgv, in1=bffb[:, :].unsqueeze(1).broadcast_to([128, NBLK, 8]), op=ALU.add), ["pgf", "bffb"], ["lfd"])
                A("activation", dict(out=lfd[:], in_=lfd[:], func=AF.Exp, scale=-1.0), ["lfd"], ["lfd"])
                A("activation", dict(out=lfd[:], in_=lfd[:], func=AF.Ln, bias=onec[:, :], scale=1.0), ["lfd", "onec"], ["lfd"])
                lf2 = lfd[:, :, :].rearrange("p b h -> p (b h)")
                T("matmul", dict(out=pgf[:, 256:256 + 8 * NBLK], lhsT=causf[:, :], rhs=lf2, start=True, stop=True), ["lfd", "causf"], ["pgf"])
                V("tensor_copy", dict(out=cnd[:, :, :].rearrange("p b h -> p (b h)"), in_=pgf[:, 256:256 + 8 * NBLK]), ["pgf"], ["cnd"])
                T("matmul", dict(out=pgf[:, 256:256 + 8 * NBLK], lhsT=onesf[:, :], rhs=lf2, start=True, stop=True), ["lfd", "onesf", "cnd"], ["pgf"])
                G("memset", dict(ap=tot[:, 0, :], constant=0.0), w=["tot"])
                for i in range(NBLK):
                    V("tensor_tensor", dict(out=tot[:, i + 1, :], in0=tot[:, i, :], in1=pgf[:, 256 + 8 * i:256 + 8 * i + 8], op=ALU.add), ["tot", "pgf"], ["tot"])
                V("tensor_tensor", dict(out=cnd[:], in0=cnd[:], in1=tot[:, 0:NBLK, :], op=ALU.add), ["cnd", "tot"], ["cnd"])
                parts = sb(pd, "parts", [128, NBLK, 48], BF16)
                r1 = sb(pd, "r1d", [128, NBLK, 8])
                pv3 = lambda a, b_: parts[:, :, a:b_]
                V("tensor_copy", dict(out=pv3(0, 8), in_=cnd[:]), ["cnd"], ["parts0"])
                V("tensor_tensor", dict(out=r1[:], in0=cnd[:], in1=pv3(0, 8), op=ALU.subtract), ["cnd", "parts0"], ["r1d"])
                V("tensor_copy", dict(out=pv3(8, 16), in_=r1[:]), ["r1d"], ["parts1"])
                V("tensor_tensor", dict(out=r1[:], in0=r1[:], in1=pv3(8, 16), op=ALU.subtract), ["r1d", "parts1"], ["r1d"])
                V("tensor_copy", dict(out=pv3(16, 24), in_=r1[:]), ["r1d"], ["parts2"])
                V("tensor_scalar", dict(out=pv3(24, 48), in0=pv3(0, 24), scalar1=-1.0, scalar2=None, op0=ALU.mult), ["parts0", "parts1", "parts2"], ["parts3"])
                PK = ["parts0", "parts1", "parts2", "parts3"]
                for i in range(NBLK):
                    tpb = ptq[i % 2]
                    T("transpose", dict(out=tpb[0:48, 0:128], in_=parts[:, i, :], identity=identb[:, :]), PK + ["identb"], [f"ptq{i % 2}"])
                    A("activation", dict(out=prow[:, 128 * i:128 * i + 128], in_=tpb[0:48, 0:128], func=AF.Copy), [f"ptq{i % 2}"], ["prow"])
                for h in range(8):
                    for j in range(3):
                        P.dma("sync", "ext", dict(out=kTe[67 + j:68 + j, h, :], in_=prow[8 * j + h:8 * j + h + 1, :]), ["prow", "kTe_ext"], [("kTe_ext", h, j)])
                        P.dma("sync", "ext", dict(out=qTe[64 + j:65 + j, h, :], in_=prow[24 + 8 * j + h:24 + 8 * j + h + 1, :]), ["prow", "qTe_ext"], [("qTe_ext", h, j)])
                wsrc = [(2056, "q"), (2568, "k"), (3080, "v")]
                sq2 = [sq, sb(pd, "sqD2", [128, 512])]

                def gen_blk(wfb, kw_, kind, i, b):
                    c0 = 128 * i
                    pqb, qsb, qtb, ssb, tpb, sqb = pq[b], qs[b], qtm[b], ssd[b], ptq[b], sq2[b]
                    for dc in range(8):
                        T("matmul", dict(out=pqb[:, :], lhsT=hnT[:, dc, c0:c0 + 128], rhs=wfb[:, dc, :], start=(dc == 0), stop=(dc == 7)), HN_ALL + [kw_], [f"pq{b}"])
                        yield
                    if kind == "v":
                        A("activation", dict(out=fvx[:, i, :, 0:64], in_=pqb[:, :].rearrange("p (h d) -> p h d", h=8), func=AF.Copy), [f"pq{b}", "fvx"], [("fvx", i)])
                        yield
                        return
                    gb_ = gq_b if kind == "q" else gk_b
                    A("activation", dict(out=qsb[:, :], in_=pqb[:, :], func=AF.Copy), [f"pq{b}"], [f"qs{b}"])
                    yield
                    V("tensor_tensor", dict(out=sqb[:, :], in0=qsb[:, :], in1=qsb[:, :], op=ALU.mult), [f"qs{b}"], [f"sqD{b}"])
                    yield
                    V("tensor_reduce", dict(out=ssb[:, 0:8], in_=sqb[:, :].rearrange("p (h d) -> p h d", h=8), axis=AX.X, op=ALU.add), [f"sqD{b}"], [f"ssd{b}a"])
                    yield
                    rstd_from_ss(ssb[:, 0:8], ssb[:, 16:24], 64.0, 128, [f"ssd{b}a"], [f"ssd{b}b", f"ssd{b}c"], ssb[:, 8:16])
                    yield
                    q3 = qsb[:, :].rearrange("p (h d) -> p h d", h=8)
                    V("tensor_tensor", dict(out=q3, in0=q3, in1=ssb[:, 16:24].unsqueeze(2).broadcast_to([128, 8, 64]), op=ALU.mult), [f"qs{b}", f"ssd{b}c"], [f"qs{b}"])
                    yield
                    V("tensor_tensor", dict(out=qtb[:, :].rearrange("p (h d) -> p h d", h=8), in0=q3, in1=gb_[:, :].unsqueeze(1).broadcast_to([128, 8, 64]), op=ALU.mult), [f"qs{b}", "gq_b", "gk_b"], [f"qtm{b}"])
                    yield
                    for h in range(8):
                        T("transpose", dict(out=tpb[0:64, 128 * h:128 * h + 128], in_=qtb[:, 64 * h:64 * h + 64], identity=identb[:, :]), [f"qtm{b}", "identb"], [f"ptq{b}"])
                        yield
                    dst = qTe if kind == "q" else kTe
                    A("activation", dict(out=dst[0:64, :, c0:c0 + 128], in_=tpb[0:64, :].rearrange("p (h t) -> p h t", h=8), func=AF.Copy), [f"ptq{b}"], [(kind + "T", i)])
                    yield

                def run_rr2(gens):
                    gens = list(gens)
                    while gens:
                        alive = []
                        for g_ in gens:
                            try:
                                next(g_)
                                alive.append(g_)
                            except StopIteration:
                                pass
                        gens = alive

                for wi, (wc0, kind) in enumerate(wsrc):
                    wfb = wf[wi % 2]
                    kw_ = f"wf{wi % 2}"
                    P.dma("gpsimd", kw_, dict(out=wfb[:], in_=w_in[:, wc0:wc0 + 512].rearrange("(dc p) c -> p dc c", p=128)), writes=[kw_])
                    for i in range(0, NBLK, 2):
                        gl = [gen_blk(wfb, kw_, kind, i, 0)]
                        if i + 1 < NBLK:
                            gl.append(gen_blk(wfb, kw_, kind, i + 1, 1))
                        run_rr2(gl)
                P.barrier(exclude=VEX)
                P.emit()
            dump("qTe", qTe[0:70, :, 0:L], [], [70, 8, L])
            dump("kTe", kTe[0:70, :, 0:L], [], [70, 8, L])
            if stop == "D1":
                P.barrier(exclude=VEX)
                P.emit()
                return nc, dbg_out
            with ExitStack() as pd2:
                pT = [sb(pd2, f"pT{i}", [128, 512], BF16) for i in range(3)]
                ytf = [sb(pd2, f"ytf{i}", [128, 4, 512], BF16) for i in range(2)]
                rinv = [sb(pd2, f"rinv{i}", [128, 4]) for i in range(2)]
                lg = [ps(pd2, f"lg{i}") for i in range(2)]
                pacc = [ps(pd2, f"pacc{i}") for i in range(4)]
                ptr2 = [ps(pd2, f"ptr2{i}", BF16) for i in range(2)]
                it = 0
                for TS in range(5):
                    jl = [j for j in range(4 * TS, min(4 * TS + 4, NBLK))]
                    nj = len(jl)
                    t0 = 512 * TS
                    WT = 128 * nj
                    ytb = ytf[TS % 2]
                    for h in range(8):
                        rb = rinv[h % 2]
                        nI = jl[-1] + 1

                        def qk(i, itv):
                            b2 = itv % 2
                            ts_ = max(t0, 128 * i)
                            Wd = t0 + WT - ts_
                            T("matmul", dict(out=lg[b2][:, 0:Wd], lhsT=kTe[0:70, h, 128 * i:128 * i + 128], rhs=qTe[0:70, h, ts_:ts_ + Wd], start=True, stop=True), [], [f"lg{b2}"])

                        qk(0, it)
                        for i in range(nI):
                            b3 = it % 3
                            b2 = it % 2
                            ts_ = max(t0, 128 * i)
                            Wd = t0 + WT - ts_
                            A("activation", dict(out=pT[b3][:, 0:Wd], in_=lg[b2][:, 0:Wd], func=AF.Exp), [f"lg{b2}"], [f"pT{b3}"])
                            if 128 * i >= t0:
                                G("tensor_tensor", dict(out=pT[b3][:, 0:128], in0=pT[b3][:, 0:128], in1=causb[:, :], op=ALU.mult), [f"pT{b3}", "causb"], [f"pT{b3}"])
                            if i + 1 < nI:
                                qk(i + 1, it + 1)
                            for j in jl:
                                if j < i:
                                    continue
                                jj = j - 4 * TS
                                co = 128 * j - ts_
                                T("matmul", dict(out=pacc[jj][:, 0:65], lhsT=pT[b3][:, co:co + 128], rhs=fvx[:, i, h, :], start=(i == 0), stop=(i == j)), [f"pT{b3}"], [f"pacc{jj}"])
                            it += 1
                        for jj in range(nj):
                            V("reciprocal", dict(out=rb[:, jj:jj + 1], in_=pacc[jj][:, 64:65]), [f"pacc{jj}"], [(f"rinv{h % 2}", jj)])
                            V("tensor_scalar", dict(out=ytb[:, jj, 64 * h:64 * h + 64], in0=pacc[jj][:, 0:64], scalar1=rb[:, jj:jj + 1], scalar2=None, op0=ALU.mult), [f"pacc{jj}", (f"rinv{h % 2}", jj)], [(f"ytf{TS % 2}", h)])
                    for jj, j in enumerate(jl):
                        tb = ptr2[jj % 2]
                        for pr in range(4):
                            T("transpose", dict(out=tb[:, 128 * pr:128 * pr + 128], in_=ytb[:, jj, 128 * pr:128 * pr + 128], identity=identb[:, :]), [(f"ytf{TS % 2}", hh) for hh in range(8)] + ["identb"], [f"ptr2{jj % 2}"])
                        A("activation", dict(out=yT[:, 4:8, 128 * j:128 * j + 128], in_=tb[:, 0:512].rearrange("p (a t) -> p a t", a=4), func=AF.Copy), [f"ptr2{jj % 2}"], [("yTf", j)])
                P.barrier(exclude=VEX)
                P.emit()
            dump("yT", yT[:, :, 0:L], [], [128, 8, L])
            if stop == "D":
                P.barrier(exclude=VEX)
                P.emit()
                return nc, dbg_out


        with ExitStack() as pf:
            wo16 = sb(pf, "wo16", [128, 8, D], BF16)
            wq16 = sb(pf, "wq16", [128, 8, 2048], BF16)
            keysT = sb(pf, "keysT", [128, 16, 128])
            P.dma("gpsimd", "wo16", dict(out=wo16[:], in_=w_out.rearrange("(cc p) d -> p cc d", p=128)), writes=["wo16"])
            for hf in range(2):
                P.dma("gpsimd", "wq16", dict(out=wq16[:, :, 1024 * hf:1024 * hf + 1024], in_=peer_query[:, 1024 * hf:1024 * hf + 1024].rearrange("(dc p) c -> p dc c", p=128)), writes=["wq16"])
            pacc = [ps(pf, f"paccF{i}") for i in range(2)]
            ptx = ps(pf, "ptx", BF16)
            pS = ps(pf, "pS")
            pa = [ps(pf, f"pa{i}") for i in range(2)]
            pgt = [ps(pf, f"pgt{i}") for i in range(2)]
            with ExitStack() as pk:
                knat = sb(pk, "knat", [128, 16, 128])
                P.dma("sync", "knat", dict(out=knat[:], in_=peer_keys.rearrange("(hp n) c -> n hp c", n=128)), writes=["knat"])
                for hp in range(16):
                    T("transpose", dict(out=pS[:, 128 * (hp % 4):128 * (hp % 4) + 128], in_=knat[:, hp, :], identity=identf[:, :]), ["knat", "identf"], ["pS"])
                    if hp % 4 == 3:
                        A("activation", dict(out=keysT[:, hp - 3:hp + 1, :], in_=pS[:, :].rearrange("p (a n) -> p a n", a=4), func=AF.Copy), ["pS"], ["keysT"])
                P.barrier(exclude=VEX)
                P.emit()
            xt2 = sb(pf, "xt2", [128, D])
            h2t = sb(pf, "h2t", [128, D])
            xn16 = sb(pf, "xn16", [128, D], BF16)
            xnT = sb(pf, "xnT", [128, 8, 128], BF16)
            scr16 = sb(pf, "scr16", [128, 2048])
            qT_sb = scr16[:, 0:2048].rearrange("p (a t) -> p a t", a=16)
            apre = scr16[:, :].rearrange("p (s e) -> p s e", s=4)
            S_sb = sb(pf, "S_sb", [128, 16, 128])
            v1 = sb(pf, "v1", [128, 16, 16])
            wk = [sb(pf, f"wk{i}", [128, 128]) for i in range(4)]
            cand = [sb(pf, f"cand{i}", [128, 256]) for i in range(2)]
            wk2 = [sb(pf, f"wk2{i}", [128, 256]) for i in range(2)]
            ct = sb(pf, "ct", [128, 8, 16])
            sv = sb(pf, "sv", [128, 64])
            dd = sb(pf, "dd", [128, 8, 16])
            S1t = sb(pf, "S1t", [128, 8, 128])
            UT = [sb(pf, f"UT{i}", [128, 8, 512], BF16) for i in range(2)]
            Vt = [sb(pf, f"Vt{i}", [128, 4, D], BF16) for i in range(2)]
            gA = sb(pf, "gA", [128, 4, 512], BF16)
            tmps = [[sb(pf, f"tmp{p_}_{i}", [128, 512], BF16) for i in range(8)] for p_ in range(2)]
            NZ = 6
            zt = [sb(pf, f"zt{i}", [128, 4, 128]) for i in range(NZ)]
            ez = [sb(pf, f"ez{i}", [128, 512], BF16) for i in range(NZ)]
            wT = [sb(pf, f"wT{i}", [128, 4, 128], BF16) for i in range(2)]
            ADDENG = os.environ.get("ADDENG", "PPPAPPPA")
            FH = [h for h in range(8) if ADDENG[h] == "F"]
            S2e = sb(pf, "S2e", [128, max(1, len(FH)), 128 if FH else 1])
            EH = [h for h in range(8) if ADDENG[h] == "E"]
            SBe = sb(pf, "SBe", [128, max(1, len(EH)), 128 if EH else 1])
            ezf = [sb(pf, f"ezf{i}", [128, 512 if EH else 1]) for i in range(2)]
            DH = [h for h in range(8) if ADDENG[h] == "D"]
            E1d = sb(pf, "E1d", [128, max(1, len(DH)), 128 if DH else 1])
            E2d = sb(pf, "E2d", [128, max(1, len(DH)), 128 if DH else 1])
            Rn = sb(pf, "Rn", [128, max(1, len(FH)), 128 if FH else 1])
            import os
            NT = int(os.environ.get("NTILE", 16))
            zi = 0
            ei = [0]
            for j in range(NT):
                c0 = 16 + 128 * j
                if j == 0:
                    P.dma("sync", "xt2", dict(out=xt2[:, :], in_=x[0:128, :]), writes=["xt2"])
                for hf in range(2):
                    for cc in range(8):
                        T("matmul", dict(out=pacc[hf][:, :], lhsT=yT[:, cc, c0:c0 + 128], rhs=wo16[:, cc, 512 * hf:512 * hf + 512], start=(cc == 0), stop=(cc == 7)), ["wo16"], [f"paccF{hf}"])
                    V("tensor_tensor", dict(out=h2t[:, 512 * hf:512 * hf + 512], in0=pacc[hf][:, :], in1=xt2[:, 512 * hf:512 * hf + 512], op=ALU.add), [f"paccF{hf}", "xt2"], [("h2t", hf)])
                H2 = [("h2t", 0), ("h2t", 1)]
                if j + 1 < NT:
                    P.dma("sync", "xt2", dict(out=xt2[:, :], in_=x[128 * (j + 1):128 * (j + 2), :]), ["xt2"], ["xt2"])
                if debug is not None and "h2" in debug:
                    if j == 0:
                        dbg_h2 = nc.dram_tensor("dbg_h2", [128, 16, D], F32, kind="ExternalOutput").ap()
                        dbg_out["h2"] = dbg_h2
                    P.dma("sync", "dbgh2", dict(out=dbg_h2[:, j, :], in_=h2t[:, :]), H2, ["dbgh2"])
                V("scalar_tensor_tensor", dict(out=xn16[:, :], in0=h2t[:, :], scalar=1.0, in1=h2t[:, :], op0=ALU.mult, op1=ALU.mult, accum_out=sv[:, 0:1]), H2, ["xn16", ("sv", "ss")])
                rstd_from_ss(sv[:, 0:1], sv[:, 2:3], float(D), 128, [("sv", "ss")], [("sv", "ln"), ("sv", "rstd")], sv[:, 1:2])
                V("scalar_tensor_tensor", dict(out=xn16[:, :], in0=h2t[:, :], scalar=sv[:, 2:3], in1=gffn_b[:, :], op0=ALU.mult, op1=ALU.mult), H2 + [("sv", "rstd"), "gffn_b"], ["xn16"])
                for dc in range(8):
                    T("transpose", dict(out=ptx[:, 128 * dc:128 * dc + 128], in_=xn16[:, 128 * dc:128 * dc + 128], identity=identb[:, :]), ["xn16", "identb"], ["ptx"])
                A("activation", dict(out=xnT[:, :, :], in_=ptx[:, :].rearrange("p (a t) -> p a t", a=8), func=AF.Copy), ["ptx"], ["xnT"])
                rot = [(pS, "pS"), (pa[0], "pa0"), (pa[1], "pa1")]
                for g4 in range(4):
                    pb_, pk_ = rot[g4 % 3]
                    for cq in range(4):
                        cc = 4 * g4 + cq
                        for dc in range(8):
                            T("matmul", dict(out=pb_[:, 128 * cq:128 * cq + 128], lhsT=wq16[:, dc, 128 * cc:128 * cc + 128], rhs=xnT[:, dc, :], start=(dc == 0), stop=(dc == 7)), ["wq16", "xnT"], [pk_])
                    A("activation", dict(out=qT_sb[:, 4 * g4:4 * g4 + 4, :], in_=pb_[:, :].rearrange("p (a t) -> p a t", a=4), func=AF.Copy), [pk_], [("scr", g4)])
                for g4 in range(4):
                    pb_, pk_ = rot[(g4 + 1) % 3]
                    for hq in range(4):
                        hp = 4 * g4 + hq
                        T("matmul", dict(out=pb_[:, 128 * hq:128 * hq + 128], lhsT=qT_sb[:, hp, :], rhs=keysT[:, hp, :], start=True, stop=True), [("scr", g4), "keysT"], [pk_])
                    A("activation", dict(out=S_sb[:, 4 * g4:4 * g4 + 4, :], in_=pb_[:, :].rearrange("p (a t) -> p a t", a=4), func=AF.Copy), [pk_], [("S_sb", g4)])
                SK = [("S_sb", g) for g in range(4)]
                for hp0 in range(0, 16, 4):
                    grp = range(hp0, hp0 + 4)
                    for hp in grp:
                        V("max", dict(out=v1[:, hp, 0:8], in_=S_sb[:, hp, :]), SK, [("v1", hp, 0)])
                    for hp in grp:
                        V("match_replace", dict(out=wk[hp % 4][:, :], in_to_replace=v1[:, hp, 0:8], in_values=S_sb[:, hp, :], imm_value=NEG), SK + [("v1", hp, 0)], [f"wk{hp % 4}"])
                    for hp in grp:
                        V("max", dict(out=v1[:, hp, 8:16], in_=wk[hp % 4][:, :]), [f"wk{hp % 4}"], [("v1", hp, 1)])
                for h0 in range(0, 8, 2):
                    grp = range(h0, h0 + 2)
                    for h in grp:
                        V("tensor_tensor", dict(out=cand[h % 2][:, :].rearrange("p (a b) -> p a b", a=16), in0=v1[:, 2 * h, :].unsqueeze(2).broadcast_to([128, 16, 16]), in1=v1[:, 2 * h + 1, :].unsqueeze(1).broadcast_to([128, 16, 16]), op=ALU.add),
                          [("v1", 2 * h, 0), ("v1", 2 * h, 1), ("v1", 2 * h + 1, 0), ("v1", 2 * h + 1, 1)], [f"cand{h % 2}"])
                    for h in grp:
                        V("max", dict(out=ct[:, h, 0:8], in_=cand[h % 2][:, :]), [f"cand{h % 2}"], [("ct", h, 0)])
                    for h in grp:
                        V("match_replace", dict(out=wk2[h % 2][:, :], in_to_replace=ct[:, h, 0:8], in_values=cand[h % 2][:, :], imm_value=NEG), [f"cand{h % 2}", ("ct", h, 0)], [f"wk2{h % 2}"])
                    for h in grp:
                        V("max", dict(out=ct[:, h, 8:16], in_=wk2[h % 2][:, :]), [f"wk2{h % 2}"], [("ct", h, 1)])
                CTK = [("ct", h, k_) for h in range(8) for k_ in range(2)]
                tau = ct[:, :, 15]
                mx_ = ct[:, :, 0]
                V("tensor_tensor", dict(out=dd[:, :, :], in0=ct[:, :, :], in1=ct[:, :, 0:1].broadcast_to([128, 8, 16]), op=ALU.subtract), CTK, ["dd"])
                A("activation", dict(out=dd[:, :, :], in_=dd[:, :, :], func=AF.Exp), ["dd"], ["dd"])
                V("tensor_reduce", dict(out=sv[:, 8:16], in_=dd[:, :, :], axis=AX.X, op=ALU.add), ["dd"], [("sv", "Z")])
                A("activation", dict(out=sv[:, 16:24], in_=sv[:, 8:16], func=AF.Ln), [("sv", "Z")], [("sv", "lnZ")])
                V("tensor_tensor", dict(out=sv[:, 24:32], in0=sv[:, 16:24], in1=mx_, op=ALU.add), [("sv", "lnZ")] + CTK, [("sv", "cst")])
                V("tensor_tensor", dict(out=sv[:, 32:40], in0=tau, in1=sv[:, 24:32], op=ALU.subtract), [("sv", "cst")] + CTK, [("sv", "bias")])
                S4 = S_sb[:, :, :].rearrange("p (h two) n -> p h two n", two=2)
                V("tensor_tensor", dict(out=S1t[:, :, :], in0=S4[:, :, 0, :], in1=ct[:, :, 15:16].broadcast_to([128, 8, 128]), op=ALU.subtract), SK + CTK, ["S1t"])
                V("tensor_scalar", dict(out=S1t[:, :, :], in0=S1t[:, :, :], scalar1=8.0e-6, scalar2=None, op0=ALU.add), ["S1t"], ["S1t"])
                for k_, h in enumerate(FH):
                    V("tensor_scalar", dict(out=S2e[:, k_, :], in0=S_sb[:, 2 * h + 1, :], scalar1=sv[:, 32 + h:33 + h], scalar2=None, op0=ALU.add), SK + [("sv", "bias")], ["S2e"])
                    V("tensor_scalar", dict(out=Rn[:, k_, :], in0=S1t[:, h, :], scalar1=-1.0, scalar2=None, op0=ALU.mult), ["S1t"], ["Rn"])
                if EH:
                    A("activation", dict(out=sv[:, 40:48], in_=sv[:, 32:40], func=AF.Exp), [("sv", "bias")], [("sv", "thr")])
                for k_, h in enumerate(EH):
                    V("tensor_scalar", dict(out=SBe[:, k_, :], in0=S1t[:, h, :], scalar1=sv[:, 32 + h:33 + h], scalar2=None, op0=ALU.add), ["S1t", ("sv", "bias")], ["SBe"])
                for k_, h in enumerate(DH):
                    A("activation", dict(out=E1d[:, k_, :], in_=S1t[:, h, :], func=AF.Exp), ["S1t"], ["E1d"])
                    A("activation", dict(out=E2d[:, k_, :], in_=S_sb[:, 2 * h + 1, :], func=AF.Exp, bias=sv[:, 32 + h:33 + h], scale=1.0), SK + [("sv", "bias")], ["E2d"])
                NIB = 32
                NDUM = int(os.environ.get("NDUM", 0))

                def loadUT(ibp):
                    b_ = ibp % 2
                    P.dma("sync", f"UT{b_}", dict(out=UT[b_][:], in_=ut_s[:, :, 512 * ibp:512 * ibp + 512].rearrange("dc p e -> p dc e")), writes=[f"UT{b_}"])

                def loadVt(ib):
                    b_ = ib % 2
                    P.dma("sync", f"Vt{b_}", dict(out=Vt[b_][:], in_=vb_s[512 * ib:512 * ib + 512, :].rearrange("(q p) d -> p q d", p=128)), ["vb_s"], [f"Vt{b_}"])

                def computeA(ibp):
                    b_ = ibp % 2
                    for q in range(4):
                        for dc in range(8):
                            T("matmul", dict(out=pa[b_][:, 128 * q:128 * q + 128], lhsT=UT[b_][:, dc, 128 * q:128 * q + 128], rhs=xnT[:, dc, :], start=(dc == 0), stop=(dc == 7)), [f"UT{b_}", "xnT"], [f"pa{b_}"])

                def evacA(ibp):
                    b_ = ibp % 2
                    A("activation", dict(out=apre[:, ibp % 4, :], in_=pa[b_][:, :], func=AF.Copy), [f"pa{b_}"], [("scr", ibp % 4)])

                def gelu_burst(ibs):
                    for ibp in ibs:
                        A("activation", dict(out=gA[:, ibp % 4, :], in_=apre[:, ibp % 4, :], func=AF.Gelu), [("scr", ibp % 4)], [("gA", ibp % 4)])

                def stageG(ib):
                    nonlocal zi
                    p_ = ib % 2
                    for h in range(8):
                        zb = zi % NZ
                        zi += 1
                        ae = ADDENG[h]
                        if ae == "E":
                            k_ = EH.index(h)
                            eb = ei[0] % 2
                            ei[0] += 1
                            for q in range(4):
                                A("activation", dict(out=ezf[eb][:, 128 * q:128 * q + 128], in_=S_sb[:, 2 * h + 1, :], func=AF.Exp, bias=SBe[:, k_, 4 * ib + q:4 * ib + q + 1], scale=1.0), SK + ["SBe"], [f"ezf{eb}"])
                            V("scalar_tensor_tensor", dict(out=tmps[p_][h][:, :], in0=ezf[eb][:, :], scalar=sv[:, 40 + h:41 + h], in1=ezf[eb][:, :], op0=ALU.is_ge, op1=ALU.mult), [f"ezf{eb}", ("sv", "thr")], [f"tmp{p_}_{h}"])
                            continue
                        if ae == "F":
                            k_ = FH.index(h)
                            for q in range(4):
                                A("activation", dict(out=ez[zb][:, 128 * q:128 * q + 128], in_=S2e[:, k_, :], func=AF.Exp, bias=S1t[:, h, 4 * ib + q:4 * ib + q + 1], scale=1.0), ["S2e", "S1t"], [f"ez{zb}"])
                            for q in range(4):
                                V("scalar_tensor_tensor", dict(out=tmps[p_][h][:, 128 * q:128 * q + 128], in0=S_sb[:, 2 * h + 1, :], scalar=Rn[:, k_, 4 * ib + q:4 * ib + q + 1], in1=ez[zb][:, 128 * q:128 * q + 128], op0=ALU.is_ge, op1=ALU.mult), SK + ["Rn", f"ez{zb}"], [f"tmp{p_}_{h}"])
                            continue
                        if ae == "A":
                            for q in range(4):
                                A("activation", dict(out=zt[zb][:, q, :], in_=S_sb[:, 2 * h + 1, :], func=AF.Identity, bias=S1t[:, h, 4 * ib + q:4 * ib + q + 1], scale=1.0), SK + ["S1t"], [f"zt{zb}"])
                        else:
                            addeng = G if ae in ("P", "D") else V
                            addeng("tensor_tensor", dict(out=zt[zb][:, :, :], in0=S_sb[:, 2 * h + 1, :].unsqueeze(1).broadcast_to([128, 4, 128]), in1=S1t[:, h, 4 * ib:4 * ib + 4].unsqueeze(2).broadcast_to([128, 4, 128]), op=ALU.add), SK + ["S1t"], [f"zt{zb}"])
                        if ae == "D":
                            k_ = DH.index(h)
                            V("tensor_tensor", dict(out=ez[zb][:, :].rearrange("p (a n) -> p a n", a=4), in0=E2d[:, k_, :].unsqueeze(1).broadcast_to([128, 4, 128]), in1=E1d[:, k_, 4 * ib:4 * ib + 4].unsqueeze(2).broadcast_to([128, 4, 128]), op=ALU.mult), ["E1d", "E2d"], [f"ez{zb}"])
                        else:
                            A("activation", dict(out=ez[zb][:, :], in_=zt[zb][:, :, :].rearrange("p a n -> p (a n)"), func=AF.Exp, bias=sv[:, 32 + h:33 + h], scale=1.0), [f"zt{zb}", ("sv", "bias")], [f"ez{zb}"])
                        V("scalar_tensor_tensor", dict(out=tmps[p_][h][:, :], in0=zt[zb][:, :, :].rearrange("p a n -> p (a n)"), scalar=0.0, in1=ez[zb][:, :], op0=ALU.is_ge, op1=ALU.mult), [f"zt{zb}", f"ez{zb}"], [f"tmp{p_}_{h}"])

                def stageT(ib):
                    p_ = ib % 2
                    for _d in range(NDUM):
                        T("matmul", dict(out=pS[:, :], lhsT=identb[:, :], rhs=gA[:, _d % 4, :], start=True, stop=True), [], ["pSdummy"])
                    for q in range(4):
                        for h in range(8):
                            T("matmul", dict(out=pgt[p_][:, 128 * q:128 * q + 128], lhsT=tmps[p_][h][:, 128 * q:128 * q + 128], rhs=identb[:, :], start=(h == 0), stop=(h == 7)), [f"tmp{p_}_{h}", "identb"], [f"pgt{p_}"])

                def stageW(ib):
                    p_ = ib % 2
                    V("tensor_tensor", dict(out=wT[p_][:, :, :].rearrange("p a t -> p (a t)"), in0=pgt[p_][:, :], in1=gA[:, ib % 4, :], op=ALU.mult), [f"pgt{p_}", ("gA", ib % 4)], [f"wT{p_}"])
                    for q in range(4):
                        for hf in range(2):
                            T("matmul", dict(out=pacc[hf][:, :], lhsT=wT[p_][:, q, :], rhs=Vt[p_][:, q, 512 * hf:512 * hf + 512], start=(ib == 0 and q == 0), stop=(ib == NIB - 1 and q == 3)), [f"wT{p_}", f"Vt{p_}"], [f"paccF{hf}"])

                loadUT(0)
                loadUT(1)
                computeA(0)
                evacA(0)
                loadUT(2)
                computeA(1)
                evacA(1)
                for k in range(NIB + 2):
                    if k + 3 < NIB:
                        loadUT(k + 3)
                    if 0 <= k - 1 < NIB:
                        loadVt(k - 1)
                    if k % 4 == 2:
                        gelu_burst(range(k - 2, min(k + 2, NIB)))
                    if 0 <= k - 1 < NIB:
                        stageT(k - 1)
                    if 0 <= k - 2 < NIB:
                        stageW(k - 2)
                    if k + 2 < NIB:
                        computeA(k + 2)
                    if k < NIB:
                        stageG(k)
                    if k + 2 < NIB:
                        evacA(k + 2)
                SCR = [("scr", s_) for s_ in range(4)]
                for hf in range(2):
                    V("tensor_tensor", dict(out=scr16[:, 512 * hf:512 * hf + 512], in0=pacc[hf][:, :], in1=h2t[:, 512 * hf:512 * hf + 512], op=ALU.add), [f"paccF{hf}"] + H2, SCR)
                P.dma("sync", "ost", dict(out=out[128 * j:128 * j + 128, :], in_=scr16[:, 0:1024]), SCR, SCR + [("out", j)])
            P.barrier()
            P.emit()
        P.barrier()
        P.emit()
    return nc, dbg_out

_CACHE = {}


def kernel(**inputs):
    n = 8
    if "nc" not in _CACHE:
        _CACHE["nc"] = build_nc()[0]
    nc = _CACHE["nc"]
    f = lambda a: np.ascontiguousarray(np.asarray(a, dtype=np.float32))
    x = f(inputs["x"])
    shared = {
        "meta_tokens": f(inputs["meta_tokens"]),
        "norm_mix": f(inputs["norm_mix"]).reshape(1, D),
        "w_in": f(inputs["w_in"]).reshape(D, 3600),
        "conv_qk": f(inputs["conv_qk"]).reshape(4, 1024),
        "b_igate": f(inputs["b_igate"]).reshape(1, 4),
        "b_fgate_m": f(inputs["b_fgate_m"]).reshape(1, 4),
        "m_out_norm": f(inputs["m_out_norm"]).reshape(1, 512),
        "b_fgate_f": f(inputs["b_fgate_f"]).reshape(8, 1),
        "f_q_norm": f(inputs["f_q_norm"]).reshape(1, 64),
        "f_k_norm": f(inputs["f_k_norm"]).reshape(1, 64),
        "w_out": f(inputs["w_out"]).reshape(D, D),
        "norm_ffn": f(inputs["norm_ffn"]).reshape(1, D),
        "peer_query": f(inputs["peer_query"]).reshape(D, 2048),
        "peer_sub_keys": f(inputs["peer_sub_keys"]).reshape(2048, 128),
        "peer_u": f(inputs["peer_u"]).reshape(16384, D),
        "peer_v": f(inputs["peer_v"]).reshape(16384, D),
    }
    in_maps = [dict(shared, x=x[b]) for b in range(n)]
    res = run_bass_kernel_spmd(nc, in_maps, core_ids=list(range(n)))
    return np.stack([np.asarray(res.results[b]["out"], dtype=np.float32) for b in range(n)], axis=0)
```

```python
import numpy as np
import concourse.bass as bass
import concourse.mybir as mybir
from concourse.bass_utils import run_bass_kernel_spmd
from contextlib import ExitStack

F32 = mybir.dt.float32
BF16 = mybir.dt.bfloat16
AF = mybir.ActivationFunctionType
ALU = mybir.AluOpType
AX = mybir.AxisListType

ENGS = ("tensor", "vector", "scalar", "gpsimd", "sync")
import os
RELAX = os.environ.get("RELAX", "0") == "1"
VCLOCK = os.environ.get("VCLOCK", "1") == "1"

D = 1024
SEQ = 2048
NMETA = 16
L = SEQ + NMETA
LM = 2112
LP = 2176
NCH = 33
EPS = 1e-6
NEG = -1.0e30


class Prog:
    def __init__(self, nc, es):
        self.nc = nc
        self.es = es
        self.ops = {e: [] for e in ENGS}
        self.cnt = {e: 0 for e in ENGS}
        self.sem = {e: es.enter_context(nc.semaphore("s_" + e)) for e in ENGS}
        self.waited = {e: {} for e in ENGS}
        self.snap = {}
        self.dclock = {}
        self.last_w = {}
        self.readers = {}
        self.dsems = {}
        self.semobj = {("e", e): self.sem[e] for e in ENGS}
        self.nops = 0
        self.nwaits = 0

    @staticmethod
    def _merge(dst, src):
        for k, v in src.items():
            if dst.get(k, 0) < v:
                dst[k] = v

    def _deps(self, eng, reads, writes):
        deps = {}

        def add(d):
            if d is None:
                return
            k, v = d
            if deps.get(k, 0) < v:
                deps[k] = v
        me = ("e", eng)
        for k in reads:
            add(self.last_w.get(k))
        for k in writes:
            lw = self.last_w.get(k)
            if lw is not None and not (RELAX and lw[0] == me):
                add(lw)
            for sk, v in self.readers.get(k, {}).items():
                if RELAX and sk == me:
                    continue
                add((sk, v))
        out = []
        w = self.waited[eng]
        for sk, v in sorted(deps.items(), key=lambda kv: -kv[1]):
            if eng == "tensor" and sk == ("e", "tensor"):
                continue
            if sk[0] == "d":
                v = self.dsems[sk[1]][1]
            if w.get(sk, 0) < v:
                w[sk] = v
                out.append((sk, v))
                if VCLOCK:
                    if sk[0] == "d":
                        self._merge(w, self.dclock.get(sk[1], {}))
                    elif sk != me:
                        self._merge_snap(w, sk, v)
        self.nwaits += len(out)
        return out

    def _merge_snap(self, w, sk, v):
        s = self.snap.get((sk, v))
        if s is not None:
            own = w.get(("e", self._cur), 0)
            self._merge(w, s)
            if s.get(("e", self._cur), 0) > own:
                w[("e", self._cur)] = own

    def op(self, eng, meth, kw, reads=(), writes=()):
        fn = (meth, kw)
        self._cur = eng
        waits = self._deps(eng, reads, writes)
        self.cnt[eng] += 1
        n = self.cnt[eng]
        me = ("e", eng)
        self.ops[eng].append((waits, fn, (me, 1)))
        self.nops += 1
        if VCLOCK:
            s = dict(self.waited[eng])
            s[me] = n
            self.snap[(me, n)] = s
        for k in reads:
            self.readers.setdefault(k, {})[me] = n
        for k in writes:
            self.last_w[k] = (me, n)
            self.readers[k] = {}

    def dma(self, eng, slot, kw, reads=(), writes=()):
        fn = ("dma_start", kw)
        if slot not in self.dsems:
            s = self.es.enter_context(self.nc.semaphore("d_" + slot))
            self.dsems[slot] = [s, 0]
            self.semobj[("d", slot)] = s
        self._cur = eng
        waits = self._deps(eng, reads, writes)
        self.dsems[slot][1] += 16
        v = self.dsems[slot][1]
        me = ("d", slot)
        self.ops[eng].append((waits, fn, (me, 16)))
        self.nops += 1
        if VCLOCK:
            dc = self.dclock.setdefault(slot, {})
            own = dict(self.waited[eng])
            own.pop(("e", eng), None)
            self._merge(dc, own)
        for k in reads:
            self.readers.setdefault(k, {})[me] = v
        for k in writes:
            self.last_w[k] = (me, v)
            self.readers[k] = {}

    def barrier(self, exclude=()):
        cur = [(("e", e), self.cnt[e]) for e in ENGS if self.cnt[e] > 0]
        cur += [(("d", s), v[1]) for s, v in self.dsems.items() if v[1] > 0 and s not in exclude]
        for e in ENGS:
            w = self.waited[e]
            waits = []
            for sk, v in cur:
                if sk == ("e", e):
                    if e == "tensor":
                        continue
                if w.get(sk, 0) < v:
                    w[sk] = v
                    waits.append((sk, v))
            if waits:
                self.ops[e].append((waits, None, None))
        self.last_w = {k: v for k, v in self.last_w.items() if v[0][0] == "d" and v[0][1] in exclude}
        self.readers = {}
        self.snap = {}

    def emit(self):
        nc = self.nc
        with nc.Block() as block:
            for e in ENGS:
                ops = self.ops[e]
                if not ops:
                    continue
                semobj = self.semobj

                def body(eng, ops=ops):
                    for waits, fn, inc in ops:
                        for sk, v in waits:
                            eng.wait_ge(semobj[sk], v)
                        if fn is not None:
                            ins = getattr(eng, fn[0])(**fn[1])
                            ins.then_inc(semobj[inc[0]], inc[1])
                getattr(block, e)(body)
        self.ops = {e: [] for e in ENGS}


def build_nc(debug=None, stop=None):
    nc = bass.Bass("TRN2", target_bir_lowering=False)

    def din(name, shape):
        return nc.dram_tensor(name, list(shape), F32, kind="ExternalInput").ap()

    x = din("x", [SEQ, D])
    meta = din("meta_tokens", [NMETA, D])
    norm_mix = din("norm_mix", [1, D])
    w_in = din("w_in", [D, 3600])
    conv_qk = din("conv_qk", [4, 1024])
    b_igate = din("b_igate", [1, 4])
    b_fgate_m = din("b_fgate_m", [1, 4])
    m_out_norm = din("m_out_norm", [1, 512])
    b_fgate_f = din("b_fgate_f", [8, 1])
    f_q_norm = din("f_q_norm", [1, 64])
    f_k_norm = din("f_k_norm", [1, 64])
    w_out = din("w_out", [D, D])
    norm_ffn = din("norm_ffn", [1, D])
    peer_query = din("peer_query", [D, 2048])
    peer_keys = din("peer_sub_keys", [2048, 128])
    peer_u = din("peer_u", [16384, D])
    peer_v = din("peer_v", [16384, D])
    out = nc.dram_tensor("out", [SEQ, D], F32, kind="ExternalOutput").ap()
    ut_s = nc.dram_tensor("ut_s", [8, 128, 16384], BF16).ap()
    vb_s = nc.dram_tensor("vb_s", [16384, D], BF16).ap()

    dbg_out = {}
    VEX = ("vprep",)

    es = ExitStack()
    with es:
        P = Prog(nc, es)

        def sb(stack, name, shape, dt=F32):
            return stack.enter_context(nc.sbuf_tensor(name, list(shape), dt))

        def ps(stack, name, dt=F32):
            shape = [128, 512] if dt == F32 else [128, 1024]
            return stack.enter_context(nc.psum_tensor(name, shape, dt))

        def dump(name, ap, key, shape):
            if debug is None or name not in debug:
                return
            t = nc.dram_tensor("dbg_" + name, list(shape), ap.dtype, kind="ExternalOutput").ap()
            dbg_out[name] = t
            P.dma("sync", "dbg", dict(out=t, in_=ap), key if isinstance(key, list) else [key], ["dbg_" + name])

        V = lambda m, kw, r=(), w=(): P.op("vector", m, kw, r, w)
        A = lambda m, kw, r=(), w=(): P.op("scalar", m, kw, r, w)
        G = lambda m, kw, r=(), w=(): P.op("gpsimd", m, kw, r, w)
        T = lambda m, kw, r=(), w=(): P.op("tensor", m, kw, r, w)

        identf = sb(es, "identf", [128, 128])
        identb = sb(es, "identb", [128, 128], BF16)
        epsc = sb(es, "epsc", [128, 1])
        onec = sb(es, "onec", [128, 1])
        lnsc = sb(es, "lnsc", [128, 1])
        gffn_b = sb(es, "gffn_b", [128, D])

        G("iota", dict(out=identf[:], pattern=[[1, 128]], base=0, channel_multiplier=-1,
                           allow_small_or_imprecise_dtypes=True), w=["identf"])
        V("tensor_scalar", dict(out=identb[:], in0=identf[:], scalar1=0.0, scalar2=None, op0=ALU.is_equal), ["identf"], ["identb"])
        iotaf = identf
        G("memset", dict(ap=epsc[:], constant=EPS), w=["epsc"])
        G("memset", dict(ap=onec[:], constant=1.0), w=["onec"])
        G("memset", dict(ap=lnsc[:], constant=float(-0.5 * np.log(128.0))), w=["lnsc"])
        P.dma("sync", "c0", dict(out=gffn_b[:], in_=norm_ffn[0:1, :].partition_broadcast(128)), writes=["gffn_b"])

        def rstd_from_ss(ss_ap, dst_ap, n, r, keys_r, keys_w, tmp_ap):
            A("activation", dict(out=tmp_ap, in_=ss_ap, func=AF.Ln, bias=epsc[0:r, :], scale=1.0 / n), keys_r + ["epsc"], keys_w[:1])
            A("activation", dict(out=dst_ap, in_=tmp_ap, func=AF.Exp, scale=-0.5), keys_w[:1], keys_w[1:])


        import os as _os

        def gen_prep(ub, uts, ptu):
            k = 0
            for eg in range(32):
                b = eg % 2
                P.dma("gpsimd", f"ub{b}", dict(out=ub[b][:], in_=peer_u[512 * eg:512 * eg + 512, :].rearrange("(q p) d -> p q d", p=128)), writes=[f"ub{b}"])
                yield
                for q in range(4):
                    tb = ptu[k % 4]
                    kt = f"ptu{k % 4}"
                    for dc in range(8):
                        T("transpose", dict(out=tb[:, 128 * dc:128 * dc + 128], in_=ub[b][:, q, 128 * dc:128 * dc + 128], identity=identb[:, :]), [f"ub{b}", "identb"], [kt])
                        yield
                    if k % 2 == 0:
                        A("activation", dict(out=uts[b][:, :, 128 * q:128 * q + 128], in_=tb[:, :].rearrange("p (a e) -> p a e", a=8), func=AF.Copy), [kt], [f"uts{b}"])
                    else:
                        V("tensor_copy", dict(out=uts[b][:, :, 128 * q:128 * q + 128], in_=tb[:, :].rearrange("p (a e) -> p a e", a=8)), [kt], [f"uts{b}"])
                    yield
                    k += 1
                P.dma("sync", f"uts{b}", dict(out=ut_s[:, :, 512 * eg:512 * eg + 512].rearrange("dc p e -> p dc e"), in_=uts[b][:]), [f"uts{b}"], ["ut_s"])
                yield

        prep_gen = [None]

        def bg_step(n):
            g_ = prep_gen[0]
            if g_ is None:
                return
            for _ in range(n):
                try:
                    next(g_)
                except StopIteration:
                    prep_gen[0] = None
                    return

        yT = sb(es, "yT", [128, 8, LP], BF16)
        with ExitStack() as ms:
            causf = sb(ms, "causf", [128, 128])
            causb = sb(ms, "causb", [128, 128], BF16)
            negm = sb(ms, "negm", [128, 128])
            onesf = sb(ms, "onesf", [128, 128])
            gmix_b = sb(ms, "gmix_b", [128, D])
            V("tensor_scalar", dict(out=causf[:], in0=iotaf[:], scalar1=0.0, scalar2=None, op0=ALU.is_ge), ["identf"], ["causf"])
            V("tensor_copy", dict(out=causb[:], in_=causf[:]), ["causf"], ["causb"])
            V("tensor_scalar", dict(out=negm[:], in0=iotaf[:], scalar1=0.0, scalar2=NEG, op0=ALU.is_gt, op1=ALU.mult), ["identf"], ["negm"])
            V("tensor_scalar", dict(out=identf[:], in0=iotaf[:], scalar1=0.0, scalar2=None, op0=ALU.is_equal), ["identf", "identb", "causf", "negm"], ["identf"])
            G("memset", dict(ap=onesf[:], constant=1.0), w=["onesf"])
            P.dma("sync", "c0b", dict(out=gmix_b[:], in_=norm_mix[0:1, :].partition_broadcast(128)), writes=["gmix_b"])
            hnT = sb(ms, "hnT", [128, 8, LP], BF16)
            big1 = sb(ms, "big1", [128, 8, LP], BF16)
            wgate = sb(ms, "wgate", [128, 8, 16], BF16)
            P.dma("gpsimd", "c1", dict(out=wgate[:, :, 0:8], in_=w_in[:, 2048:2056].rearrange("(dc p) c -> p dc c", p=128)), writes=["wgate"])
            P.dma("gpsimd", "c1", dict(out=wgate[:, :, 8:16], in_=w_in[:, 3592:3600].rearrange("(dc p) c -> p dc c", p=128)), writes=["wgate"])

            def tile_cols(j):
                return (0, 16) if j == 0 else (16 + 128 * (j - 1), 128)

            pps = ExitStack()
            pps.__enter__()
            if not _os.environ.get("NOPREP"):
                ub_ = [sb(pps, f"ub{i}", [128, 4, D], BF16) for i in range(2)]
                uts_ = [sb(pps, f"uts{i}", [128, 8, 512], BF16) for i in range(2)]
                ptu_ = [ps(pps, f"ptu{i}", BF16) for i in range(4)]
                prep_gen[0] = gen_prep(ub_, uts_, ptu_)
            with ExitStack() as pa:
                xt = [sb(pa, f"xt{i}", [128, D]) for i in range(2)]
                junk = sb(pa, "junkA", [128, D], BF16)
                hn = [sb(pa, f"hn{i}", [128, D], BF16) for i in range(2)]
                ssA = [sb(pa, f"ssA{i}", [128, 3]) for i in range(2)]
                tpA = [ps(pa, f"tpA{i}", BF16) for i in range(2)]
                for j in range(17):
                    c0, r = tile_cols(j)
                    b = j % 2
                    xtb, hnb, ssb, tpb = xt[b], hn[b], ssA[b], tpA[b]
                    kx, kh, ks, kt = f"xt{b}", f"hn{b}", f"ssA{b}", f"tpA{b}"
                    if j == 0:
                        P.dma("sync", kx, dict(out=xtb[0:16, :], in_=meta[:, :]), writes=[kx])
                    else:
                        P.dma("sync", kx, dict(out=xtb[:, :], in_=x[128 * (j - 1):128 * j, :]), writes=[kx])
                    V("scalar_tensor_tensor", dict(out=junk[0:r, :], in0=xtb[0:r, :], scalar=1.0, in1=xtb[0:r, :], op0=ALU.mult, op1=ALU.mult, accum_out=ssb[0:r, 0:1]), [kx], ["junkA", ks + "a"])
                    rstd_from_ss(ssb[0:r, 0:1], ssb[0:r, 2:3], float(D), r, [ks + "a"], [ks + "b", ks + "c"], ssb[0:r, 1:2])
                    V("scalar_tensor_tensor", dict(out=hnb[0:r, :], in0=xtb[0:r, :], scalar=ssb[0:r, 2:3], in1=gmix_b[0:r, :], op0=ALU.mult, op1=ALU.mult), [kx, ks + "c", "gmix_b"], [kh])
                    for dc in range(8):
                        T("transpose", dict(out=tpb[:, dc * 128:dc * 128 + r], in_=hnb[0:r, dc * 128:(dc + 1) * 128], identity=identb[0:r, 0:r]), [kh, "identb"], [kt])
                    tpv = tpb[:, :].rearrange("p (a b) -> p a b", a=8)
                    A("activation", dict(out=hnT[:, :, c0:c0 + r], in_=tpv[:, :, 0:r], func=AF.Copy), [kt], [("hnT", j)])
                    bg_step(int(_os.environ.get("BGA", 4)))
                G("memset", dict(ap=hnT[:, :, L:LP], constant=0.0), w=[("hnT", 17)])
                P.barrier(exclude=VEX)
                P.emit()
            HN_ALL = [("hnT", j) for j in range(18)]
            dump("hnT", hnT[:, :, 0:L], HN_ALL, [128, 8, L])

            if stop == "A":
                P.barrier(exclude=VEX)
                P.emit()
                return nc, dbg_out

            CB = [(0, 512), (512, 512), (1024, 512), (1536, 512), (2048, 64)]
            with ExitStack() as pb:
                wg = [sb(pb, f"wg{i}", [128, 8, 512], BF16) for i in range(2)]
                cw = sb(pb, "cw", [128, 4, 8])
                zc = [sb(pb, f"zc{i}", [128, LM]) for i in range(2)]
                ac = [sb(pb, f"ac{i}", [128, LM]) for i in range(2)]
                pz = [ps(pb, f"pz{i}") for i in range(2)]
                for j in range(4):
                    P.dma("sync", "cw", dict(out=cw[:, j, :], in_=conv_qk[j:j + 1, :].rearrange("o (k c) -> c (o k)", c=128), allow_slow_non_contiguous=True), writes=["cw"])
                it = 0
                for g in range(2):
                    wgb = wg[g]
                    P.dma("gpsimd", f"wg{g}", dict(out=wgb[:], in_=w_in[:, 512 * g:512 * (g + 1)].rearrange("(dc p) c -> p dc c", p=128)), writes=[f"wg{g}"])
                    for ck in range(4):
                        idx = g * 4 + ck
                        b = idx % 2
                        zcb, acb = zc[b], ac[b]
                        for (c0, wdt) in CB:
                            pzb = pz[it % 2]
                            kp = f"pz{it % 2}"
                            it += 1
                            for dc in range(8):
                                T("matmul", dict(out=pzb[:, 0:wdt], lhsT=wgb[:, dc, ck * 128:(ck + 1) * 128], rhs=hnT[:, dc, c0:c0 + wdt], start=(dc == 0), stop=(dc == 7)), [f"wg{g}"] + HN_ALL, [kp])
                            A("activation", dict(out=zcb[:, c0:c0 + wdt], in_=pzb[:, 0:wdt], func=AF.Copy), [kp], [f"zc{b}"])
                            bg_step(int(_os.environ.get("BGB", 30)))
                        V("tensor_scalar", dict(out=acb[:, :], in0=zcb[:, :], scalar1=cw[:, 3, idx:idx + 1], scalar2=None, op0=ALU.mult), [f"zc{b}", "cw"], [f"ac{b}"])
                        for sh in (1, 2, 3):
                            V("scalar_tensor_tensor", dict(out=acb[:, sh:LM], in0=zcb[:, 0:LM - sh], scalar=cw[:, 3 - sh, idx:idx + 1], in1=acb[:, sh:LM], op0=ALU.mult, op1=ALU.add), [f"zc{b}", "cw", f"ac{b}"], [f"ac{b}"])
                        A("activation", dict(out=big1[:, idx, 0:LM], in_=acb[:, :], func=AF.Silu), [f"ac{b}"], [("qk", idx)])
                bg_step(100000)
                P.barrier(exclude=VEX)
                P.emit()
            pps.close()
            QK_ALL = [("qk", i) for i in range(8)]
            dump("qk", big1[:, :, 0:L], QK_ALL, [128, 8, L])
            if stop == "B":
                P.barrier(exclude=VEX)
                P.emit()
                return nc, dbg_out

            with ExitStack() as pc:
                wv = sb(pc, "wv", [128, 8, 512], BF16)
                wo = sb(pc, "wo", [128, 8, 512], BF16)
                P.dma("gpsimd", "wv", dict(out=wv[:], in_=w_in[:, 1024:1536].rearrange("(dc p) c -> p dc c", p=128)), writes=["wv"])
                P.dma("gpsimd", "wo", dict(out=wo[:], in_=w_in[:, 1536:2048].rearrange("(dc p) c -> p dc c", p=128)), writes=["wo"])
                bm_b = sb(pc, "bm_b", [128, 8])
                P.dma("sync", "c2", dict(out=bm_b[:, 0:4], in_=b_igate[0:1, :].partition_broadcast(128)), writes=["bm_b"])
                P.dma("sync", "c2", dict(out=bm_b[:, 4:8], in_=b_fgate_m[0:1, :].partition_broadcast(128)), writes=["bm_b"])
                mg_b = sb(pc, "mg_b", [128, 512])
                P.dma("sync", "c2", dict(out=mg_b[:], in_=m_out_norm[0:1, :].partition_broadcast(128)), writes=["mg_b"])
                CT = sb(pc, "CT", [128, 4, 129])
                CTb = sb(pc, "CTb", [128, 4, 129], BF16)
                mst = sb(pc, "mst", [128, 4])
                G("memset", dict(ap=CT[:], constant=0.0), w=[("CT", h) for h in range(4)])
                G("memset", dict(ap=CTb[:], constant=0.0), w=[("CTb", h) for h in range(4)])
                G("memset", dict(ap=mst[:], constant=0.0), w=["mst"])
                NB = 2
                vx = [sb(pc, f"vx{i}", [64, 4, 129], BF16) for i in range(NB)]
                for i in range(NB):
                    G("memset", dict(ap=vx[i][:], constant=1.0), w=[f"vx{i}"])
                og = [sb(pc, f"og{i}", [64, 512]) for i in range(NB)]
                sm = [sb(pc, f"sm{i}", [128, 64]) for i in range(NB)]
                dg = [sb(pc, f"dg{i}", [64, 4, 64]) for i in range(NB)]
                mk = [sb(pc, f"mk{i}", [64, 4, 64]) for i in range(NB)]
                spT = [sb(pc, f"spT{i}", [64, 64], BF16) for i in range(4)]
                tmpB = [sb(pc, f"tmpB{i}", [64, 129]) for i in range(2)]
                num = [sb(pc, f"num{i}", [64, 129]) for i in range(2)]
                junkc = sb(pc, "junkc", [64, 128], BF16)
                hs = [sb(pc, f"hs{i}", [64, 8]) for i in range(4)]
                ytm = [sb(pc, f"ytm{i}", [64, 4, 128], BF16) for i in range(NB)]
                kw = [sb(pc, f"kw{i}", [64, 128], BF16) for i in range(2)]
                pg = ps(pc, "pg")
                pv = ps(pc, "pv")
                po = ps(pc, "po")
                pqk = ps(pc, "pqk")
                pAB = [ps(pc, f"pAB{i}") for i in range(2)]
                pU = ps(pc, "pU")
                ptr = ps(pc, "ptr", BF16)

                SM = dict(gt=0, e1=8, lfn=12, a=16, ea=20, cm=24, amax=28, M=32, Mend=36, eM=40, w=44, wg=48, dec=52, emt=56, t0=60)
                import os
                CLIM = int(os.environ.get("CLIM", NCH))
                LVL = int(os.environ.get("LVL", 99))
                def chunk_ctx(c):
                    Tn = 64
                    c0 = 64 * c
                    b = c % NB
                    s_ = sm[b]
                    ksm = lambda nm, b=b: (f"sm{b}", nm)
                    col = lambda nm, w=4, s_=s_, r=None: s_[0:(Tn if r is None else r), SM[nm]:SM[nm] + w]
                    hk = HN_ALL
                    return Tn, c0, b, s_, ksm, col, hk

                def gen_pre(c):
                    Tn, c0, b, s_, ksm, col, hk = chunk_ctx(c)
                    if c < 32 and not _os.environ.get("NOPREP"):
                        for r in (2 * c, 2 * c + 1):
                            P.dma("gpsimd", "vprep", dict(out=vb_s[256 * r:256 * r + 256, :], in_=peer_v[256 * r:256 * r + 256, :]), writes=["vb_s"])
                            yield
                    for dc in range(8):
                        T("matmul", dict(out=pg[0:Tn, 0:8], lhsT=hnT[:, dc, c0:c0 + Tn], rhs=wgate[:, dc, 0:8], start=(dc == 0), stop=(dc == 7)), hk + ["wgate"], ["pg"])
                        yield
                    V("tensor_tensor", dict(out=col("gt", 8), in0=pg[0:Tn, 0:8], in1=bm_b[0:Tn, :], op=ALU.add), ["pg", "bm_b"], [ksm("gt")])
                    yield
                    A("activation", dict(out=col("e1"), in_=s_[0:Tn, 4:8], func=AF.Exp, scale=-1.0), [ksm("gt")], [ksm("e1")])
                    yield
                    A("activation", dict(out=col("lfn"), in_=col("e1"), func=AF.Ln, bias=onec[0:Tn, :], scale=1.0), [ksm("e1"), "onec"], [ksm("lfn")])
                    yield
                    T("matmul", dict(out=pg[0:Tn, 8:12], lhsT=causf[0:Tn, 0:Tn], rhs=col("lfn"), start=True, stop=True), [ksm("lfn"), "causf"], ["pg"])
                    yield
                    T("matmul", dict(out=pg[:, 16:20], lhsT=onesf[0:Tn, :], rhs=col("lfn"), start=True, stop=True), [ksm("lfn"), "onesf"], ["pg"])
                    yield
                    V("tensor_tensor", dict(out=col("a"), in0=s_[0:Tn, 0:4], in1=pg[0:Tn, 8:12], op=ALU.add), [ksm("gt"), "pg"], [ksm("a")])
                    yield
                    dgb, mkb = dg[b], mk[b]
                    V("tensor_tensor", dict(out=dgb[0:Tn, :, 0:Tn], in0=identf[0:Tn, 0:Tn].unsqueeze(1).broadcast_to([Tn, 4, Tn]), in1=col("a").unsqueeze(2).broadcast_to([Tn, 4, Tn]), op=ALU.mult), [ksm("a"), "identf"], [f"dg{b}"])
                    yield
                    abc = pg[:, 256:512].rearrange("p (h s) -> p h s", h=4)
                    for h in range(4):
                        T("matmul", dict(out=pg[:, 256 + 64 * h:256 + 64 * h + Tn], lhsT=onesf[0:Tn, :], rhs=dgb[0:Tn, h, 0:Tn], start=True, stop=True), [f"dg{b}", "onesf"], ["pg"])
                        yield
                    V("tensor_tensor", dict(out=mkb[0:Tn, :, 0:Tn], in0=abc[0:Tn, :, 0:Tn], in1=negm[0:Tn, 0:Tn].unsqueeze(1).broadcast_to([Tn, 4, Tn]), op=ALU.add), ["pg", "negm"], [f"mk{b}"])
                    yield
                    V("tensor_reduce", dict(out=col("cm"), in_=mkb[0:Tn, :, 0:Tn], axis=AX.X, op=ALU.max), [f"mk{b}"], [ksm("cm")])
                    yield
                    V("tensor_reduce", dict(out=s_[:, SM["amax"]:SM["amax"] + 4], in_=abc[:, :, 0:Tn], axis=AX.X, op=ALU.max), ["pg"], [ksm("amax")])
                    yield
                    A("activation", dict(out=col("ea"), in_=col("a"), func=AF.Exp, bias=lnsc[0:Tn, :], scale=1.0), [ksm("a"), "lnsc"], [ksm("ea")])
                    yield
                    V("tensor_tensor", dict(out=col("M"), in0=col("cm"), in1=mst[0:Tn, :], op=ALU.max), [ksm("cm"), "mst"], [ksm("M")])
                    yield
                    V("tensor_tensor", dict(out=s_[:, SM["Mend"]:SM["Mend"] + 4], in0=s_[:, SM["amax"]:SM["amax"] + 4], in1=mst[:, :], op=ALU.max), [ksm("amax"), "mst"], [ksm("Mend")])
                    yield
                    A("activation", dict(out=col("eM"), in_=col("M"), func=AF.Exp, scale=-1.0), [ksm("M")], [ksm("eM")])
                    yield
                    V("tensor_tensor", dict(out=col("w"), in0=mst[0:Tn, :], in1=col("M"), op=ALU.subtract), [ksm("M"), "mst"], [ksm("w")])
                    yield
                    A("activation", dict(out=col("w"), in_=col("w"), func=AF.Exp), [ksm("w")], [ksm("w")])
                    yield
                    V("tensor_tensor", dict(out=col("wg"), in0=col("a"), in1=col("Mend"), op=ALU.subtract), [ksm("a"), ksm("Mend")], [ksm("wg")])
                    yield
                    A("activation", dict(out=col("wg"), in_=col("wg"), func=AF.Exp, bias=lnsc[0:Tn, :], scale=1.0), [ksm("wg"), "lnsc"], [ksm("wg")])
                    yield
                    V("tensor_tensor", dict(out=s_[:, SM["dec"]:SM["dec"] + 4], in0=mst[:, :], in1=s_[:, SM["Mend"]:SM["Mend"] + 4], op=ALU.subtract), [ksm("Mend"), "mst"], [ksm("dec")])
                    yield
                    A("activation", dict(out=s_[:, SM["dec"]:SM["dec"] + 4], in_=s_[:, SM["dec"]:SM["dec"] + 4], func=AF.Exp), [ksm("dec")], [ksm("dec")])
                    yield
                    V("tensor_tensor", dict(out=col("emt"), in0=pg[0:Tn, 8:12], in1=col("M"), op=ALU.subtract), [ksm("M"), "pg"], [ksm("emt")])
                    yield
                    A("activation", dict(out=col("emt"), in_=col("emt"), func=AF.Exp), [ksm("emt")], [ksm("emt")])
                    yield
                    V("tensor_tensor", dict(out=mst[:, :], in0=s_[:, SM["Mend"]:SM["Mend"] + 4], in1=pg[:, 16:20], op=ALU.subtract), [ksm("Mend"), "pg", "mst"], ["mst"])
                    yield
                    vxb, ogb = vx[b], og[b]
                    for dc in range(8):
                        T("matmul", dict(out=pv[0:Tn, :], lhsT=hnT[:, dc, c0:c0 + Tn], rhs=wv[:, dc, :], start=(dc == 0), stop=(dc == 7)), hk + ["wv"], ["pv"])
                        yield
                    A("activation", dict(out=vxb[0:Tn, :, 0:128], in_=pv[0:Tn, :].rearrange("p (h d) -> p h d", h=4), func=AF.Copy), ["pv"], [f"vx{b}"])
                    yield
                    for dc in range(8):
                        T("matmul", dict(out=po[0:Tn, :], lhsT=hnT[:, dc, c0:c0 + Tn], rhs=wo[:, dc, :], start=(dc == 0), stop=(dc == 7)), hk + ["wo"], ["po"])
                        yield
                    A("activation", dict(out=ogb[0:Tn, :], in_=po[0:Tn, :], func=AF.Exp, scale=-1.0), ["po"], [f"og{b}"])
                    yield
                    V("tensor_scalar", dict(out=ogb[0:Tn, :], in0=ogb[0:Tn, :], scalar1=1.0, scalar2=None, op0=ALU.add), [f"og{b}"], [f"og{b}"])
                    yield
                    V("reciprocal", dict(out=ogb[0:Tn, :], in_=ogb[0:Tn, :]), [f"og{b}"], [f"og{b}"])
                    yield
                    V("tensor_tensor", dict(out=ogb[0:Tn, :], in0=ogb[0:Tn, :], in1=mg_b[0:Tn, :], op=ALU.mult), [f"og{b}", "mg_b"], [f"og{b}"])
                    yield
                    ytb = ytm[b]

                def gen_head(c, h):
                    Tn, c0, b, s_, ksm, col, hk = chunk_ctx(c)
                    vxb, ogb, ytb = vx[b], og[b], ytm[b]
                    hb = h % 2
                    qTc = big1[:, h, c0:c0 + Tn]
                    kTc = big1[:, 4 + h, c0:c0 + Tn]
                    sp = spT[h]
                    pab = pAB[hb]
                    hsb = hs[h]
                    khs = lambda nm, h=h: (f"hs{h}", nm)
                    T("matmul", dict(out=pqk[0:Tn, 64 * h:64 * h + Tn], lhsT=kTc, rhs=qTc, start=True, stop=True), [("qk", h), ("qk", 4 + h)], ["pqk"])
                    yield
                    V("scalar_tensor_tensor", dict(out=sp[0:Tn, 0:Tn], in0=pqk[0:Tn, 64 * h:64 * h + Tn], scalar=col("ea")[:, h:h + 1], in1=causf[0:Tn, 0:Tn], op0=ALU.mult, op1=ALU.mult), ["pqk", ksm("ea"), "causf"], [f"spT{h}"])
                    yield
                    T("matmul", dict(out=pab[0:Tn, 0:129], lhsT=sp[0:Tn, 0:Tn], rhs=vxb[0:Tn, h, :], start=True, stop=True), [f"spT{h}", f"vx{b}"], [f"pAB{hb}"])
                    yield
                    T("matmul", dict(out=pab[0:Tn, 256:385], lhsT=qTc, rhs=CTb[:, h, :], start=True, stop=True), [("qk", h), ("CTb", h)], [f"pAB{hb}"])
                    yield
                    tB, nm = tmpB[hb], num[hb]
                    A("activation", dict(out=tB[0:Tn, :], in_=pab[0:Tn, 256:385], func=AF.Copy, scale=col("w")[:, h:h + 1]), [f"pAB{hb}", ksm("w")], [f"tmpB{hb}"])
                    yield
                    V("scalar_tensor_tensor", dict(out=nm[0:Tn, :], in0=pab[0:Tn, 0:129], scalar=col("eM")[:, h:h + 1], in1=tB[0:Tn, :], op0=ALU.mult, op1=ALU.add), [f"pAB{hb}", f"tmpB{hb}", ksm("eM")], [f"num{hb}"])
                    yield
                    A("activation", dict(out=hsb[0:Tn, 7:8], in_=nm[0:Tn, 128:129], func=AF.Abs), [f"num{hb}"], [khs("abs")])
                    yield
                    V("tensor_scalar", dict(out=hsb[0:Tn, 0:1], in0=hsb[0:Tn, 7:8], scalar1=col("emt")[:, h:h + 1], scalar2=None, op0=ALU.max), [khs("abs"), ksm("emt")], [khs("den")])
                    yield
                    V("reciprocal", dict(out=hsb[0:Tn, 1:2], in_=hsb[0:Tn, 0:1]), [khs("den")], [khs("rden")])
                    yield
                    V("scalar_tensor_tensor", dict(out=junkc[0:Tn, :], in0=nm[0:Tn, 0:128], scalar=1.0, in1=nm[0:Tn, 0:128], op0=ALU.mult, op1=ALU.mult, accum_out=hsb[0:Tn, 2:3]), [f"num{hb}"], ["junkc", khs("ss")])
                    yield
                    V("tensor_scalar", dict(out=hsb[0:Tn, 3:4], in0=hsb[0:Tn, 2:3], scalar1=hsb[0:Tn, 1:2], scalar2=hsb[0:Tn, 1:2], op0=ALU.mult, op1=ALU.mult), [khs("ss"), khs("rden")], [khs("t1")])
                    yield
                    rstd_from_ss(hsb[0:Tn, 3:4], hsb[0:Tn, 5:6], 128.0, Tn, [khs("t1")], [khs("ln"), khs("rstd")], hsb[0:Tn, 4:5])
                    yield
                    V("tensor_tensor", dict(out=hsb[0:Tn, 6:7], in0=hsb[0:Tn, 5:6], in1=hsb[0:Tn, 1:2], op=ALU.mult), [khs("rstd"), khs("rden")], [khs("sc")])
                    yield
                    V("scalar_tensor_tensor", dict(out=ytb[0:Tn, h, :], in0=nm[0:Tn, 0:128], scalar=hsb[0:Tn, 6:7], in1=ogb[0:Tn, 128 * h:128 * (h + 1)], op0=ALU.mult, op1=ALU.mult), [f"num{hb}", khs("sc"), f"og{b}"], [(f"ytm{b}", h)])
                    yield
                    T("transpose", dict(out=ptr[:, 64 * h:64 * h + Tn], in_=ytb[0:Tn, h, :], identity=identb[0:Tn, 0:Tn]), [(f"ytm{b}", h), "identb"], ["ptr"])
                    yield
                    A("activation", dict(out=yT[:, h, c0:c0 + Tn], in_=ptr[:, 64 * h:64 * h + Tn], func=AF.Copy), ["ptr"], [("yT", h, c)])
                    yield
                    kwb = kw[hb]
                    T("transpose", dict(out=ptr[0:Tn, 512 + 128 * hb:512 + 128 * hb + 128], in_=kTc, identity=identb[:, :]), [("qk", 4 + h), "identb"], ["ptr"])
                    yield
                    A("activation", dict(out=kwb[0:Tn, :], in_=ptr[0:Tn, 512 + 128 * hb:512 + 128 * hb + 128], func=AF.Copy, scale=col("wg")[:, h:h + 1]), ["ptr", ksm("wg")], [f"kw{hb}"])
                    yield
                    T("matmul", dict(out=pU[:, 256 * hb:256 * hb + 129], lhsT=kwb[0:Tn, :], rhs=vxb[0:Tn, h, :], start=True, stop=True), [f"kw{hb}", f"vx{b}"], ["pU"])
                    yield
                    V("scalar_tensor_tensor", dict(out=CT[:, h, :], in0=CT[:, h, :], scalar=s_[:, SM["dec"] + h:SM["dec"] + h + 1], in1=pU[:, 256 * hb:256 * hb + 129], op0=ALU.mult, op1=ALU.add), [("CT", h), ksm("dec"), "pU"], [("CT", h)])
                    yield
                    A("activation", dict(out=CTb[:, h, :], in_=CT[:, h, :], func=AF.Copy), [("CT", h)], [("CTb", h)])
                    yield


                def run_rr(gens, bg=()):
                    gens = list(gens)
                    bg = list(bg)
                    while gens:
                        alive = []
                        for g_ in gens:
                            try:
                                next(g_)
                                alive.append(g_)
                            except StopIteration:
                                pass
                        gens = alive
                        for g_ in list(bg):
                            try:
                                next(g_)
                            except StopIteration:
                                bg.remove(g_)
                    return bg

                NCC = min(NCH, CLIM)
                run_rr([gen_pre(0)])
                for c in range(NCC):
                    bg = [gen_pre(c + 1)] if c + 1 < NCC else []
                    bg = run_rr([gen_head(c, 0), gen_head(c, 1)], bg)
                    bg = run_rr([gen_head(c, 2), gen_head(c, 3)], bg)
                    run_rr(bg)
                P.barrier(exclude=VEX)
                P.emit()
            dump("yTm", yT[:, 0:4, 0:L], [], [128, 4, L])
            if stop == "C":
                P.barrier(exclude=VEX)
                P.emit()
                return nc, dbg_out

            NBLK = 17
            kTe = sb(ms, "kTe", [70, 8, LP], BF16)
            fvx = sb(ms, "fvx", [128, NBLK, 8, 65], BF16)
            prow = sb(ms, "prow", [48, LP], BF16)
            qTe = big1
            with ExitStack() as pd:
                gq_b = sb(pd, "gq_b", [128, 64])
                gk_b = sb(pd, "gk_b", [128, 64])
                nbff = sb(pd, "nbff", [8, 1])
                P.dma("sync", "c3", dict(out=gq_b[:], in_=f_q_norm[0:1, :].partition_broadcast(128)), writes=["gq_b"])
                P.dma("sync", "c3", dict(out=gk_b[:], in_=f_k_norm[0:1, :].partition_broadcast(128)), writes=["gk_b"])
                P.dma("sync", "c3", dict(out=nbff[:], in_=b_fgate_f[:, :]), writes=["nbff"])
                bffb = sb(pd, "bffb", [128, 8])
                P.dma("sync", "c3", dict(out=bffb[:], in_=b_fgate_f.rearrange("h o -> o h").partition_broadcast(128)), writes=["bffb"])
                V("tensor_scalar", dict(out=gq_b[:], in0=gq_b[:], scalar1=0.125, scalar2=None, op0=ALU.mult), ["gq_b"], ["gq_b"])
                V("tensor_scalar", dict(out=nbff[:], in0=nbff[:], scalar1=-1.0, scalar2=None, op0=ALU.mult), ["nbff"], ["nbff"])
                G("memset", dict(ap=fvx[:], constant=1.0), w=["fvx"])
                G("memset", dict(ap=qTe[64:70, :, :], constant=1.0), w=["qTe_ext"])
                G("memset", dict(ap=kTe[64:70, :, :], constant=1.0), w=["kTe_ext"])
                wf = [sb(pd, f"wf{i}", [128, 8, 512], BF16) for i in range(2)]
                qs = [sb(pd, f"qs{i}", [128, 512]) for i in range(2)]
                sq = sb(pd, "sqD", [128, 512])
                qtm = [sb(pd, f"qtm{i}", [128, 512], BF16) for i in range(2)]
                ssd = [sb(pd, f"ssd{i}", [128, 24]) for i in range(2)]
                pq = [ps(pd, f"pq{i}") for i in range(2)]
                ptq = [ps(pd, f"ptq{i}", BF16) for i in range(2)]
                pgf = ps(pd, "pgf")
                lfd = sb(pd, "lfd", [128, NBLK, 8])
                cnd = sb(pd, "cnd", [128, NBLK, 8])
                tot = sb(pd, "tot", [128, NBLK + 1, 8])
                for i in range(NBLK):
                    c0 = 128 * i
                    for dc in range(8):
                        T("matmul", dict(out=pgf[:, 8 * i:8 * i + 8], lhsT=hnT[:, dc, c0:c0 + 128], rhs=wgate[:, dc, 8:16], start=(dc == 0), stop=(dc == 7)), HN_ALL + ["wgate"], ["pgf"])
                pgv = pgf[:, 0:8 * NBLK].rearrange("p (b h) -> p b h", h=8)
                V("tensor_tensor", dict(out=lfd[:], in0=pgv, in1=bffb[:, :].unsqueeze(1).broadcast_to([128, NBLK, 8]), op=ALU.add), ["pgf", "bffb"], ["lfd"])
                A("activation", dict(out=lfd[:], in_=lfd[:], func=AF.Exp, scale=-1.0), ["lfd"], ["lfd"])
                A("activation", dict(out=lfd[:], in_=lfd[:], func=AF.Ln, bias=onec[:, :], scale=1.0), ["lfd", "onec"], ["lfd"])
                lf2 = lfd[:, :, :].rearrange("p b h -> p (b h)")
                T("matmul", dict(out=pgf[:, 256:256 + 8 * NBLK], lhsT=causf[:, :], rhs=lf2, start=True, stop=True), ["lfd", "causf"], ["pgf"])
                V("tensor_copy", dict(out=cnd[:, :, :].rearrange("p b h -> p (b h)"), in_=pgf[:, 256:256 + 8 * NBLK]), ["pgf"], ["cnd"])
                T("matmul", dict(out=pgf[:, 256:256 + 8 * NBLK], lhsT=onesf[:, :], rhs=lf2, start=True, stop=True), ["lfd", "onesf", "cnd"], ["pgf"])
                G("memset", dict(ap=tot[:, 0, :], constant=0.0), w=["tot"])
                for i in range(NBLK):
                    V("tensor_tensor", dict(out=tot[:, i + 1, :], in0=tot[:, i, :], in1=pgf[:, 256 + 8 * i:256 + 8 * i + 8], op=ALU.add), ["tot", "pgf"], ["tot"])
                V("tensor_tensor", dict(out=cnd[:], in0=cnd[:], in1=tot[:, 0:NBLK, :], op=ALU.add), ["cnd", "tot"], ["cnd"])
                parts = sb(pd, "parts", [128, NBLK, 48], BF16)
                r1 = sb(pd, "r1d", [128, NBLK, 8])
                pv3 = lambda a, b_: parts[:, :, a:b_]
                V("tensor_copy", dict(out=pv3(0, 8), in_=cnd[:]), ["cnd"], ["parts0"])
                V("tensor_tensor", dict(out=r1[:], in0=cnd[:], in1=pv3(0, 8), op=ALU.subtract), ["cnd", "parts0"], ["r1d"])
                V("tensor_copy", dict(out=pv3(8, 16), in_=r1[:]), ["r1d"], ["parts1"])
                V("tensor_tensor", dict(out=r1[:], in0=r1[:], in1=pv3(8, 16), op=ALU.subtract), ["r1d", "parts1"], ["r1d"])
                V("tensor_copy", dict(out=pv3(16, 24), in_=r1[:]), ["r1d"], ["parts2"])
                V("tensor_scalar", dict(out=pv3(24, 48), in0=pv3(0, 24), scalar1=-1.0, scalar2=None, op0=ALU.mult), ["parts0", "parts1", "parts2"], ["parts3"])
                PK = ["parts0", "parts1", "parts2", "parts3"]
                for i in range(NBLK):
                    tpb = ptq[i % 2]
                    T("transpose", dict(out=tpb[0:48, 0:128], in_=parts[:, i, :], identity=identb[:, :]), PK + ["identb"], [f"ptq{i % 2}"])
                    A("activation", dict(out=prow[:, 128 * i:128 * i + 128], in_=tpb[0:48, 0:128], func=AF.Copy), [f"ptq{i % 2}"], ["prow"])
                for h in range(8):
                    for j in range(3):
                        P.dma("sync", "ext", dict(out=kTe[67 + j:68 + j, h, :], in_=prow[8 * j + h:8 * j + h + 1, :]), ["prow", "kTe_ext"], [("kTe_ext", h, j)])
                        P.dma("sync", "ext", dict(out=qTe[64 + j:65 + j, h, :], in_=prow[24 + 8 * j + h:24 + 8 * j + h + 1, :]), ["prow", "qTe_ext"], [("qTe_ext", h, j)])
                wsrc = [(2056, "q"), (2568, "k"), (3080, "v")]
                sq2 = [sq, sb(pd, "sqD2", [128, 512])]

                def gen_blk(wfb, kw_, kind, i, b):
                    c0 = 128 * i
                    pqb, qsb, qtb, ssb, tpb, sqb = pq[b], qs[b], qtm[b], ssd[b], ptq[b], sq2[b]
                    for dc in range(8):
                        T("matmul", dict(out=pqb[:, :], lhsT=hnT[:, dc, c0:c0 + 128], rhs=wfb[:, dc, :], start=(dc == 0), stop=(dc == 7)), HN_ALL + [kw_], [f"pq{b}"])
                        yield
                    if kind == "v":
                        A("activation", dict(out=fvx[:, i, :, 0:64], in_=pqb[:, :].rearrange("p (h d) -> p h d", h=8), func=AF.Copy), [f"pq{b}", "fvx"], [("fvx", i)])
                        yield
                        return
                    gb_ = gq_b if kind == "q" else gk_b
                    A("activation", dict(out=qsb[:, :], in_=pqb[:, :], func=AF.Copy), [f"pq{b}"], [f"qs{b}"])
                    yield
                    V("tensor_tensor", dict(out=sqb[:, :], in0=qsb[:, :], in1=qsb[:, :], op=ALU.mult), [f"qs{b}"], [f"sqD{b}"])
                    yield
                    V("tensor_reduce", dict(out=ssb[:, 0:8], in_=sqb[:, :].rearrange("p (h d) -> p h d", h=8), axis=AX.X, op=ALU.add), [f"sqD{b}"], [f"ssd{b}a"])
                    yield
                    rstd_from_ss(ssb[:, 0:8], ssb[:, 16:24], 64.0, 128, [f"ssd{b}a"], [f"ssd{b}b", f"ssd{b}c"], ssb[:, 8:16])
                    yield
                    q3 = qsb[:, :].rearrange("p (h d) -> p h d", h=8)
                    V("tensor_tensor", dict(out=q3, in0=q3, in1=ssb[:, 16:24].unsqueeze(2).broadcast_to([128, 8, 64]), op=ALU.mult), [f"qs{b}", f"ssd{b}c"], [f"qs{b}"])
                    yield
                    V("tensor_tensor", dict(out=qtb[:, :].rearrange("p (h d) -> p h d", h=8), in0=q3, in1=gb_[:, :].unsqueeze(1).broadcast_to([128, 8, 64]), op=ALU.mult), [f"qs{b}", "gq_b", "gk_b"], [f"qtm{b}"])
                    yield
                    for h in range(8):
                        T("transpose", dict(out=tpb[0:64, 128 * h:128 * h + 128], in_=qtb[:, 64 * h:64 * h + 64], identity=identb[:, :]), [f"qtm{b}", "identb"], [f"ptq{b}"])
                        yield
                    dst = qTe if kind == "q" else kTe
                    A("activation", dict(out=dst[0:64, :, c0:c0 + 128], in_=tpb[0:64, :].rearrange("p (h t) -> p h t", h=8), func=AF.Copy), [f"ptq{b}"], [(kind + "T", i)])
                    yield

                def run_rr2(gens):
                    gens = list(gens)
                    while gens:
                        alive = []
                        for g_ in gens:
                            try:
                                next(g_)
                                alive.append(g_)
                            except StopIteration:
                                pass
                        gens = alive

                for wi, (wc0, kind) in enumerate(wsrc):
                    wfb = wf[wi % 2]
                    kw_ = f"wf{wi % 2}"
                    P.dma("gpsimd", kw_, dict(out=wfb[:], in_=w_in[:, wc0:wc0 + 512].rearrange("(dc p) c -> p dc c", p=128)), writes=[kw_])
                    for i in range(0, NBLK, 2):
                        gl = [gen_blk(wfb, kw_, kind, i, 0)]
                        if i + 1 < NBLK:
                            gl.append(gen_blk(wfb, kw_, kind, i + 1, 1))
                        run_rr2(gl)
                P.barrier(exclude=VEX)
                P.emit()
            dump("qTe", qTe[0:70, :, 0:L], [], [70, 8, L])
            dump("kTe", kTe[0:70, :, 0:L], [], [70, 8, L])
            if stop == "D1":
                P.barrier(exclude=VEX)
                P.emit()
                return nc, dbg_out
            with ExitStack() as pd2:
                pT = [sb(pd2, f"pT{i}", [128, 512], BF16) for i in range(3)]
                ytf = [sb(pd2, f"ytf{i}", [128, 4, 512], BF16) for i in range(2)]
                rinv = [sb(pd2, f"rinv{i}", [128, 4]) for i in range(2)]
                lg = [ps(pd2, f"lg{i}") for i in range(2)]
                pacc = [ps(pd2, f"pacc{i}") for i in range(4)]
                ptr2 = [ps(pd2, f"ptr2{i}", BF16) for i in range(2)]
                it = 0
                for TS in range(5):
                    jl = [j for j in range(4 * TS, min(4 * TS + 4, NBLK))]
                    nj = len(jl)
                    t0 = 512 * TS
                    WT = 128 * nj
                    ytb = ytf[TS % 2]
                    for h in range(8):
                        rb = rinv[h % 2]
                        nI = jl[-1] + 1

                        def qk(i, itv):
                            b2 = itv % 2
                            ts_ = max(t0, 128 * i)
                            Wd = t0 + WT - ts_
                            T("matmul", dict(out=lg[b2][:, 0:Wd], lhsT=kTe[0:70, h, 128 * i:128 * i + 128], rhs=qTe[0:70, h, ts_:ts_ + Wd], start=True, stop=True), [], [f"lg{b2}"])

                        qk(0, it)
                        for i in range(nI):
                            b3 = it % 3
                            b2 = it % 2
                            ts_ = max(t0, 128 * i)
                            Wd = t0 + WT - ts_
                            A("activation", dict(out=pT[b3][:, 0:Wd], in_=lg[b2][:, 0:Wd], func=AF.Exp), [f"lg{b2}"], [f"pT{b3}"])
                            if 128 * i >= t0:
                                G("tensor_tensor", dict(out=pT[b3][:, 0:128], in0=pT[b3][:, 0:128], in1=causb[:, :], op=ALU.mult), [f"pT{b3}", "causb"], [f"pT{b3}"])
                            if i + 1 < nI:
                                qk(i + 1, it + 1)
                            for j in jl:
                                if j < i:
                                    continue
                                jj = j - 4 * TS
                                co = 128 * j - ts_
                                T("matmul", dict(out=pacc[jj][:, 0:65], lhsT=pT[b3][:, co:co + 128], rhs=fvx[:, i, h, :], start=(i == 0), stop=(i == j)), [f"pT{b3}"], [f"pacc{jj}"])
                            it += 1
                        for jj in range(nj):
                            V("reciprocal", dict(out=rb[:, jj:jj + 1], in_=pacc[jj][:, 64:65]), [f"pacc{jj}"], [(f"rinv{h % 2}", jj)])
                            V("tensor_scalar", dict(out=ytb[:, jj, 64 * h:64 * h + 64], in0=pacc[jj][:, 0:64], scalar1=rb[:, jj:jj + 1], scalar2=None, op0=ALU.mult), [f"pacc{jj}", (f"rinv{h % 2}", jj)], [(f"ytf{TS % 2}", h)])
                    for jj, j in enumerate(jl):
                        tb = ptr2[jj % 2]
                        for pr in range(4):
                            T("transpose", dict(out=tb[:, 128 * pr:128 * pr + 128], in_=ytb[:, jj, 128 * pr:128 * pr + 128], identity=identb[:, :]), [(f"ytf{TS % 2}", hh) for hh in range(8)] + ["identb"], [f"ptr2{jj % 2}"])
                        A("activation", dict(out=yT[:, 4:8, 128 * j:128 * j + 128], in_=tb[:, 0:512].rearrange("p (a t) -> p a t", a=4), func=AF.Copy), [f"ptr2{jj % 2}"], [("yTf", j)])
                P.barrier(exclude=VEX)
                P.emit()
            dump("yT", yT[:, :, 0:L], [], [128, 8, L])
            if stop == "D":
                P.barrier(exclude=VEX)
                P.emit()
                return nc, dbg_out


        with ExitStack() as pf:
            wo16 = sb(pf, "wo16", [128, 8, D], BF16)
            wq16 = sb(pf, "wq16", [128, 8, 2048], BF16)
            keysT = sb(pf, "keysT", [128, 16, 128])
            P.dma("gpsimd", "wo16", dict(out=wo16[:], in_=w_out.rearrange("(cc p) d -> p cc d", p=128)), writes=["wo16"])
            for hf in range(2):
                P.dma("gpsimd", "wq16", dict(out=wq16[:, :, 1024 * hf:1024 * hf + 1024], in_=peer_query[:, 1024 * hf:1024 * hf + 1024].rearrange("(dc p) c -> p dc c", p=128)), writes=["wq16"])
            pacc = [ps(pf, f"paccF{i}") for i in range(2)]
            ptx = ps(pf, "ptx", BF16)
            pS = ps(pf, "pS")
            pa = [ps(pf, f"pa{i}") for i in range(2)]
            pgt = [ps(pf, f"pgt{i}") for i in range(2)]
            with ExitStack() as pk:
                knat = sb(pk, "knat", [128, 16, 128])
                P.dma("sync", "knat", dict(out=knat[:], in_=peer_keys.rearrange("(hp n) c -> n hp c", n=128)), writes=["knat"])
                for hp in range(16):
                    T("transpose", dict(out=pS[:, 128 * (hp % 4):128 * (hp % 4) + 128], in_=knat[:, hp, :], identity=identf[:, :]), ["knat", "identf"], ["pS"])
                    if hp % 4 == 3:
                        A("activation", dict(out=keysT[:, hp - 3:hp + 1, :], in_=pS[:, :].rearrange("p (a n) -> p a n", a=4), func=AF.Copy), ["pS"], ["keysT"])
                P.barrier(exclude=VEX)
                P.emit()
            xt2 = sb(pf, "xt2", [128, D])
            h2t = sb(pf, "h2t", [128, D])
            xn16 = sb(pf, "xn16", [128, D], BF16)
            xnT = sb(pf, "xnT", [128, 8, 128], BF16)
            scr16 = sb(pf, "scr16", [128, 2048])
            qT_sb = scr16[:, 0:2048].rearrange("p (a t) -> p a t", a=16)
            apre = scr16[:, :].rearrange("p (s e) -> p s e", s=4)
            S_sb = sb(pf, "S_sb", [128, 16, 128])
            v1 = sb(pf, "v1", [128, 16, 16])
            wk = [sb(pf, f"wk{i}", [128, 128]) for i in range(4)]
            cand = [sb(pf, f"cand{i}", [128, 256]) for i in range(2)]
            wk2 = [sb(pf, f"wk2{i}", [128, 256]) for i in range(2)]
            ct = sb(pf, "ct", [128, 8, 16])
            sv = sb(pf, "sv", [128, 64])
            dd = sb(pf, "dd", [128, 8, 16])
            S1t = sb(pf, "S1t", [128, 8, 128])
            UT = [sb(pf, f"UT{i}", [128, 8, 512], BF16) for i in range(2)]
            Vt = [sb(pf, f"Vt{i}", [128, 4, D], BF16) for i in range(2)]
            gA = sb(pf, "gA", [128, 4, 512], BF16)
            tmps = [[sb(pf, f"tmp{p_}_{i}", [128, 512], BF16) for i in range(8)] for p_ in range(2)]
            NZ = 6
            zt = [sb(pf, f"zt{i}", [128, 4, 128]) for i in range(NZ)]
            ez = [sb(pf, f"ez{i}", [128, 512], BF16) for i in range(NZ)]
            wT = [sb(pf, f"wT{i}", [128, 4, 128], BF16) for i in range(2)]
            ADDENG = os.environ.get("ADDENG", "PPPAPPPA")
            FH = [h for h in range(8) if ADDENG[h] == "F"]
            S2e = sb(pf, "S2e", [128, max(1, len(FH)), 128 if FH else 1])
            EH = [h for h in range(8) if ADDENG[h] == "E"]
            SBe = sb(pf, "SBe", [128, max(1, len(EH)), 128 if EH else 1])
            ezf = [sb(pf, f"ezf{i}", [128, 512 if EH else 1]) for i in range(2)]
            DH = [h for h in range(8) if ADDENG[h] == "D"]
            E1d = sb(pf, "E1d", [128, max(1, len(DH)), 128 if DH else 1])
            E2d = sb(pf, "E2d", [128, max(1, len(DH)), 128 if DH else 1])
            Rn = sb(pf, "Rn", [128, max(1, len(FH)), 128 if FH else 1])
            import os
            NT = int(os.environ.get("NTILE", 16))
            zi = 0
            ei = [0]
            for j in range(NT):
                c0 = 16 + 128 * j
                if j == 0:
                    P.dma("sync", "xt2", dict(out=xt2[:, :], in_=x[0:128, :]), writes=["xt2"])
                for hf in range(2):
                    for cc in range(8):
                        T("matmul", dict(out=pacc[hf][:, :], lhsT=yT[:, cc, c0:c0 + 128], rhs=wo16[:, cc, 512 * hf:512 * hf + 512], start=(cc == 0), stop=(cc == 7)), ["wo16"], [f"paccF{hf}"])
                    V("tensor_tensor", dict(out=h2t[:, 512 * hf:512 * hf + 512], in0=pacc[hf][:, :], in1=xt2[:, 512 * hf:512 * hf + 512], op=ALU.add), [f"paccF{hf}", "xt2"], [("h2t", hf)])
                H2 = [("h2t", 0), ("h2t", 1)]
                if j + 1 < NT:
                    P.dma("sync", "xt2", dict(out=xt2[:, :], in_=x[128 * (j + 1):128 * (j + 2), :]), ["xt2"], ["xt2"])
                if debug is not None and "h2" in debug:
                    if j == 0:
                        dbg_h2 = nc.dram_tensor("dbg_h2", [128, 16, D], F32, kind="ExternalOutput").ap()
                        dbg_out["h2"] = dbg_h2
                    P.dma("sync", "dbgh2", dict(out=dbg_h2[:, j, :], in_=h2t[:, :]), H2, ["dbgh2"])
                V("scalar_tensor_tensor", dict(out=xn16[:, :], in0=h2t[:, :], scalar=1.0, in1=h2t[:, :], op0=ALU.mult, op1=ALU.mult, accum_out=sv[:, 0:1]), H2, ["xn16", ("sv", "ss")])
                rstd_from_ss(sv[:, 0:1], sv[:, 2:3], float(D), 128, [("sv", "ss")], [("sv", "ln"), ("sv", "rstd")], sv[:, 1:2])
                V("scalar_tensor_tensor", dict(out=xn16[:, :], in0=h2t[:, :], scalar=sv[:, 2:3], in1=gffn_b[:, :], op0=ALU.mult, op1=ALU.mult), H2 + [("sv", "rstd"), "gffn_b"], ["xn16"])
                for dc in range(8):
                    T("transpose", dict(out=ptx[:, 128 * dc:128 * dc + 128], in_=xn16[:, 128 * dc:128 * dc + 128], identity=identb[:, :]), ["xn16", "identb"], ["ptx"])
                A("activation", dict(out=xnT[:, :, :], in_=ptx[:, :].rearrange("p (a t) -> p a t", a=8), func=AF.Copy), ["ptx"], ["xnT"])
                rot = [(pS, "pS"), (pa[0], "pa0"), (pa[1], "pa1")]
                for g4 in range(4):
                    pb_, pk_ = rot[g4 % 3]
                    for cq in range(4):
                        cc = 4 * g4 + cq
                        for dc in range(8):
                            T("matmul", dict(out=pb_[:, 128 * cq:128 * cq + 128], lhsT=wq16[:, dc, 128 * cc:128 * cc + 128], rhs=xnT[:, dc, :], start=(dc == 0), stop=(dc == 7)), ["wq16", "xnT"], [pk_])
                    A("activation", dict(out=qT_sb[:, 4 * g4:4 * g4 + 4, :], in_=pb_[:, :].rearrange("p (a t) -> p a t", a=4), func=AF.Copy), [pk_], [("scr", g4)])
                for g4 in range(4):
                    pb_, pk_ = rot[(g4 + 1) % 3]
                    for hq in range(4):
                        hp = 4 * g4 + hq
                        T("matmul", dict(out=pb_[:, 128 * hq:128 * hq + 128], lhsT=qT_sb[:, hp, :], rhs=keysT[:, hp, :], start=True, stop=True), [("scr", g4), "keysT"], [pk_])
                    A("activation", dict(out=S_sb[:, 4 * g4:4 * g4 + 4, :], in_=pb_[:, :].rearrange("p (a t) -> p a t", a=4), func=AF.Copy), [pk_], [("S_sb", g4)])
                SK = [("S_sb", g) for g in range(4)]
                for hp0 in range(0, 16, 4):
                    grp = range(hp0, hp0 + 4)
                    for hp in grp:
                        V("max", dict(out=v1[:, hp, 0:8], in_=S_sb[:, hp, :]), SK, [("v1", hp, 0)])
                    for hp in grp:
                        V("match_replace", dict(out=wk[hp % 4][:, :], in_to_replace=v1[:, hp, 0:8], in_values=S_sb[:, hp, :], imm_value=NEG), SK + [("v1", hp, 0)], [f"wk{hp % 4}"])
                    for hp in grp:
                        V("max", dict(out=v1[:, hp, 8:16], in_=wk[hp % 4][:, :]), [f"wk{hp % 4}"], [("v1", hp, 1)])
                for h0 in range(0, 8, 2):
                    grp = range(h0, h0 + 2)
                    for h in grp:
                        V("tensor_tensor", dict(out=cand[h % 2][:, :].rearrange("p (a b) -> p a b", a=16), in0=v1[:, 2 * h, :].unsqueeze(2).broadcast_to([128, 16, 16]), in1=v1[:, 2 * h + 1, :].unsqueeze(1).broadcast_to([128, 16, 16]), op=ALU.add),
                          [("v1", 2 * h, 0), ("v1", 2 * h, 1), ("v1", 2 * h + 1, 0), ("v1", 2 * h + 1, 1)], [f"cand{h % 2}"])
                    for h in grp:
                        V("max", dict(out=ct[:, h, 0:8], in_=cand[h % 2][:, :]), [f"cand{h % 2}"], [("ct", h, 0)])
                    for h in grp:
                        V("match_replace", dict(out=wk2[h % 2][:, :], in_to_replace=ct[:, h, 0:8], in_values=cand[h % 2][:, :], imm_value=NEG), [f"cand{h % 2}", ("ct", h, 0)], [f"wk2{h % 2}"])
                    for h in grp:
                        V("max", dict(out=ct[:, h, 8:16], in_=wk2[h % 2][:, :]), [f"wk2{h % 2}"], [("ct", h, 1)])
                CTK = [("ct", h, k_) for h in range(8) for k_ in range(2)]
                tau = ct[:, :, 15]
                mx_ = ct[:, :, 0]
                V("tensor_tensor", dict(out=dd[:, :, :], in0=ct[:, :, :], in1=ct[:, :, 0:1].broadcast_to([128, 8, 16]), op=ALU.subtract), CTK, ["dd"])
                A("activation", dict(out=dd[:, :, :], in_=dd[:, :, :], func=AF.Exp), ["dd"], ["dd"])
                V("tensor_reduce", dict(out=sv[:, 8:16], in_=dd[:, :, :], axis=AX.X, op=ALU.add), ["dd"], [("sv", "Z")])
                A("activation", dict(out=sv[:, 16:24], in_=sv[:, 8:16], func=AF.Ln), [("sv", "Z")], [("sv", "lnZ")])
                V("tensor_tensor", dict(out=sv[:, 24:32], in0=sv[:, 16:24], in1=mx_, op=ALU.add), [("sv", "lnZ")] + CTK, [("sv", "cst")])
                V("tensor_tensor", dict(out=sv[:, 32:40], in0=tau, in1=sv[:, 24:32], op=ALU.subtract), [("sv", "cst")] + CTK, [("sv", "bias")])
                S4 = S_sb[:, :, :].rearrange("p (h two) n -> p h two n", two=2)
                V("tensor_tensor", dict(out=S1t[:, :, :], in0=S4[:, :, 0, :], in1=ct[:, :, 15:16].broadcast_to([128, 8, 128]), op=ALU.subtract), SK + CTK, ["S1t"])
                V("tensor_scalar", dict(out=S1t[:, :, :], in0=S1t[:, :, :], scalar1=8.0e-6, scalar2=None, op0=ALU.add), ["S1t"], ["S1t"])
                for k_, h in enumerate(FH):
                    V("tensor_scalar", dict(out=S2e[:, k_, :], in0=S_sb[:, 2 * h + 1, :], scalar1=sv[:, 32 + h:33 + h], scalar2=None, op0=ALU.add), SK + [("sv", "bias")], ["S2e"])
                    V("tensor_scalar", dict(out=Rn[:, k_, :], in0=S1t[:, h, :], scalar1=-1.0, scalar2=None, op0=ALU.mult), ["S1t"], ["Rn"])
                if EH:
                    A("activation", dict(out=sv[:, 40:48], in_=sv[:, 32:40], func=AF.Exp), [("sv", "bias")], [("sv", "thr")])
                for k_, h in enumerate(EH):
                    V("tensor_scalar", dict(out=SBe[:, k_, :], in0=S1t[:, h, :], scalar1=sv[:, 32 + h:33 + h], scalar2=None, op0=ALU.add), ["S1t", ("sv", "bias")], ["SBe"])
                for k_, h in enumerate(DH):
                    A("activation", dict(out=E1d[:, k_, :], in_=S1t[:, h, :], func=AF.Exp), ["S1t"], ["E1d"])
                    A("activation", dict(out=E2d[:, k_, :], in_=S_sb[:, 2 * h + 1, :], func=AF.Exp, bias=sv[:, 32 + h:33 + h], scale=1.0), SK + [("sv", "bias")], ["E2d"])
                NIB = 32
                NDUM = int(os.environ.get("NDUM", 0))

                def loadUT(ibp):
                    b_ = ibp % 2
                    P.dma("sync", f"UT{b_}", dict(out=UT[b_][:], in_=ut_s[:, :, 512 * ibp:512 * ibp + 512].rearrange("dc p e -> p dc e")), writes=[f"UT{b_}"])

                def loadVt(ib):
                    b_ = ib % 2
                    P.dma("sync", f"Vt{b_}", dict(out=Vt[b_][:], in_=vb_s[512 * ib:512 * ib + 512, :].rearrange("(q p) d -> p q d", p=128)), ["vb_s"], [f"Vt{b_}"])

                def computeA(ibp):
                    b_ = ibp % 2
                    for q in range(4):
                        for dc in range(8):
                            T("matmul", dict(out=pa[b_][:, 128 * q:128 * q + 128], lhsT=UT[b_][:, dc, 128 * q:128 * q + 128], rhs=xnT[:, dc, :], start=(dc == 0), stop=(dc == 7)), [f"UT{b_}", "xnT"], [f"pa{b_}"])

                def evacA(ibp):
                    b_ = ibp % 2
                    A("activation", dict(out=apre[:, ibp % 4, :], in_=pa[b_][:, :], func=AF.Copy), [f"pa{b_}"], [("scr", ibp % 4)])

                def gelu_burst(ibs):
                    for ibp in ibs:
                        A("activation", dict(out=gA[:, ibp % 4, :], in_=apre[:, ibp % 4, :], func=AF.Gelu), [("scr", ibp % 4)], [("gA", ibp % 4)])

                def stageG(ib):
                    nonlocal zi
                    p_ = ib % 2
                    for h in range(8):
                        zb = zi % NZ
                        zi += 1
                        ae = ADDENG[h]
                        if ae == "E":
                            k_ = EH.index(h)
                            eb = ei[0] % 2
                            ei[0] += 1
                            for q in range(4):
                                A("activation", dict(out=ezf[eb][:, 128 * q:128 * q + 128], in_=S_sb[:, 2 * h + 1, :], func=AF.Exp, bias=SBe[:, k_, 4 * ib + q:4 * ib + q + 1], scale=1.0), SK + ["SBe"], [f"ezf{eb}"])
                            V("scalar_tensor_tensor", dict(out=tmps[p_][h][:, :], in0=ezf[eb][:, :], scalar=sv[:, 40 + h:41 + h], in1=ezf[eb][:, :], op0=ALU.is_ge, op1=ALU.mult), [f"ezf{eb}", ("sv", "thr")], [f"tmp{p_}_{h}"])
                            continue
                        if ae == "F":
                            k_ = FH.index(h)
                            for q in range(4):
                                A("activation", dict(out=ez[zb][:, 128 * q:128 * q + 128], in_=S2e[:, k_, :], func=AF.Exp, bias=S1t[:, h, 4 * ib + q:4 * ib + q + 1], scale=1.0), ["S2e", "S1t"], [f"ez{zb}"])
                            for q in range(4):
                                V("scalar_tensor_tensor", dict(out=tmps[p_][h][:, 128 * q:128 * q + 128], in0=S_sb[:, 2 * h + 1, :], scalar=Rn[:, k_, 4 * ib + q:4 * ib + q + 1], in1=ez[zb][:, 128 * q:128 * q + 128], op0=ALU.is_ge, op1=ALU.mult), SK + ["Rn", f"ez{zb}"], [f"tmp{p_}_{h}"])
                            continue
                        if ae == "A":
                            for q in range(4):
                                A("activation", dict(out=zt[zb][:, q, :], in_=S_sb[:, 2 * h + 1, :], func=AF.Identity, bias=S1t[:, h, 4 * ib + q:4 * ib + q + 1], scale=1.0), SK + ["S1t"], [f"zt{zb}"])
                        else:
                            addeng = G if ae in ("P", "D") else V
                            addeng("tensor_tensor", dict(out=zt[zb][:, :, :], in0=S_sb[:, 2 * h + 1, :].unsqueeze(1).broadcast_to([128, 4, 128]), in1=S1t[:, h, 4 * ib:4 * ib + 4].unsqueeze(2).broadcast_to([128, 4, 128]), op=ALU.add), SK + ["S1t"], [f"zt{zb}"])
                        if ae == "D":
                            k_ = DH.index(h)
                            V("tensor_tensor", dict(out=ez[zb][:, :].rearrange("p (a n) -> p a n", a=4), in0=E2d[:, k_, :].unsqueeze(1).broadcast_to([128, 4, 128]), in1=E1d[:, k_, 4 * ib:4 * ib + 4].unsqueeze(2).broadcast_to([128, 4, 128]), op=ALU.mult), ["E1d", "E2d"], [f"ez{zb}"])
                        else:
                            A("activation", dict(out=ez[zb][:, :], in_=zt[zb][:, :, :].rearrange("p a n -> p (a n)"), func=AF.Exp, bias=sv[:, 32 + h:33 + h], scale=1.0), [f"zt{zb}", ("sv", "bias")], [f"ez{zb}"])
                        V("scalar_tensor_tensor", dict(out=tmps[p_][h][:, :], in0=zt[zb][:, :, :].rearrange("p a n -> p (a n)"), scalar=0.0, in1=ez[zb][:, :], op0=ALU.is_ge, op1=ALU.mult), [f"zt{zb}", f"ez{zb}"], [f"tmp{p_}_{h}"])

                def stageT(ib):
                    p_ = ib % 2
                    for _d in range(NDUM):
                        T("matmul", dict(out=pS[:, :], lhsT=identb[:, :], rhs=gA[:, _d % 4, :], start=True, stop=True), [], ["pSdummy"])
                    for q in range(4):
                        for h in range(8):
                            T("matmul", dict(out=pgt[p_][:, 128 * q:128 * q + 128], lhsT=tmps[p_][h][:, 128 * q:128 * q + 128], rhs=identb[:, :], start=(h == 0), stop=(h == 7)), [f"tmp{p_}_{h}", "identb"], [f"pgt{p_}"])

                def stageW(ib):
                    p_ = ib % 2
                    V("tensor_tensor", dict(out=wT[p_][:, :, :].rearrange("p a t -> p (a t)"), in0=pgt[p_][:, :], in1=gA[:, ib % 4, :], op=ALU.mult), [f"pgt{p_}", ("gA", ib % 4)], [f"wT{p_}"])
                    for q in range(4):
                        for hf in range(2):
                            T("matmul", dict(out=pacc[hf][:, :], lhsT=wT[p_][:, q, :], rhs=Vt[p_][:, q, 512 * hf:512 * hf + 512], start=(ib == 0 and q == 0), stop=(ib == NIB - 1 and q == 3)), [f"wT{p_}", f"Vt{p_}"], [f"paccF{hf}"])

                loadUT(0)
                loadUT(1)
                computeA(0)
                evacA(0)
                loadUT(2)
                computeA(1)
                evacA(1)
                for k in range(NIB + 2):
                    if k + 3 < NIB:
                        loadUT(k + 3)
                    if 0 <= k - 1 < NIB:
                        loadVt(k - 1)
                    if k % 4 == 2:
                        gelu_burst(range(k - 2, min(k + 2, NIB)))
                    if 0 <= k - 1 < NIB:
                        stageT(k - 1)
                    if 0 <= k - 2 < NIB:
                        stageW(k - 2)
                    if k + 2 < NIB:
                        computeA(k + 2)
                    if k < NIB:
                        stageG(k)
                    if k + 2 < NIB:
                        evacA(k + 2)
                SCR = [("scr", s_) for s_ in range(4)]
                for hf in range(2):
                    V("tensor_tensor", dict(out=scr16[:, 512 * hf:512 * hf + 512], in0=pacc[hf][:, :], in1=h2t[:, 512 * hf:512 * hf + 512], op=ALU.add), [f"paccF{hf}"] + H2, SCR)
                P.dma("sync", "ost", dict(out=out[128 * j:128 * j + 128, :], in_=scr16[:, 0:1024]), SCR, SCR + [("out", j)])
            P.barrier()
            P.emit()
        P.barrier()
        P.emit()
    return nc, dbg_out

_CACHE = {}


def kernel(**inputs):
    n = 8
    if "nc" not in _CACHE:
        _CACHE["nc"] = build_nc()[0]
    nc = _CACHE["nc"]
    f = lambda a: np.ascontiguousarray(np.asarray(a, dtype=np.float32))
    x = f(inputs["x"])
    shared = {
        "meta_tokens": f(inputs["meta_tokens"]),
        "norm_mix": f(inputs["norm_mix"]).reshape(1, D),
        "w_in": f(inputs["w_in"]).reshape(D, 3600),
        "conv_qk": f(inputs["conv_qk"]).reshape(4, 1024),
        "b_igate": f(inputs["b_igate"]).reshape(1, 4),
        "b_fgate_m": f(inputs["b_fgate_m"]).reshape(1, 4),
        "m_out_norm": f(inputs["m_out_norm"]).reshape(1, 512),
        "b_fgate_f": f(inputs["b_fgate_f"]).reshape(8, 1),
        "f_q_norm": f(inputs["f_q_norm"]).reshape(1, 64),
        "f_k_norm": f(inputs["f_k_norm"]).reshape(1, 64),
        "w_out": f(inputs["w_out"]).reshape(D, D),
        "norm_ffn": f(inputs["norm_ffn"]).reshape(1, D),
        "peer_query": f(inputs["peer_query"]).reshape(D, 2048),
        "peer_sub_keys": f(inputs["peer_sub_keys"]).reshape(2048, 128),
        "peer_u": f(inputs["peer_u"]).reshape(16384, D),
        "peer_v": f(inputs["peer_v"]).reshape(16384, D),
    }
    in_maps = [dict(shared, x=x[b]) for b in range(n)]
    res = run_bass_kernel_spmd(nc, in_maps, core_ids=list(range(n)))
    return np.stack([np.asarray(res.results[b]["out"], dtype=np.float32) for b in range(n)], axis=0)
```

```python
import numpy as np
import concourse.bass as bass
import concourse.mybir as mybir
from concourse.bass_utils import run_bass_kernel_spmd
from contextlib import ExitStack

F32 = mybir.dt.float32
BF16 = mybir.dt.bfloat16
AF = mybir.ActivationFunctionType
ALU = mybir.AluOpType
AX = mybir.AxisListType

ENGS = ("tensor", "vector", "scalar", "gpsimd", "sync")
import os
RELAX = os.environ.get("RELAX", "0") == "1"
VCLOCK = os.environ.get("VCLOCK", "1") == "1"

D = 1024
SEQ = 2048
NMETA = 16
L = SEQ + NMETA
LM = 2112
LP = 2176
NCH = 33
EPS = 1e-6
NEG = -1.0e30


class Prog:
    def __init__(self, nc, es):
        self.nc = nc
        self.es = es
        self.ops = {e: [] for e in ENGS}
        self.cnt = {e: 0 for e in ENGS}
        self.sem = {e: es.enter_context(nc.semaphore("s_" + e)) for e in ENGS}
        self.waited = {e: {} for e in ENGS}
        self.snap = {}
        self.dclock = {}
        self.last_w = {}
        self.readers = {}
        self.dsems = {}
        self.semobj = {("e", e): self.sem[e] for e in ENGS}
        self.nops = 0
        self.nwaits = 0

    @staticmethod
    def _merge(dst, src):
        for k, v in src.items():
            if dst.get(k, 0) < v:
                dst[k] = v

    def _deps(self, eng, reads, writes):
        deps = {}

        def add(d):
            if d is None:
                return
            k, v = d
            if deps.get(k, 0) < v:
                deps[k] = v
        me = ("e", eng)
        for k in reads:
            add(self.last_w.get(k))
        for k in writes:
            lw = self.last_w.get(k)
            if lw is not None and not (RELAX and lw[0] == me):
                add(lw)
            for sk, v in self.readers.get(k, {}).items():
                if RELAX and sk == me:
                    continue
                add((sk, v))
        out = []
        w = self.waited[eng]
        for sk, v in sorted(deps.items(), key=lambda kv: -kv[1]):
            if eng == "tensor" and sk == ("e", "tensor"):
                continue
            if sk[0] == "d":
                v = self.dsems[sk[1]][1]
            if w.get(sk, 0) < v:
                w[sk] = v
                out.append((sk, v))
                if VCLOCK:
                    if sk[0] == "d":
                        self._merge(w, self.dclock.get(sk[1], {}))
                    elif sk != me:
                        self._merge_snap(w, sk, v)
        self.nwaits += len(out)
        return out

    def _merge_snap(self, w, sk, v):
        s = self.snap.get((sk, v))
        if s is not None:
            own = w.get(("e", self._cur), 0)
            self._merge(w, s)
            if s.get(("e", self._cur), 0) > own:
                w[("e", self._cur)] = own

    def op(self, eng, meth, kw, reads=(), writes=()):
        fn = (meth, kw)
        self._cur = eng
        waits = self._deps(eng, reads, writes)
        self.cnt[eng] += 1
        n = self.cnt[eng]
        me = ("e", eng)
        self.ops[eng].append((waits, fn, (me, 1)))
        self.nops += 1
        if VCLOCK:
            s = dict(self.waited[eng])
            s[me] = n
            self.snap[(me, n)] = s
        for k in reads:
            self.readers.setdefault(k, {})[me] = n
        for k in writes:
            self.last_w[k] = (me, n)
            self.readers[k] = {}

    def dma(self, eng, slot, kw, reads=(), writes=()):
        fn = ("dma_start", kw)
        if slot not in self.dsems:
            s = self.es.enter_context(self.nc.semaphore("d_" + slot))
            self.dsems[slot] = [s, 0]
            self.semobj[("d", slot)] = s
        self._cur = eng
        waits = self._deps(eng, reads, writes)
        self.dsems[slot][1] += 16
        v = self.dsems[slot][1]
        me = ("d", slot)
        self.ops[eng].append((waits, fn, (me, 16)))
        self.nops += 1
        if VCLOCK:
            dc = self.dclock.setdefault(slot, {})
            own = dict(self.waited[eng])
            own.pop(("e", eng), None)
            self._merge(dc, own)
        for k in reads:
            self.readers.setdefault(k, {})[me] = v
        for k in writes:
            self.last_w[k] = (me, v)
            self.readers[k] = {}

    def barrier(self, exclude=()):
        cur = [(("e", e), self.cnt[e]) for e in ENGS if self.cnt[e] > 0]
        cur += [(("d", s), v[1]) for s, v in self.dsems.items() if v[1] > 0 and s not in exclude]
        for e in ENGS:
            w = self.waited[e]
            waits = []
            for sk, v in cur:
                if sk == ("e", e):
                    if e == "tensor":
                        continue
                if w.get(sk, 0) < v:
                    w[sk] = v
                    waits.append((sk, v))
            if waits:
                self.ops[e].append((waits, None, None))
        self.last_w = {k: v for k, v in self.last_w.items() if v[0][0] == "d" and v[0][1] in exclude}
        self.readers = {}
        self.snap = {}

    def emit(self):
        nc = self.nc
        with nc.Block() as block:
            for e in ENGS:
                ops = self.ops[e]
                if not ops:
                    continue
                semobj = self.semobj

                def body(eng, ops=ops):
                    for waits, fn, inc in ops:
                        for sk, v in waits:
                            eng.wait_ge(semobj[sk], v)
                        if fn is not None:
                            ins = getattr(eng, fn[0])(**fn[1])
                            ins.then_inc(semobj[inc[0]], inc[1])
                getattr(block, e)(body)
        self.ops = {e: [] for e in ENGS}


def build_nc(debug=None, stop=None):
    nc = bass.Bass("TRN2", target_bir_lowering=False)

    def din(name, shape):
        return nc.dram_tensor(name, list(shape), F32, kind="ExternalInput").ap()

    x = din("x", [SEQ, D])
    meta = din("meta_tokens", [NMETA, D])
    norm_mix = din("norm_mix", [1, D])
    w_in = din("w_in", [D, 3600])
    conv_qk = din("conv_qk", [4, 1024])
    b_igate = din("b_igate", [1, 4])
    b_fgate_m = din("b_fgate_m", [1, 4])
    m_out_norm = din("m_out_norm", [1, 512])
    b_fgate_f = din("b_fgate_f", [8, 1])
    f_q_norm = din("f_q_norm", [1, 64])
    f_k_norm = din("f_k_norm", [1, 64])
    w_out = din("w_out", [D, D])
    norm_ffn = din("norm_ffn", [1, D])
    peer_query = din("peer_query", [D, 2048])
    peer_keys = din("peer_sub_keys", [2048, 128])
    peer_u = din("peer_u", [16384, D])
    peer_v = din("peer_v", [16384, D])
    out = nc.dram_tensor("out", [SEQ, D], F32, kind="ExternalOutput").ap()
    ut_s = nc.dram_tensor("ut_s", [8, 128, 16384], BF16).ap()
    vb_s = nc.dram_tensor("vb_s", [16384, D], BF16).ap()

    dbg_out = {}
    VEX = ("vprep",)

    es = ExitStack()
    with es:
        P = Prog(nc, es)

        def sb(stack, name, shape, dt=F32):
            return stack.enter_context(nc.sbuf_tensor(name, list(shape), dt))

        def ps(stack, name, dt=F32):
            shape = [128, 512] if dt == F32 else [128, 1024]
            return stack.enter_context(nc.psum_tensor(name, shape, dt))

        def dump(name, ap, key, shape):
            if debug is None or name not in debug:
                return
            t = nc.dram_tensor("dbg_" + name, list(shape), ap.dtype, kind="ExternalOutput").ap()
            dbg_out[name] = t
            P.dma("sync", "dbg", dict(out=t, in_=ap), key if isinstance(key, list) else [key], ["dbg_" + name])

        V = lambda m, kw, r=(), w=(): P.op("vector", m, kw, r, w)
        A = lambda m, kw, r=(), w=(): P.op("scalar", m, kw, r, w)
        G = lambda m, kw, r=(), w=(): P.op("gpsimd", m, kw, r, w)
        T = lambda m, kw, r=(), w=(): P.op("tensor", m, kw, r, w)

        identf = sb(es, "identf", [128, 128])
        identb = sb(es, "identb", [128, 128], BF16)
        epsc = sb(es, "epsc", [128, 1])
        onec = sb(es, "onec", [128, 1])
        lnsc = sb(es, "lnsc", [128, 1])
        gffn_b = sb(es, "gffn_b", [128, D])

        G("iota", dict(out=identf[:], pattern=[[1, 128]], base=0, channel_multiplier=-1,
                           allow_small_or_imprecise_dtypes=True), w=["identf"])
        V("tensor_scalar", dict(out=identb[:], in0=identf[:], scalar1=0.0, scalar2=None, op0=ALU.is_equal), ["identf"], ["identb"])
        iotaf = identf
        G("memset", dict(ap=epsc[:], constant=EPS), w=["epsc"])
        G("memset", dict(ap=onec[:], constant=1.0), w=["onec"])
        G("memset", dict(ap=lnsc[:], constant=float(-0.5 * np.log(128.0))), w=["lnsc"])
        P.dma("sync", "c0", dict(out=gffn_b[:], in_=norm_ffn[0:1, :].partition_broadcast(128)), writes=["gffn_b"])

        def rstd_from_ss(ss_ap, dst_ap, n, r, keys_r, keys_w, tmp_ap):
            A("activation", dict(out=tmp_ap, in_=ss_ap, func=AF.Ln, bias=epsc[0:r, :], scale=1.0 / n), keys_r + ["epsc"], keys_w[:1])
            A("activation", dict(out=dst_ap, in_=tmp_ap, func=AF.Exp, scale=-0.5), keys_w[:1], keys_w[1:])


        import os as _os

        def gen_prep(ub, uts, ptu):
            k = 0
            for eg in range(32):
                b = eg % 2
                P.dma("gpsimd", f"ub{b}", dict(out=ub[b][:], in_=peer_u[512 * eg:512 * eg + 512, :].rearrange("(q p) d -> p q d", p=128)), writes=[f"ub{b}"])
                yield
                for q in range(4):
                    tb = ptu[k % 4]
                    kt = f"ptu{k % 4}"
                    for dc in range(8):
                        T("transpose", dict(out=tb[:, 128 * dc:128 * dc + 128], in_=ub[b][:, q, 128 * dc:128 * dc + 128], identity=identb[:, :]), [f"ub{b}", "identb"], [kt])
                        yield
                    if k % 2 == 0:
                        A("activation", dict(out=uts[b][:, :, 128 * q:128 * q + 128], in_=tb[:, :].rearrange("p (a e) -> p a e", a=8), func=AF.Copy), [kt], [f"uts{b}"])
                    else:
                        V("tensor_copy", dict(out=uts[b][:, :, 128 * q:128 * q + 128], in_=tb[:, :].rearrange("p (a e) -> p a e", a=8)), [kt], [f"uts{b}"])
                    yield
                    k += 1
                P.dma("sync", f"uts{b}", dict(out=ut_s[:, :, 512 * eg:512 * eg + 512].rearrange("dc p e -> p dc e"), in_=uts[b][:]), [f"uts{b}"], ["ut_s"])
                yield

        prep_gen = [None]

        def bg_step(n):
            g_ = prep_gen[0]
            if g_ is None:
                return
            for _ in range(n):
                try:
                    next(g_)
                except StopIteration:
                    prep_gen[0] = None
                    return

        yT = sb(es, "yT", [128, 8, LP], BF16)
        with ExitStack() as ms:
            causf = sb(ms, "causf", [128, 128])
            causb = sb(ms, "causb", [128, 128], BF16)
            negm = sb(ms, "negm", [128, 128])
            onesf = sb(ms, "onesf", [128, 128])
            gmix_b = sb(ms, "gmix_b", [128, D])
            V("tensor_scalar", dict(out=causf[:], in0=iotaf[:], scalar1=0.0, scalar2=None, op0=ALU.is_ge), ["identf"], ["causf"])
            V("tensor_copy", dict(out=causb[:], in_=causf[:]), ["causf"], ["causb"])
            V("tensor_scalar", dict(out=negm[:], in0=iotaf[:], scalar1=0.0, scalar2=NEG, op0=ALU.is_gt, op1=ALU.mult), ["identf"], ["negm"])
            V("tensor_scalar", dict(out=identf[:], in0=iotaf[:], scalar1=0.0, scalar2=None, op0=ALU.is_equal), ["identf", "identb", "causf", "negm"], ["identf"])
            G("memset", dict(ap=onesf[:], constant=1.0), w=["onesf"])
            P.dma("sync", "c0b", dict(out=gmix_b[:], in_=norm_mix[0:1, :].partition_broadcast(128)), writes=["gmix_b"])
            hnT = sb(ms, "hnT", [128, 8, LP], BF16)
            big1 = sb(ms, "big1", [128, 8, LP], BF16)
            wgate = sb(ms, "wgate", [128, 8, 16], BF16)
            P.dma("gpsimd", "c1", dict(out=wgate[:, :, 0:8], in_=w_in[:, 2048:2056].rearrange("(dc p) c -> p dc c", p=128)), writes=["wgate"])
            P.dma("gpsimd", "c1", dict(out=wgate[:, :, 8:16], in_=w_in[:, 3592:3600].rearrange("(dc p) c -> p dc c", p=128)), writes=["wgate"])

            def tile_cols(j):
                return (0, 16) if j == 0 else (16 + 128 * (j - 1), 128)

            pps = ExitStack()
            pps.__enter__()
            if not _os.environ.get("NOPREP"):
                ub_ = [sb(pps, f"ub{i}", [128, 4, D], BF16) for i in range(2)]
                uts_ = [sb(pps, f"uts{i}", [128, 8, 512], BF16) for i in range(2)]
                ptu_ = [ps(pps, f"ptu{i}", BF16) for i in range(4)]
                prep_gen[0] = gen_prep(ub_, uts_, ptu_)
            with ExitStack() as pa:
                xt = [sb(pa, f"xt{i}", [128, D]) for i in range(2)]
                junk = sb(pa, "junkA", [128, D], BF16)
                hn = [sb(pa, f"hn{i}", [128, D], BF16) for i in range(2)]
                ssA = [sb(pa, f"ssA{i}", [128, 3]) for i in range(2)]
                tpA = [ps(pa, f"tpA{i}", BF16) for i in range(2)]
                for j in range(17):
                    c0, r = tile_cols(j)
                    b = j % 2
                    xtb, hnb, ssb, tpb = xt[b], hn[b], ssA[b], tpA[b]
                    kx, kh, ks, kt = f"xt{b}", f"hn{b}", f"ssA{b}", f"tpA{b}"
                    if j == 0:
                        P.dma("sync", kx, dict(out=xtb[0:16, :], in_=meta[:, :]), writes=[kx])
                    else:
                        P.dma("sync", kx, dict(out=xtb[:, :], in_=x[128 * (j - 1):128 * j, :]), writes=[kx])
                    V("scalar_tensor_tensor", dict(out=junk[0:r, :], in0=xtb[0:r, :], scalar=1.0, in1=xtb[0:r, :], op0=ALU.mult, op1=ALU.mult, accum_out=ssb[0:r, 0:1]), [kx], ["junkA", ks + "a"])
                    rstd_from_ss(ssb[0:r, 0:1], ssb[0:r, 2:3], float(D), r, [ks + "a"], [ks + "b", ks + "c"], ssb[0:r, 1:2])
                    V("scalar_tensor_tensor", dict(out=hnb[0:r, :], in0=xtb[0:r, :], scalar=ssb[0:r, 2:3], in1=gmix_b[0:r, :], op0=ALU.mult, op1=ALU.mult), [kx, ks + "c", "gmix_b"], [kh])
                    for dc in range(8):
                        T("transpose", dict(out=tpb[:, dc * 128:dc * 128 + r], in_=hnb[0:r, dc * 128:(dc + 1) * 128], identity=identb[0:r, 0:r]), [kh, "identb"], [kt])
                    tpv = tpb[:, :].rearrange("p (a b) -> p a b", a=8)
                    A("activation", dict(out=hnT[:, :, c0:c0 + r], in_=tpv[:, :, 0:r], func=AF.Copy), [kt], [("hnT", j)])
                    bg_step(int(_os.environ.get("BGA", 4)))
                G("memset", dict(ap=hnT[:, :, L:LP], constant=0.0), w=[("hnT", 17)])
                P.barrier(exclude=VEX)
                P.emit()
            HN_ALL = [("hnT", j) for j in range(18)]
            dump("hnT", hnT[:, :, 0:L], HN_ALL, [128, 8, L])

            if stop == "A":
                P.barrier(exclude=VEX)
                P.emit()
                return nc, dbg_out

            CB = [(0, 512), (512, 512), (1024, 512), (1536, 512), (2048, 64)]
            with ExitStack() as pb:
                wg = [sb(pb, f"wg{i}", [128, 8, 512], BF16) for i in range(2)]
                cw = sb(pb, "cw", [128, 4, 8])
                zc = [sb(pb, f"zc{i}", [128, LM]) for i in range(2)]
                ac = [sb(pb, f"ac{i}", [128, LM]) for i in range(2)]
                pz = [ps(pb, f"pz{i}") for i in range(2)]
                for j in range(4):
                    P.dma("sync", "cw", dict(out=cw[:, j, :], in_=conv_qk[j:j + 1, :].rearrange("o (k c) -> c (o k)", c=128), allow_slow_non_contiguous=True), writes=["cw"])
                it = 0
                for g in range(2):
                    wgb = wg[g]
                    P.dma("gpsimd", f"wg{g}", dict(out=wgb[:], in_=w_in[:, 512 * g:512 * (g + 1)].rearrange("(dc p) c -> p dc c", p=128)), writes=[f"wg{g}"])
                    for ck in range(4):
                        idx = g * 4 + ck
                        b = idx % 2
                        zcb, acb = zc[b], ac[b]
                        for (c0, wdt) in CB:
                            pzb = pz[it % 2]
                            kp = f"pz{it % 2}"
                            it += 1
                            for dc in range(8):
                                T("matmul", dict(out=pzb[:, 0:wdt], lhsT=wgb[:, dc, ck * 128:(ck + 1) * 128], rhs=hnT[:, dc, c0:c0 + wdt], start=(dc == 0), stop=(dc == 7)), [f"wg{g}"] + HN_ALL, [kp])
                            A("activation", dict(out=zcb[:, c0:c0 + wdt], in_=pzb[:, 0:wdt], func=AF.Copy), [kp], [f"zc{b}"])
                            bg_step(int(_os.environ.get("BGB", 30)))
                        V("tensor_scalar", dict(out=acb[:, :], in0=zcb[:, :], scalar1=cw[:, 3, idx:idx + 1], scalar2=None, op0=ALU.mult), [f"zc{b}", "cw"], [f"ac{b}"])
                        for sh in (1, 2, 3):
                            V("scalar_tensor_tensor", dict(out=acb[:, sh:LM], in0=zcb[:, 0:LM - sh], scalar=cw[:, 3 - sh, idx:idx + 1], in1=acb[:, sh:LM], op0=ALU.mult, op1=ALU.add), [f"zc{b}", "cw", f"ac{b}"], [f"ac{b}"])
                        A("activation", dict(out=big1[:, idx, 0:LM], in_=acb[:, :], func=AF.Silu), [f"ac{b}"], [("qk", idx)])
                bg_step(100000)
                P.barrier(exclude=VEX)
                P.emit()
            pps.close()
            QK_ALL = [("qk", i) for i in range(8)]
            dump("qk", big1[:, :, 0:L], QK_ALL, [128, 8, L])
            if stop == "B":
                P.barrier(exclude=VEX)
                P.emit()
                return nc, dbg_out

            with ExitStack() as pc:
                wv = sb(pc, "wv", [128, 8, 512], BF16)
                wo = sb(pc, "wo", [128, 8, 512], BF16)
                P.dma("gpsimd", "wv", dict(out=wv[:], in_=w_in[:, 1024:1536].rearrange("(dc p) c -> p dc c", p=128)), writes=["wv"])
                P.dma("gpsimd", "wo", dict(out=wo[:], in_=w_in[:, 1536:2048].rearrange("(dc p) c -> p dc c", p=128)), writes=["wo"])
                bm_b = sb(pc, "bm_b", [128, 8])
                P.dma("sync", "c2", dict(out=bm_b[:, 0:4], in_=b_igate[0:1, :].partition_broadcast(128)), writes=["bm_b"])
                P.dma("sync", "c2", dict(out=bm_b[:, 4:8], in_=b_fgate_m[0:1, :].partition_broadcast(128)), writes=["bm_b"])
                mg_b = sb(pc, "mg_b", [128, 512])
                P.dma("sync", "c2", dict(out=mg_b[:], in_=m_out_norm[0:1, :].partition_broadcast(128)), writes=["mg_b"])
                CT = sb(pc, "CT", [128, 4, 129])
                CTb = sb(pc, "CTb", [128, 4, 129], BF16)
                mst = sb(pc, "mst", [128, 4])
                G("memset", dict(ap=CT[:], constant=0.0), w=[("CT", h) for h in range(4)])
                G("memset", dict(ap=CTb[:], constant=0.0), w=[("CTb", h) for h in range(4)])
                G("memset", dict(ap=mst[:], constant=0.0), w=["mst"])
                NB = 2
                vx = [sb(pc, f"vx{i}", [64, 4, 129], BF16) for i in range(NB)]
                for i in range(NB):
                    G("memset", dict(ap=vx[i][:], constant=1.0), w=[f"vx{i}"])
                og = [sb(pc, f"og{i}", [64, 512]) for i in range(NB)]
                sm = [sb(pc, f"sm{i}", [128, 64]) for i in range(NB)]
                dg = [sb(pc, f"dg{i}", [64, 4, 64]) for i in range(NB)]
                mk = [sb(pc, f"mk{i}", [64, 4, 64]) for i in range(NB)]
                spT = [sb(pc, f"spT{i}", [64, 64], BF16) for i in range(4)]
                tmpB = [sb(pc, f"tmpB{i}", [64, 129]) for i in range(2)]
                num = [sb(pc, f"num{i}", [64, 129]) for i in range(2)]
                junkc = sb(pc, "junkc", [64, 128], BF16)
                hs = [sb(pc, f"hs{i}", [64, 8]) for i in range(4)]
                ytm = [sb(pc, f"ytm{i}", [64, 4, 128], BF16) for i in range(NB)]
                kw = [sb(pc, f"kw{i}", [64, 128], BF16) for i in range(2)]
                pg = ps(pc, "pg")
                pv = ps(pc, "pv")
                po = ps(pc, "po")
                pqk = ps(pc, "pqk")
                pAB = [ps(pc, f"pAB{i}") for i in range(2)]
                pU = ps(pc, "pU")
                ptr = ps(pc, "ptr", BF16)

                SM = dict(gt=0, e1=8, lfn=12, a=16, ea=20, cm=24, amax=28, M=32, Mend=36, eM=40, w=44, wg=48, dec=52, emt=56, t0=60)
                import os
                CLIM = int(os.environ.get("CLIM", NCH))
                LVL = int(os.environ.get("LVL", 99))
                def chunk_ctx(c):
                    Tn = 64
                    c0 = 64 * c
                    b = c % NB
                    s_ = sm[b]
                    ksm = lambda nm, b=b: (f"sm{b}", nm)
                    col = lambda nm, w=4, s_=s_, r=None: s_[0:(Tn if r is None else r), SM[nm]:SM[nm] + w]
                    hk = HN_ALL
                    return Tn, c0, b, s_, ksm, col, hk

                def gen_pre(c):
                    Tn, c0, b, s_, ksm, col, hk = chunk_ctx(c)
                    if c < 32 and not _os.environ.get("NOPREP"):
                        for r in (2 * c, 2 * c + 1):
                            P.dma("gpsimd", "vprep", dict(out=vb_s[256 * r:256 * r + 256, :], in_=peer_v[256 * r:256 * r + 256, :]), writes=["vb_s"])
                            yield
                    for dc in range(8):
                        T("matmul", dict(out=pg[0:Tn, 0:8], lhsT=hnT[:, dc, c0:c0 + Tn], rhs=wgate[:, dc, 0:8], start=(dc == 0), stop=(dc == 7)), hk + ["wgate"], ["pg"])
                        yield
                    V("tensor_tensor", dict(out=col("gt", 8), in0=pg[0:Tn, 0:8], in1=bm_b[0:Tn, :], op=ALU.add), ["pg", "bm_b"], [ksm("gt")])
                    yield
                    A("activation", dict(out=col("e1"), in_=s_[0:Tn, 4:8], func=AF.Exp, scale=-1.0), [ksm("gt")], [ksm("e1")])
                    yield
                    A("activation", dict(out=col("lfn"), in_=col("e1"), func=AF.Ln, bias=onec[0:Tn, :], scale=1.0), [ksm("e1"), "onec"], [ksm("lfn")])
                    yield
                    T("matmul", dict(out=pg[0:Tn, 8:12], lhsT=causf[0:Tn, 0:Tn], rhs=col("lfn"), start=True, stop=True), [ksm("lfn"), "causf"], ["pg"])
                    yield
                    T("matmul", dict(out=pg[:, 16:20], lhsT=onesf[0:Tn, :], rhs=col("lfn"), start=True, stop=True), [ksm("lfn"), "onesf"], ["pg"])
                    yield
                    V("tensor_tensor", dict(out=col("a"), in0=s_[0:Tn, 0:4], in1=pg[0:Tn, 8:12], op=ALU.add), [ksm("gt"), "pg"], [ksm("a")])
                    yield
                    dgb, mkb = dg[b], mk[b]
                    V("tensor_tensor", dict(out=dgb[0:Tn, :, 0:Tn], in0=identf[0:Tn, 0:Tn].unsqueeze(1).broadcast_to([Tn, 4, Tn]), in1=col("a").unsqueeze(2).broadcast_to([Tn, 4, Tn]), op=ALU.mult), [ksm("a"), "identf"], [f"dg{b}"])
                    yield
                    abc = pg[:, 256:512].rearrange("p (h s) -> p h s", h=4)
                    for h in range(4):
                        T("matmul", dict(out=pg[:, 256 + 64 * h:256 + 64 * h + Tn], lhsT=onesf[0:Tn, :], rhs=dgb[0:Tn, h, 0:Tn], start=True, stop=True), [f"dg{b}", "onesf"], ["pg"])
                        yield
                    V("tensor_tensor", dict(out=mkb[0:Tn, :, 0:Tn], in0=abc[0:Tn, :, 0:Tn], in1=negm[0:Tn, 0:Tn].unsqueeze(1).broadcast_to([Tn, 4, Tn]), op=ALU.add), ["pg", "negm"], [f"mk{b}"])
                    yield
                    V("tensor_reduce", dict(out=col("cm"), in_=mkb[0:Tn, :, 0:Tn], axis=AX.X, op=ALU.max), [f"mk{b}"], [ksm("cm")])
                    yield
                    V("tensor_reduce", dict(out=s_[:, SM["amax"]:SM["amax"] + 4], in_=abc[:, :, 0:Tn], axis=AX.X, op=ALU.max), ["pg"], [ksm("amax")])
                    yield
                    A("activation", dict(out=col("ea"), in_=col("a"), func=AF.Exp, bias=lnsc[0:Tn, :], scale=1.0), [ksm("a"), "lnsc"], [ksm("ea")])
                    yield
                    V("tensor_tensor", dict(out=col("M"), in0=col("cm"), in1=mst[0:Tn, :], op=ALU.max), [ksm("cm"), "mst"], [ksm("M")])
                    yield
                    V("tensor_tensor", dict(out=s_[:, SM["Mend"]:SM["Mend"] + 4], in0=s_[:, SM["amax"]:SM["amax"] + 4], in1=mst[:, :], op=ALU.max), [ksm("amax"), "mst"], [ksm("Mend")])
                    yield
                    A("activation", dict(out=col("eM"), in_=col("M"), func=AF.Exp, scale=-1.0), [ksm("M")], [ksm("eM")])
                    yield
                    V("tensor_tensor", dict(out=col("w"), in0=mst[0:Tn, :], in1=col("M"), op=ALU.subtract), [ksm("M"), "mst"], [ksm("w")])
                    yield
                    A("activation", dict(out=col("w"), in_=col("w"), func=AF.Exp), [ksm("w")], [ksm("w")])
                    yield
                    V("tensor_tensor", dict(out=col("wg"), in0=col("a"), in1=col("Mend"), op=ALU.subtract), [ksm("a"), ksm("Mend")], [ksm("wg")])
                    yield
                    A("activation", dict(out=col("wg"), in_=col("wg"), func=AF.Exp, bias=lnsc[0:Tn, :], scale=1.0), [ksm("wg"), "lnsc"], [ksm("wg")])
                    yield
                    V("tensor_tensor", dict(out=s_[:, SM["dec"]:SM["dec"] + 4], in0=mst[:, :], in1=s_[:, SM["Mend"]:SM["Mend"] + 4], op=ALU.subtract), [ksm("Mend"), "mst"], [ksm("dec")])
                    yield
                    A("activation", dict(out=s_[:, SM["dec"]:SM["dec"] + 4], in_=s_[:, SM["dec"]:SM["dec"] + 4], func=AF.Exp), [ksm("dec")], [ksm("dec")])
                    yield
                    V("tensor_tensor", dict(out=col("emt"), in0=pg[0:Tn, 8:12], in1=col("M"), op=ALU.subtract), [ksm("M"), "pg"], [ksm("emt")])
                    yield
                    A("activation", dict(out=col("emt"), in_=col("emt"), func=AF.Exp), [ksm("emt")], [ksm("emt")])
                    yield
                    V("tensor_tensor", dict(out=mst[:, :], in0=s_[:, SM["Mend"]:SM["Mend"] + 4], in1=pg[:, 16:20], op=ALU.subtract), [ksm("Mend"), "pg", "mst"], ["mst"])
                    yield
                    vxb, ogb = vx[b], og[b]
                    for dc in range(8):
                        T("matmul", dict(out=pv[0:Tn, :], lhsT=hnT[:, dc, c0:c0 + Tn], rhs=wv[:, dc, :], start=(dc == 0), stop=(dc == 7)), hk + ["wv"], ["pv"])
                        yield
                    A("activation", dict(out=vxb[0:Tn, :, 0:128], in_=pv[0:Tn, :].rearrange("p (h d) -> p h d", h=4), func=AF.Copy), ["pv"], [f"vx{b}"])
                    yield
                    for dc in range(8):
                        T("matmul", dict(out=po[0:Tn, :], lhsT=hnT[:, dc, c0:c0 + Tn], rhs=wo[:, dc, :], start=(dc == 0), stop=(dc == 7)), hk + ["wo"], ["po"])
                        yield
                    A("activation", dict(out=ogb[0:Tn, :], in_=po[0:Tn, :], func=AF.Exp, scale=-1.0), ["po"], [f"og{b}"])
                    yield
                    V("tensor_scalar", dict(out=ogb[0:Tn, :], in0=ogb[0:Tn, :], scalar1=1.0, scalar2=None, op0=ALU.add), [f"og{b}"], [f"og{b}"])
                    yield
                    V("reciprocal", dict(out=ogb[0:Tn, :], in_=ogb[0:Tn, :]), [f"og{b}"], [f"og{b}"])
                    yield
                    V("tensor_tensor", dict(out=ogb[0:Tn, :], in0=ogb[0:Tn, :], in1=mg_b[0:Tn, :], op=ALU.mult), [f"og{b}", "mg_b"], [f"og{b}"])
                    yield
                    ytb = ytm[b]

                def gen_head(c, h):
                    Tn, c0, b, s_, ksm, col, hk = chunk_ctx(c)
                    vxb, ogb, ytb = vx[b], og[b], ytm[b]
                    hb = h % 2
                    qTc = big1[:, h, c0:c0 + Tn]
                    kTc = big1[:, 4 + h, c0:c0 + Tn]
                    sp = spT[h]
                    pab = pAB[hb]
                    hsb = hs[h]
                    khs = lambda nm, h=h: (f"hs{h}", nm)
                    T("matmul", dict(out=pqk[0:Tn, 64 * h:64 * h + Tn], lhsT=kTc, rhs=qTc, start=True, stop=True), [("qk", h), ("qk", 4 + h)], ["pqk"])
                    yield
                    T("matmul", dict(out=pab[0:Tn, 256:385], lhsT=qTc, rhs=CTb[:, h, :], start=True, stop=True), [("qk", h), ("CTb", h)], [f"pAB{hb}"])
                    yield
                    V("scalar_tensor_tensor", dict(out=sp[0:Tn, 0:Tn], in0=pqk[0:Tn, 64 * h:64 * h + Tn], scalar=col("ea")[:, h:h + 1], in1=causf[0:Tn, 0:Tn], op0=ALU.mult, op1=ALU.mult), ["pqk", ksm("ea"), "causf"], [f"spT{h}"])
                    yield
                    T("matmul", dict(out=pab[0:Tn, 0:129], lhsT=sp[0:Tn, 0:Tn], rhs=vxb[0:Tn, h, :], start=True, stop=True), [f"spT{h}", f"vx{b}"], [f"pAB{hb}"])
                    yield
                    tB, nm = tmpB[hb], num[hb]
                    A("activation", dict(out=tB[0:Tn, :], in_=pab[0:Tn, 256:385], func=AF.Copy, scale=col("w")[:, h:h + 1]), [f"pAB{hb}", ksm("w")], [f"tmpB{hb}"])
                    yield
                    V("scalar_tensor_tensor", dict(out=nm[0:Tn, :], in0=pab[0:Tn, 0:129], scalar=col("eM")[:, h:h + 1], in1=tB[0:Tn, :], op0=ALU.mult, op1=ALU.add), [f"pAB{hb}", f"tmpB{hb}", ksm("eM")], [f"num{hb}"])
                    yield
                    A("activation", dict(out=hsb[0:Tn, 7:8], in_=nm[0:Tn, 128:129], func=AF.Abs), [f"num{hb}"], [khs("abs")])
                    yield
                    V("tensor_scalar", dict(out=hsb[0:Tn, 0:1], in0=hsb[0:Tn, 7:8], scalar1=col("emt")[:, h:h + 1], scalar2=None, op0=ALU.max), [khs("abs"), ksm("emt")], [khs("den")])
                    yield
                    V("reciprocal", dict(out=hsb[0:Tn, 1:2], in_=hsb[0:Tn, 0:1]), [khs("den")], [khs("rden")])
                    yield
                    V("scalar_tensor_tensor", dict(out=junkc[0:Tn, :], in0=nm[0:Tn, 0:128], scalar=1.0, in1=nm[0:Tn, 0:128], op0=ALU.mult, op1=ALU.mult, accum_out=hsb[0:Tn, 2:3]), [f"num{hb}"], ["junkc", khs("ss")])
                    yield
                    V("tensor_scalar", dict(out=hsb[0:Tn, 3:4], in0=hsb[0:Tn, 2:3], scalar1=hsb[0:Tn, 1:2], scalar2=hsb[0:Tn, 1:2], op0=ALU.mult, op1=ALU.mult), [khs("ss"), khs("rden")], [khs("t1")])
                    yield
                    rstd_from_ss(hsb[0:Tn, 3:4], hsb[0:Tn, 5:6], 128.0, Tn, [khs("t1")], [khs("ln"), khs("rstd")], hsb[0:Tn, 4:5])
                    yield
                    V("tensor_tensor", dict(out=hsb[0:Tn, 6:7], in0=hsb[0:Tn, 5:6], in1=hsb[0:Tn, 1:2], op=ALU.mult), [khs("rstd"), khs("rden")], [khs("sc")])
                    yield
                    V("scalar_tensor_tensor", dict(out=ytb[0:Tn, h, :], in0=nm[0:Tn, 0:128], scalar=hsb[0:Tn, 6:7], in1=ogb[0:Tn, 128 * h:128 * (h + 1)], op0=ALU.mult, op1=ALU.mult), [f"num{hb}", khs("sc"), f"og{b}"], [(f"ytm{b}", h)])
                    yield
                    T("transpose", dict(out=ptr[:, 64 * h:64 * h + Tn], in_=ytb[0:Tn, h, :], identity=identb[0:Tn, 0:Tn]), [(f"ytm{b}", h), "identb"], ["ptr"])
                    yield
                    A("activation", dict(out=yT[:, h, c0:c0 + Tn], in_=ptr[:, 64 * h:64 * h + Tn], func=AF.Copy), ["ptr"], [("yT", h, c)])
                    yield

                def gen_state(c, h):
                    Tn, c0, b, s_, ksm, col, hk = chunk_ctx(c)
                    vxb = vx[b]
                    hb = h % 2
                    kTc = big1[:, 4 + h, c0:c0 + Tn]
                    yield
                    yield
                    kwb = kw[hb]
                    T("transpose", dict(out=ptr[0:Tn, 512 + 128 * hb:512 + 128 * hb + 128], in_=kTc, identity=identb[:, :]), [("qk", 4 + h), "identb"], ["ptr"])
                    yield
                    A("activation", dict(out=kwb[0:Tn, :], in_=ptr[0:Tn, 512 + 128 * hb:512 + 128 * hb + 128], func=AF.Copy, scale=col("wg")[:, h:h + 1]), ["ptr", ksm("wg")], [f"kw{hb}"])
                    yield
                    T("matmul", dict(out=pU[:, 256 * hb:256 * hb + 129], lhsT=kwb[0:Tn, :], rhs=vxb[0:Tn, h, :], start=True, stop=True), [f"kw{hb}", f"vx{b}"], ["pU"])
                    yield
                    V("scalar_tensor_tensor", dict(out=CT[:, h, :], in0=CT[:, h, :], scalar=s_[:, SM["dec"] + h:SM["dec"] + h + 1], in1=pU[:, 256 * hb:256 * hb + 129], op0=ALU.mult, op1=ALU.add), [("CT", h), ksm("dec"), "pU"], [("CT", h)])
                    yield
                    A("activation", dict(out=CTb[:, h, :], in_=CT[:, h, :], func=AF.Copy), [("CT", h)], [("CTb", h)])
                    yield


                def run_rr(gens, bg=()):
                    gens = list(gens)
                    bg = list(bg)
                    while gens:
                        alive = []
                        for g_ in gens:
                            try:
                                next(g_)
                                alive.append(g_)
                            except StopIteration:
                                pass
                        gens = alive
                        for g_ in list(bg):
                            try:
                                next(g_)
                            except StopIteration:
                                bg.remove(g_)
                    return bg

                NCC = min(NCH, CLIM)
                run_rr([gen_pre(0)])
                for c in range(NCC):
                    bg = [gen_pre(c + 1)] if c + 1 < NCC else []
                    bg = run_rr([gen_head(c, 0), gen_head(c, 1), gen_state(c, 0), gen_state(c, 1)], bg)
                    bg = run_rr([gen_head(c, 2), gen_head(c, 3), gen_state(c, 2), gen_state(c, 3)], bg)
                    run_rr(bg)
                P.barrier(exclude=VEX)
                P.emit()
            dump("yTm", yT[:, 0:4, 0:L], [], [128, 4, L])
            if stop == "C":
                P.barrier(exclude=VEX)
                P.emit()
                return nc, dbg_out

            NBLK = 17
            kTe = sb(ms, "kTe", [70, 8, LP], BF16)
            fvx = sb(ms, "fvx", [128, NBLK, 8, 65], BF16)
            prow = sb(ms, "prow", [48, LP], BF16)
            qTe = big1
            with ExitStack() as pd:
                gq_b = sb(pd, "gq_b", [128, 64])
                gk_b = sb(pd, "gk_b", [128, 64])
                nbff = sb(pd, "nbff", [8, 1])
                P.dma("sync", "c3", dict(out=gq_b[:], in_=f_q_norm[0:1, :].partition_broadcast(128)), writes=["gq_b"])
                P.dma("sync", "c3", dict(out=gk_b[:], in_=f_k_norm[0:1, :].partition_broadcast(128)), writes=["gk_b"])
                P.dma("sync", "c3", dict(out=nbff[:], in_=b_fgate_f[:, :]), writes=["nbff"])
                bffb = sb(pd, "bffb", [128, 8])
                P.dma("sync", "c3", dict(out=bffb[:], in_=b_fgate_f.rearrange("h o -> o h").partition_broadcast(128)), writes=["bffb"])
                V("tensor_scalar", dict(out=gq_b[:], in0=gq_b[:], scalar1=0.125, scalar2=None, op0=ALU.mult), ["gq_b"], ["gq_b"])
                V("tensor_scalar", dict(out=nbff[:], in0=nbff[:], scalar1=-1.0, scalar2=None, op0=ALU.mult), ["nbff"], ["nbff"])
                G("memset", dict(ap=fvx[:], constant=1.0), w=["fvx"])
                G("memset", dict(ap=qTe[64:70, :, :], constant=1.0), w=["qTe_ext"])
                G("memset", dict(ap=kTe[64:70, :, :], constant=1.0), w=["kTe_ext"])
                wf = [sb(pd, f"wf{i}", [128, 8, 512], BF16) for i in range(2)]
                qs = [sb(pd, f"qs{i}", [128, 512]) for i in range(2)]
                sq = sb(pd, "sqD", [128, 512])
                qtm = [sb(pd, f"qtm{i}", [128, 512], BF16) for i in range(2)]
                ssd = [sb(pd, f"ssd{i}", [128, 24]) for i in range(2)]
                pq = [ps(pd, f"pq{i}") for i in range(2)]
                ptq = [ps(pd, f"ptq{i}", BF16) for i in range(2)]
                pgf = ps(pd, "pgf")
                lfd = sb(pd, "lfd", [128, NBLK, 8])
                cnd = sb(pd, "cnd", [128, NBLK, 8])
                tot = sb(pd, "tot", [128, NBLK + 1, 8])
                for i in range(NBLK):
                    c0 = 128 * i
                    for dc in range(8):
                        T("matmul", dict(out=pgf[:, 8 * i:8 * i + 8], lhsT=hnT[:, dc, c0:c0 + 128], rhs=wgate[:, dc, 8:16], start=(dc == 0), stop=(dc == 7)), HN_ALL + ["wgate"], ["pgf"])
                pgv = pgf[:, 0:8 * NBLK].rearrange("p (b h) -> p b h", h=8)
                V("tensor_tensor", dict(out=lfd[:], in0=pgv, in1=bffb[:, :].unsqueeze(1).broadcast_to([128, NBLK, 8]), op=ALU.add), ["pgf", "bffb"], ["lfd"])
                A("activation", dict(out=lfd[:], in_=lfd[:], func=AF.Exp, scale=-1.0), ["lfd"], ["lfd"])
                A("activation", dict(out=lfd[:], in_=lfd[:], func=AF.Ln, bias=onec[:, :], scale=1.0), ["lfd", "onec"], ["lfd"])
                lf2 = lfd[:, :, :].rearrange("p b h -> p (b h)")
                T("matmul", dict(out=pgf[:, 256:256 + 8 * NBLK], lhsT=causf[:, :], rhs=lf2, start=True, stop=True), ["lfd", "causf"], ["pgf"])
                V("tensor_copy", dict(out=cnd[:, :, :].rearrange("p b h -> p (b h)"), in_=pgf[:, 256:256 + 8 * NBLK]), ["pgf"], ["cnd"])
                T("matmul", dict(out=pgf[:, 256:256 + 8 * NBLK], lhsT=onesf[:, :], rhs=lf2, start=True, stop=True), ["lfd", "onesf", "cnd"], ["pgf"])
                G("memset", dict(ap=tot[:, 0, :], constant=0.0), w=["tot"])
                for i in range(NBLK):
                    V("tensor_tensor", dict(out=tot[:, i + 1, :], in0=tot[:, i, :], in1=pgf[:, 256 + 8 * i:256 + 8 * i + 8], op=ALU.add), ["tot", "pgf"], ["tot"])
                V("tensor_tensor", dict(out=cnd[:], in0=cnd[:], in1=tot[:, 0:NBLK, :], op=ALU.add), ["cnd", "tot"], ["cnd"])
                parts = sb(pd, "parts", [128, NBLK, 48], BF16)
                r1 = sb(pd, "r1d", [128, NBLK, 8])
                pv3 = lambda a, b_: parts[:, :, a:b_]
                V("tensor_copy", dict(out=pv3(0, 8), in_=cnd[:]), ["cnd"], ["parts0"])
                V("tensor_tensor", dict(out=r1[:], in0=cnd[:], in1=pv3(0, 8), op=ALU.subtract), ["cnd", "parts0"], ["r1d"])
                V("tensor_copy", dict(out=pv3(8, 16), in_=r1[:]), ["r1d"], ["parts1"])
                V("tensor_tensor", dict(out=r1[:], in0=r1[:], in1=pv3(8, 16), op=ALU.subtract), ["r1d", "parts1"], ["r1d"])
                V("tensor_copy", dict(out=pv3(16, 24), in_=r1[:]), ["r1d"], ["parts2"])
                V("tensor_scalar", dict(out=pv3(24, 48), in0=pv3(0, 24), scalar1=-1.0, scalar2=None, op0=ALU.mult), ["parts0", "parts1", "parts2"], ["parts3"])
                PK = ["parts0", "parts1", "parts2", "parts3"]
                for i in range(NBLK):
                    tpb = ptq[i % 2]
                    T("transpose", dict(out=tpb[0:48, 0:128], in_=parts[:, i, :], identity=identb[:, :]), PK + ["identb"], [f"ptq{i % 2}"])
                    A("activation", dict(out=prow[:, 128 * i:128 * i + 128], in_=tpb[0:48, 0:128], func=AF.Copy), [f"ptq{i % 2}"], ["prow"])
                for h in range(8):
                    for j in range(3):
                        P.dma("sync", "ext", dict(out=kTe[67 + j:68 + j, h, :], in_=prow[8 * j + h:8 * j + h + 1, :]), ["prow", "kTe_ext"], [("kTe_ext", h, j)])
                        P.dma("sync", "ext", dict(out=qTe[64 + j:65 + j, h, :], in_=prow[24 + 8 * j + h:24 + 8 * j + h + 1, :]), ["prow", "qTe_ext"], [("qTe_ext", h, j)])
                wsrc = [(2056, "q"), (2568, "k"), (3080, "v")]
                sq2 = [sq, sb(pd, "sqD2", [128, 512])]

                def gen_blk(wfb, kw_, kind, i, b):
                    c0 = 128 * i
                    pqb, qsb, qtb, ssb, tpb, sqb = pq[b], qs[b], qtm[b], ssd[b], ptq[b], sq2[b]
                    for dc in range(8):
                        T("matmul", dict(out=pqb[:, :], lhsT=hnT[:, dc, c0:c0 + 128], rhs=wfb[:, dc, :], start=(dc == 0), stop=(dc == 7)), HN_ALL + [kw_], [f"pq{b}"])
                        yield
                    if kind == "v":
                        A("activation", dict(out=fvx[:, i, :, 0:64], in_=pqb[:, :].rearrange("p (h d) -> p h d", h=8), func=AF.Copy), [f"pq{b}", "fvx"], [("fvx", i)])
                        yield
                        return
                    gb_ = gq_b if kind == "q" else gk_b
                    A("activation", dict(out=qsb[:, :], in_=pqb[:, :], func=AF.Copy), [f"pq{b}"], [f"qs{b}"])
                    yield
                    V("tensor_tensor", dict(out=sqb[:, :], in0=qsb[:, :], in1=qsb[:, :], op=ALU.mult), [f"qs{b}"], [f"sqD{b}"])
                    yield
                    V("tensor_reduce", dict(out=ssb[:, 0:8], in_=sqb[:, :].rearrange("p (h d) -> p h d", h=8), axis=AX.X, op=ALU.add), [f"sqD{b}"], [f"ssd{b}a"])
                    yield
                    rstd_from_ss(ssb[:, 0:8], ssb[:, 16:24], 64.0, 128, [f"ssd{b}a"], [f"ssd{b}b", f"ssd{b}c"], ssb[:, 8:16])
                    yield
                    q3 = qsb[:, :].rearrange("p (h d) -> p h d", h=8)
                    V("tensor_tensor", dict(out=q3, in0=q3, in1=ssb[:, 16:24].unsqueeze(2).broadcast_to([128, 8, 64]), op=ALU.mult), [f"qs{b}", f"ssd{b}c"], [f"qs{b}"])
                    yield
                    V("tensor_tensor", dict(out=qtb[:, :].rearrange("p (h d) -> p h d", h=8), in0=q3, in1=gb_[:, :].unsqueeze(1).broadcast_to([128, 8, 64]), op=ALU.mult), [f"qs{b}", "gq_b", "gk_b"], [f"qtm{b}"])
                    yield
                    for h in range(8):
                        T("transpose", dict(out=tpb[0:64, 128 * h:128 * h + 128], in_=qtb[:, 64 * h:64 * h + 64], identity=identb[:, :]), [f"qtm{b}", "identb"], [f"ptq{b}"])
                        yield
                    dst = qTe if kind == "q" else kTe
                    A("activation", dict(out=dst[0:64, :, c0:c0 + 128], in_=tpb[0:64, :].rearrange("p (h t) -> p h t", h=8), func=AF.Copy), [f"ptq{b}"], [(kind + "T", i)])
                    yield

                def run_rr2(gens):
                    gens = list(gens)
                    while gens:
                        alive = []
                        for g_ in gens:
                            try:
                                next(g_)
                                alive.append(g_)
                            except StopIteration:
                                pass
                        gens = alive

                for wi, (wc0, kind) in enumerate(wsrc):
                    wfb = wf[wi % 2]
                    kw_ = f"wf{wi % 2}"
                    P.dma("gpsimd", kw_, dict(out=wfb[:], in_=w_in[:, wc0:wc0 + 512].rearrange("(dc p) c -> p dc c", p=128)), writes=[kw_])
                    for i in range(0, NBLK, 2):
                        gl = [gen_blk(wfb, kw_, kind, i, 0)]
                        if i + 1 < NBLK:
                            gl.append(gen_blk(wfb, kw_, kind, i + 1, 1))
                        run_rr2(gl)
                P.barrier(exclude=VEX)
                P.emit()
            dump("qTe", qTe[0:70, :, 0:L], [], [70, 8, L])
            dump("kTe", kTe[0:70, :, 0:L], [], [70, 8, L])
            if stop == "D1":
                P.barrier(exclude=VEX)
                P.emit()
                return nc, dbg_out
            with ExitStack() as pd2:
                pT = [sb(pd2, f"pT{i}", [128, 512], BF16) for i in range(3)]
                ytf = [sb(pd2, f"ytf{i}", [128, 4, 512], BF16) for i in range(2)]
                rinv = [sb(pd2, f"rinv{i}", [128, 4]) for i in range(2)]
                lg = [ps(pd2, f"lg{i}") for i in range(2)]
                pacc = [ps(pd2, f"pacc{i}") for i in range(4)]
                ptr2 = [ps(pd2, f"ptr2{i}", BF16) for i in range(2)]
                it = 0
                for TS in range(5):
                    jl = [j for j in range(4 * TS, min(4 * TS + 4, NBLK))]
                    nj = len(jl)
                    t0 = 512 * TS
                    WT = 128 * nj
                    ytb = ytf[TS % 2]
                    for h in range(8):
                        rb = rinv[h % 2]
                        nI = jl[-1] + 1

                        def qk(i, itv):
                            b2 = itv % 2
                            ts_ = max(t0, 128 * i)
                            Wd = t0 + WT - ts_
                            T("matmul", dict(out=lg[b2][:, 0:Wd], lhsT=kTe[0:70, h, 128 * i:128 * i + 128], rhs=qTe[0:70, h, ts_:ts_ + Wd], start=True, stop=True), [], [f"lg{b2}"])

                        qk(0, it)
                        for i in range(nI):
                            b3 = it % 3
                            b2 = it % 2
                            ts_ = max(t0, 128 * i)
                            Wd = t0 + WT - ts_
                            A("activation", dict(out=pT[b3][:, 0:Wd], in_=lg[b2][:, 0:Wd], func=AF.Exp), [f"lg{b2}"], [f"pT{b3}"])
                            if 128 * i >= t0:
                                G("tensor_tensor", dict(out=pT[b3][:, 0:128], in0=pT[b3][:, 0:128], in1=causb[:, :], op=ALU.mult), [f"pT{b3}", "causb"], [f"pT{b3}"])
                            if i + 1 < nI:
                                qk(i + 1, it + 1)
                            for j in jl:
                                if j < i:
                                    continue
                                jj = j - 4 * TS
                                co = 128 * j - ts_
                                T("matmul", dict(out=pacc[jj][:, 0:65], lhsT=pT[b3][:, co:co + 128], rhs=fvx[:, i, h, :], start=(i == 0), stop=(i == j)), [f"pT{b3}"], [f"pacc{jj}"])
                            it += 1
                        for jj in range(nj):
                            V("reciprocal", dict(out=rb[:, jj:jj + 1], in_=pacc[jj][:, 64:65]), [f"pacc{jj}"], [(f"rinv{h % 2}", jj)])
                            V("tensor_scalar", dict(out=ytb[:, jj, 64 * h:64 * h + 64], in0=pacc[jj][:, 0:64], scalar1=rb[:, jj:jj + 1], scalar2=None, op0=ALU.mult), [f"pacc{jj}", (f"rinv{h % 2}", jj)], [(f"ytf{TS % 2}", h)])
                    for jj, j in enumerate(jl):
                        tb = ptr2[jj % 2]
                        for pr in range(4):
                            T("transpose", dict(out=tb[:, 128 * pr:128 * pr + 128], in_=ytb[:, jj, 128 * pr:128 * pr + 128], identity=identb[:, :]), [(f"ytf{TS % 2}", hh) for hh in range(8)] + ["identb"], [f"ptr2{jj % 2}"])
                        A("activation", dict(out=yT[:, 4:8, 128 * j:128 * j + 128], in_=tb[:, 0:512].rearrange("p (a t) -> p a t", a=4), func=AF.Copy), [f"ptr2{jj % 2}"], [("yTf", j)])
                P.barrier(exclude=VEX)
                P.emit()
            dump("yT", yT[:, :, 0:L], [], [128, 8, L])
            if stop == "D":
                P.barrier(exclude=VEX)
                P.emit()
                return nc, dbg_out


        with ExitStack() as pf:
            wo16 = sb(pf, "wo16", [128, 8, D], BF16)
            wq16 = sb(pf, "wq16", [128, 8, 2048], BF16)
            keysT = sb(pf, "keysT", [128, 16, 128])
            P.dma("gpsimd", "wo16", dict(out=wo16[:], in_=w_out.rearrange("(cc p) d -> p cc d", p=128)), writes=["wo16"])
            for hf in range(2):
                P.dma("gpsimd", "wq16", dict(out=wq16[:, :, 1024 * hf:1024 * hf + 1024], in_=peer_query[:, 1024 * hf:1024 * hf + 1024].rearrange("(dc p) c -> p dc c", p=128)), writes=["wq16"])
            pacc = [ps(pf, f"paccF{i}") for i in range(2)]
            ptx = ps(pf, "ptx", BF16)
            pS = ps(pf, "pS")
            pa = [ps(pf, f"pa{i}") for i in range(2)]
            pgt = [ps(pf, f"pgt{i}") for i in range(2)]
            with ExitStack() as pk:
                knat = sb(pk, "knat", [128, 16, 128])
                P.dma("sync", "knat", dict(out=knat[:], in_=peer_keys.rearrange("(hp n) c -> n hp c", n=128)), writes=["knat"])
                for hp in range(16):
                    T("transpose", dict(out=pS[:, 128 * (hp % 4):128 * (hp % 4) + 128], in_=knat[:, hp, :], identity=identf[:, :]), ["knat", "identf"], ["pS"])
                    if hp % 4 == 3:
                        A("activation", dict(out=keysT[:, hp - 3:hp + 1, :], in_=pS[:, :].rearrange("p (a n) -> p a n", a=4), func=AF.Copy), ["pS"], ["keysT"])
                P.barrier(exclude=VEX)
                P.emit()
            xt2 = sb(pf, "xt2", [128, D])
            h2t = sb(pf, "h2t", [128, D])
            xn16 = sb(pf, "xn16", [128, D], BF16)
            xnT = sb(pf, "xnT", [128, 8, 128], BF16)
            scr16 = sb(pf, "scr16", [128, 2048])
            qT_sb = scr16[:, 0:2048].rearrange("p (a t) -> p a t", a=16)
            apre = scr16[:, :].rearrange("p (s e) -> p s e", s=4)
            S_sb = sb(pf, "S_sb", [128, 16, 128])
            v1 = sb(pf, "v1", [128, 16, 16])
            wk = [sb(pf, f"wk{i}", [128, 128]) for i in range(4)]
            cand = [sb(pf, f"cand{i}", [128, 256]) for i in range(2)]
            wk2 = [sb(pf, f"wk2{i}", [128, 256]) for i in range(2)]
            ct = sb(pf, "ct", [128, 8, 16])
            sv = sb(pf, "sv", [128, 64])
            dd = sb(pf, "dd", [128, 8, 16])
            S1t = sb(pf, "S1t", [128, 8, 128])
            UT = [sb(pf, f"UT{i}", [128, 8, 512], BF16) for i in range(2)]
            Vt = [sb(pf, f"Vt{i}", [128, 4, D], BF16) for i in range(2)]
            gA = sb(pf, "gA", [128, 4, 512], BF16)
            tmps = [[sb(pf, f"tmp{p_}_{i}", [128, 512], BF16) for i in range(8)] for p_ in range(2)]
            NZ = 6
            zt = [sb(pf, f"zt{i}", [128, 4, 128]) for i in range(NZ)]
            ez = [sb(pf, f"ez{i}", [128, 512], BF16) for i in range(NZ)]
            wT = [sb(pf, f"wT{i}", [128, 4, 128], BF16) for i in range(2)]
            ADDENG = os.environ.get("ADDENG", "PPPAPPPA")
            FH = [h for h in range(8) if ADDENG[h] == "F"]
            S2e = sb(pf, "S2e", [128, max(1, len(FH)), 128 if FH else 1])
            EH = [h for h in range(8) if ADDENG[h] == "E"]
            SBe = sb(pf, "SBe", [128, max(1, len(EH)), 128 if EH else 1])
            ezf = [sb(pf, f"ezf{i}", [128, 512 if EH else 1]) for i in range(2)]
            DH = [h for h in range(8) if ADDENG[h] == "D"]
            E1d = sb(pf, "E1d", [128, max(1, len(DH)), 128 if DH else 1])
            E2d = sb(pf, "E2d", [128, max(1, len(DH)), 128 if DH else 1])
            Rn = sb(pf, "Rn", [128, max(1, len(FH)), 128 if FH else 1])
            import os
            NT = int(os.environ.get("NTILE", 16))
            zi = 0
            ei = [0]
            for j in range(NT):
                c0 = 16 + 128 * j
                if j == 0:
                    P.dma("sync", "xt2", dict(out=xt2[:, :], in_=x[0:128, :]), writes=["xt2"])
                for hf in range(2):
                    for cc in range(8):
                        T("matmul", dict(out=pacc[hf][:, :], lhsT=yT[:, cc, c0:c0 + 128], rhs=wo16[:, cc, 512 * hf:512 * hf + 512], start=(cc == 0), stop=(cc == 7)), ["wo16"], [f"paccF{hf}"])
                    V("tensor_tensor", dict(out=h2t[:, 512 * hf:512 * hf + 512], in0=pacc[hf][:, :], in1=xt2[:, 512 * hf:512 * hf + 512], op=ALU.add), [f"paccF{hf}", "xt2"], [("h2t", hf)])
                H2 = [("h2t", 0), ("h2t", 1)]
                if j + 1 < NT:
                    P.dma("sync", "xt2", dict(out=xt2[:, :], in_=x[128 * (j + 1):128 * (j + 2), :]), ["xt2"], ["xt2"])
                if debug is not None and "h2" in debug:
                    if j == 0:
                        dbg_h2 = nc.dram_tensor("dbg_h2", [128, 16, D], F32, kind="ExternalOutput").ap()
                        dbg_out["h2"] = dbg_h2
                    P.dma("sync", "dbgh2", dict(out=dbg_h2[:, j, :], in_=h2t[:, :]), H2, ["dbgh2"])
                V("scalar_tensor_tensor", dict(out=xn16[:, :], in0=h2t[:, :], scalar=1.0, in1=h2t[:, :], op0=ALU.mult, op1=ALU.mult, accum_out=sv[:, 0:1]), H2, ["xn16", ("sv", "ss")])
                rstd_from_ss(sv[:, 0:1], sv[:, 2:3], float(D), 128, [("sv", "ss")], [("sv", "ln"), ("sv", "rstd")], sv[:, 1:2])
                V("scalar_tensor_tensor", dict(out=xn16[:, :], in0=h2t[:, :], scalar=sv[:, 2:3], in1=gffn_b[:, :], op0=ALU.mult, op1=ALU.mult), H2 + [("sv", "rstd"), "gffn_b"], ["xn16"])
                for dc in range(8):
                    T("transpose", dict(out=ptx[:, 128 * dc:128 * dc + 128], in_=xn16[:, 128 * dc:128 * dc + 128], identity=identb[:, :]), ["xn16", "identb"], ["ptx"])
                A("activation", dict(out=xnT[:, :, :], in_=ptx[:, :].rearrange("p (a t) -> p a t", a=8), func=AF.Copy), ["ptx"], ["xnT"])
                rot = [(pS, "pS"), (pa[0], "pa0"), (pa[1], "pa1")]
                for g4 in range(4):
                    pb_, pk_ = rot[g4 % 3]
                    for cq in range(4):
                        cc = 4 * g4 + cq
                        for dc in range(8):
                            T("matmul", dict(out=pb_[:, 128 * cq:128 * cq + 128], lhsT=wq16[:, dc, 128 * cc:128 * cc + 128], rhs=xnT[:, dc, :], start=(dc == 0), stop=(dc == 7)), ["wq16", "xnT"], [pk_])
                    A("activation", dict(out=qT_sb[:, 4 * g4:4 * g4 + 4, :], in_=pb_[:, :].rearrange("p (a t) -> p a t", a=4), func=AF.Copy), [pk_], [("scr", g4)])
                for g4 in range(4):
                    pb_, pk_ = rot[(g4 + 1) % 3]
                    for hq in range(4):
                        hp = 4 * g4 + hq
                        T("matmul", dict(out=pb_[:, 128 * hq:128 * hq + 128], lhsT=qT_sb[:, hp, :], rhs=keysT[:, hp, :], start=True, stop=True), [("scr", g4), "keysT"], [pk_])
                    A("activation", dict(out=S_sb[:, 4 * g4:4 * g4 + 4, :], in_=pb_[:, :].rearrange("p (a t) -> p a t", a=4), func=AF.Copy), [pk_], [("S_sb", g4)])
                SK = [("S_sb", g) for g in range(4)]
                for hp0 in range(0, 16, 4):
                    grp = range(hp0, hp0 + 4)
                    for hp in grp:
                        V("max", dict(out=v1[:, hp, 0:8], in_=S_sb[:, hp, :]), SK, [("v1", hp, 0)])
                    for hp in grp:
                        V("match_replace", dict(out=wk[hp % 4][:, :], in_to_replace=v1[:, hp, 0:8], in_values=S_sb[:, hp, :], imm_value=NEG), SK + [("v1", hp, 0)], [f"wk{hp % 4}"])
                    for hp in grp:
                        V("max", dict(out=v1[:, hp, 8:16], in_=wk[hp % 4][:, :]), [f"wk{hp % 4}"], [("v1", hp, 1)])
                for h0 in range(0, 8, 2):
                    grp = range(h0, h0 + 2)
                    for h in grp:
                        V("tensor_tensor", dict(out=cand[h % 2][:, :].rearrange("p (a b) -> p a b", a=16), in0=v1[:, 2 * h, :].unsqueeze(2).broadcast_to([128, 16, 16]), in1=v1[:, 2 * h + 1, :].unsqueeze(1).broadcast_to([128, 16, 16]), op=ALU.add),
                          [("v1", 2 * h, 0), ("v1", 2 * h, 1), ("v1", 2 * h + 1, 0), ("v1", 2 * h + 1, 1)], [f"cand{h % 2}"])
                    for h in grp:
                        V("max", dict(out=ct[:, h, 0:8], in_=cand[h % 2][:, :]), [f"cand{h % 2}"], [("ct", h, 0)])
                    for h in grp:
                        V("match_replace", dict(out=wk2[h % 2][:, :], in_to_replace=ct[:, h, 0:8], in_values=cand[h % 2][:, :], imm_value=NEG), [f"cand{h % 2}", ("ct", h, 0)], [f"wk2{h % 2}"])
                    for h in grp:
                        V("max", dict(out=ct[:, h, 8:16], in_=wk2[h % 2][:, :]), [f"wk2{h % 2}"], [("ct", h, 1)])
                CTK = [("ct", h, k_) for h in range(8) for k_ in range(2)]
                tau = ct[:, :, 15]
                mx_ = ct[:, :, 0]
                V("tensor_tensor", dict(out=dd[:, :, :], in0=ct[:, :, :], in1=ct[:, :, 0:1].broadcast_to([128, 8, 16]), op=ALU.subtract), CTK, ["dd"])
                A("activation", dict(out=dd[:, :, :], in_=dd[:, :, :], func=AF.Exp), ["dd"], ["dd"])
                V("tensor_reduce", dict(out=sv[:, 8:16], in_=dd[:, :, :], axis=AX.X, op=ALU.add), ["dd"], [("sv", "Z")])
                A("activation", dict(out=sv[:, 16:24], in_=sv[:, 8:16], func=AF.Ln), [("sv", "Z")], [("sv", "lnZ")])
                V("tensor_tensor", dict(out=sv[:, 24:32], in0=sv[:, 16:24], in1=mx_, op=ALU.add), [("sv", "lnZ")] + CTK, [("sv", "cst")])
                V("tensor_tensor", dict(out=sv[:, 32:40], in0=tau, in1=sv[:, 24:32], op=ALU.subtract), [("sv", "cst")] + CTK, [("sv", "bias")])
                S4 = S_sb[:, :, :].rearrange("p (h two) n -> p h two n", two=2)
                V("tensor_tensor", dict(out=S1t[:, :, :], in0=S4[:, :, 0, :], in1=ct[:, :, 15:16].broadcast_to([128, 8, 128]), op=ALU.subtract), SK + CTK, ["S1t"])
                V("tensor_scalar", dict(out=S1t[:, :, :], in0=S1t[:, :, :], scalar1=8.0e-6, scalar2=None, op0=ALU.add), ["S1t"], ["S1t"])
                for k_, h in enumerate(FH):
                    V("tensor_scalar", dict(out=S2e[:, k_, :], in0=S_sb[:, 2 * h + 1, :], scalar1=sv[:, 32 + h:33 + h], scalar2=None, op0=ALU.add), SK + [("sv", "bias")], ["S2e"])
                    V("tensor_scalar", dict(out=Rn[:, k_, :], in0=S1t[:, h, :], scalar1=-1.0, scalar2=None, op0=ALU.mult), ["S1t"], ["Rn"])
                if EH:
                    A("activation", dict(out=sv[:, 40:48], in_=sv[:, 32:40], func=AF.Exp), [("sv", "bias")], [("sv", "thr")])
                for k_, h in enumerate(EH):
                    V("tensor_scalar", dict(out=SBe[:, k_, :], in0=S1t[:, h, :], scalar1=sv[:, 32 + h:33 + h], scalar2=None, op0=ALU.add), ["S1t", ("sv", "bias")], ["SBe"])
                for k_, h in enumerate(DH):
                    A("activation", dict(out=E1d[:, k_, :], in_=S1t[:, h, :], func=AF.Exp), ["S1t"], ["E1d"])
                    A("activation", dict(out=E2d[:, k_, :], in_=S_sb[:, 2 * h + 1, :], func=AF.Exp, bias=sv[:, 32 + h:33 + h], scale=1.0), SK + [("sv", "bias")], ["E2d"])
                NIB = 32
                NDUM = int(os.environ.get("NDUM", 0))

                def loadUT(ibp):
                    b_ = ibp % 2
                    P.dma("sync", f"UT{b_}", dict(out=UT[b_][:], in_=ut_s[:, :, 512 * ibp:512 * ibp + 512].rearrange("dc p e -> p dc e")), writes=[f"UT{b_}"])

                def loadVt(ib):
                    b_ = ib % 2
                    P.dma("sync", f"Vt{b_}", dict(out=Vt[b_][:], in_=vb_s[512 * ib:512 * ib + 512, :].rearrange("(q p) d -> p q d", p=128)), ["vb_s"], [f"Vt{b_}"])

                def computeA(ibp):
                    b_ = ibp % 2
                    for q in range(4):
                        for dc in range(8):
                            T("matmul", dict(out=pa[b_][:, 128 * q:128 * q + 128], lhsT=UT[b_][:, dc, 128 * q:128 * q + 128], rhs=xnT[:, dc, :], start=(dc == 0), stop=(dc == 7)), [f"UT{b_}", "xnT"], [f"pa{b_}"])

                def evacA(ibp):
                    b_ = ibp % 2
                    A("activation", dict(out=apre[:, ibp % 4, :], in_=pa[b_][:, :], func=AF.Copy), [f"pa{b_}"], [("scr", ibp % 4)])

                def gelu_burst(ibs):
                    for ibp in ibs:
                        A("activation", dict(out=gA[:, ibp % 4, :], in_=apre[:, ibp % 4, :], func=AF.Gelu), [("scr", ibp % 4)], [("gA", ibp % 4)])

                def stageG(ib):
                    nonlocal zi
                    p_ = ib % 2
                    for h in range(8):
                        zb = zi % NZ
                        zi += 1
                        ae = ADDENG[h]
                        if ae == "E":
                            k_ = EH.index(h)
                            eb = ei[0] % 2
                            ei[0] += 1
                            for q in range(4):
                                A("activation", dict(out=ezf[eb][:, 128 * q:128 * q + 128], in_=S_sb[:, 2 * h + 1, :], func=AF.Exp, bias=SBe[:, k_, 4 * ib + q:4 * ib + q + 1], scale=1.0), SK + ["SBe"], [f"ezf{eb}"])
                            V("scalar_tensor_tensor", dict(out=tmps[p_][h][:, :], in0=ezf[eb][:, :], scalar=sv[:, 40 + h:41 + h], in1=ezf[eb][:, :], op0=ALU.is_ge, op1=ALU.mult), [f"ezf{eb}", ("sv", "thr")], [f"tmp{p_}_{h}"])
                            continue
                        if ae == "F":
                            k_ = FH.index(h)
                            for q in range(4):
                                A("activation", dict(out=ez[zb][:, 128 * q:128 * q + 128], in_=S2e[:, k_, :], func=AF.Exp, bias=S1t[:, h, 4 * ib + q:4 * ib + q + 1], scale=1.0), ["S2e", "S1t"], [f"ez{zb}"])
                            for q in range(4):
                                V("scalar_tensor_tensor", dict(out=tmps[p_][h][:, 128 * q:128 * q + 128], in0=S_sb[:, 2 * h + 1, :], scalar=Rn[:, k_, 4 * ib + q:4 * ib + q + 1], in1=ez[zb][:, 128 * q:128 * q + 128], op0=ALU.is_ge, op1=ALU.mult), SK + ["Rn", f"ez{zb}"], [f"tmp{p_}_{h}"])
                            continue
                        if ae == "A":
                            for q in range(4):
                                A("activation", dict(out=zt[zb][:, q, :], in_=S_sb[:, 2 * h + 1, :], func=AF.Identity, bias=S1t[:, h, 4 * ib + q:4 * ib + q + 1], scale=1.0), SK + ["S1t"], [f"zt{zb}"])
                        else:
                            addeng = G if ae in ("P", "D") else V
                            addeng("tensor_tensor", dict(out=zt[zb][:, :, :], in0=S_sb[:, 2 * h + 1, :].unsqueeze(1).broadcast_to([128, 4, 128]), in1=S1t[:, h, 4 * ib:4 * ib + 4].unsqueeze(2).broadcast_to([128, 4, 128]), op=ALU.add), SK + ["S1t"], [f"zt{zb}"])
                        if ae == "D":
                            k_ = DH.index(h)
                            V("tensor_tensor", dict(out=ez[zb][:, :].rearrange("p (a n) -> p a n", a=4), in0=E2d[:, k_, :].unsqueeze(1).broadcast_to([128, 4, 128]), in1=E1d[:, k_, 4 * ib:4 * ib + 4].unsqueeze(2).broadcast_to([128, 4, 128]), op=ALU.mult), ["E1d", "E2d"], [f"ez{zb}"])
                        else:
                            A("activation", dict(out=ez[zb][:, :], in_=zt[zb][:, :, :].rearrange("p a n -> p (a n)"), func=AF.Exp, bias=sv[:, 32 + h:33 + h], scale=1.0), [f"zt{zb}", ("sv", "bias")], [f"ez{zb}"])
                        V("scalar_tensor_tensor", dict(out=tmps[p_][h][:, :], in0=zt[zb][:, :, :].rearrange("p a n -> p (a n)"), scalar=0.0, in1=ez[zb][:, :], op0=ALU.is_ge, op1=ALU.mult), [f"zt{zb}", f"ez{zb}"], [f"tmp{p_}_{h}"])

                def stageT(ib):
                    p_ = ib % 2
                    for _d in range(NDUM):
                        T("matmul", dict(out=pS[:, :], lhsT=identb[:, :], rhs=gA[:, _d % 4, :], start=True, stop=True), [], ["pSdummy"])
                    for q in range(4):
                        for h in range(8):
                            T("matmul", dict(out=pgt[p_][:, 128 * q:128 * q + 128], lhsT=tmps[p_][h][:, 128 * q:128 * q + 128], rhs=identb[:, :], start=(h == 0), stop=(h == 7)), [f"tmp{p_}_{h}", "identb"], [f"pgt{p_}"])

                def stageW(ib):
                    p_ = ib % 2
                    V("tensor_tensor", dict(out=wT[p_][:, :, :].rearrange("p a t -> p (a t)"), in0=pgt[p_][:, :], in1=gA[:, ib % 4, :], op=ALU.mult), [f"pgt{p_}", ("gA", ib % 4)], [f"wT{p_}"])
                    for q in range(4):
                        for hf in range(2):
                            T("matmul", dict(out=pacc[hf][:, :], lhsT=wT[p_][:, q, :], rhs=Vt[p_][:, q, 512 * hf:512 * hf + 512], start=(ib == 0 and q == 0), stop=(ib == NIB - 1 and q == 3)), [f"wT{p_}", f"Vt{p_}"], [f"paccF{hf}"])

                loadUT(0)
                loadUT(1)
                computeA(0)
                evacA(0)
                loadUT(2)
                computeA(1)
                evacA(1)
                for k in range(NIB + 2):
                    if k + 3 < NIB:
                        loadUT(k + 3)
                    if 0 <= k - 1 < NIB:
                        loadVt(k - 1)
                    if k % 4 == 2:
                        gelu_burst(range(k - 2, min(k + 2, NIB)))
                    if 0 <= k - 1 < NIB:
                        stageT(k - 1)
                    if 0 <= k - 2 < NIB:
                        stageW(k - 2)
                    if k + 2 < NIB:
                        computeA(k + 2)
                    if k < NIB:
                        stageG(k)
                    if k + 2 < NIB:
                        evacA(k + 2)
                SCR = [("scr", s_) for s_ in range(4)]
                for hf in range(2):
                    V("tensor_tensor", dict(out=scr16[:, 512 * hf:512 * hf + 512], in0=pacc[hf][:, :], in1=h2t[:, 512 * hf:512 * hf + 512], op=ALU.add), [f"paccF{hf}"] + H2, SCR)
                P.dma("sync", "ost", dict(out=out[128 * j:128 * j + 128, :], in_=scr16[:, 0:1024]), SCR, SCR + [("out", j)])
            P.barrier()
            P.emit()
        P.barrier()
        P.emit()
    return nc, dbg_out

_CACHE = {}


def kernel(**inputs):
    n = 8
    if "nc" not in _CACHE:
        _CACHE["nc"] = build_nc()[0]
    nc = _CACHE["nc"]
    f = lambda a: np.ascontiguousarray(np.asarray(a, dtype=np.float32))
    x = f(inputs["x"])
    shared = {
        "meta_tokens": f(inputs["meta_tokens"]),
        "norm_mix": f(inputs["norm_mix"]).reshape(1, D),
        "w_in": f(inputs["w_in"]).reshape(D, 3600),
        "conv_qk": f(inputs["conv_qk"]).reshape(4, 1024),
        "b_igate": f(inputs["b_igate"]).reshape(1, 4),
        "b_fgate_m": f(inputs["b_fgate_m"]).reshape(1, 4),
        "m_out_norm": f(inputs["m_out_norm"]).reshape(1, 512),
        "b_fgate_f": f(inputs["b_fgate_f"]).reshape(8, 1),
        "f_q_norm": f(inputs["f_q_norm"]).reshape(1, 64),
        "f_k_norm": f(inputs["f_k_norm"]).reshape(1, 64),
        "w_out": f(inputs["w_out"]).reshape(D, D),
        "norm_ffn": f(inputs["norm_ffn"]).reshape(1, D),
        "peer_query": f(inputs["peer_query"]).reshape(D, 2048),
        "peer_sub_keys": f(inputs["peer_sub_keys"]).reshape(2048, 128),
        "peer_u": f(inputs["peer_u"]).reshape(16384, D),
        "peer_v": f(inputs["peer_v"]).reshape(16384, D),
    }
    in_maps = [dict(shared, x=x[b]) for b in range(n)]
    res = run_bass_kernel_spmd(nc, in_maps, core_ids=list(range(n)))
    return np.stack([np.asarray(res.results[b]["out"], dtype=np.float32) for b in range(n)], axis=0)
```

```python
import numpy as np
import concourse.bass as bass
import concourse.mybir as mybir
from concourse.bass_utils import run_bass_kernel_spmd
from contextlib import ExitStack

F32 = mybir.dt.float32
BF16 = mybir.dt.bfloat16
AF = mybir.ActivationFunctionType
ALU = mybir.AluOpType
AX = mybir.AxisListType

ENGS = ("tensor", "vector", "scalar", "gpsimd", "sync")
import os
RELAX = os.environ.get("RELAX", "0") == "1"
VCLOCK = os.environ.get("VCLOCK", "1") == "1"

D = 1024
SEQ = 2048
NMETA = 16
L = SEQ + NMETA
LM = 2112
LP = 2176
NCH = 33
EPS = 1e-6
NEG = -1.0e30


class Prog:
    def __init__(self, nc, es):
        self.nc = nc
        self.es = es
        self.ops = {e: [] for e in ENGS}
        self.cnt = {e: 0 for e in ENGS}
        self.sem = {e: es.enter_context(nc.semaphore("s_" + e)) for e in ENGS}
        self.waited = {e: {} for e in ENGS}
        self.snap = {}
        self.dclock = {}
        self.last_w = {}
        self.readers = {}
        self.dsems = {}
        self.semobj = {("e", e): self.sem[e] for e in ENGS}
        self.nops = 0
        self.nwaits = 0

    @staticmethod
    def _merge(dst, src):
        for k, v in src.items():
            if dst.get(k, 0) < v:
                dst[k] = v

    def _deps(self, eng, reads, writes):
        deps = {}

        def add(d):
            if d is None:
                return
            k, v = d
            if deps.get(k, 0) < v:
                deps[k] = v
        me = ("e", eng)
        for k in reads:
            add(self.last_w.get(k))
        for k in writes:
            lw = self.last_w.get(k)
            if lw is not None and not (RELAX and lw[0] == me):
                add(lw)
            for sk, v in self.readers.get(k, {}).items():
                if RELAX and sk == me:
                    continue
                add((sk, v))
        out = []
        w = self.waited[eng]
        for sk, v in sorted(deps.items(), key=lambda kv: -kv[1]):
            if eng == "tensor" and sk == ("e", "tensor"):
                continue
            if sk[0] == "d":
                v = self.dsems[sk[1]][1]
            if w.get(sk, 0) < v:
                w[sk] = v
                out.append((sk, v))
                if VCLOCK:
                    if sk[0] == "d":
                        self._merge(w, self.dclock.get(sk[1], {}))
                    elif sk != me:
                        self._merge_snap(w, sk, v)
        self.nwaits += len(out)
        return out

    def _merge_snap(self, w, sk, v):
        s = self.snap.get((sk, v))
        if s is not None:
            own = w.get(("e", self._cur), 0)
            self._merge(w, s)
            if s.get(("e", self._cur), 0) > own:
                w[("e", self._cur)] = own

    def op(self, eng, meth, kw, reads=(), writes=()):
        fn = (meth, kw)
        self._cur = eng
        waits = self._deps(eng, reads, writes)
        self.cnt[eng] += 1
        n = self.cnt[eng]
        me = ("e", eng)
        self.ops[eng].append((waits, fn, (me, 1)))
        self.nops += 1
        if VCLOCK:
            s = dict(self.waited[eng])
            s[me] = n
            self.snap[(me, n)] = s
        for k in reads:
            self.readers.setdefault(k, {})[me] = n
        for k in writes:
            self.last_w[k] = (me, n)
            self.readers[k] = {}

    def dma(self, eng, slot, kw, reads=(), writes=()):
        fn = ("dma_start", kw)
        if slot not in self.dsems:
            s = self.es.enter_context(self.nc.semaphore("d_" + slot))
            self.dsems[slot] = [s, 0]
            self.semobj[("d", slot)] = s
        self._cur = eng
        waits = self._deps(eng, reads, writes)
        self.dsems[slot][1] += 16
        v = self.dsems[slot][1]
        me = ("d", slot)
        self.ops[eng].append((waits, fn, (me, 16)))
        self.nops += 1
        if VCLOCK:
            dc = self.dclock.setdefault(slot, {})
            own = dict(self.waited[eng])
            own.pop(("e", eng), None)
            self._merge(dc, own)
        for k in reads:
            self.readers.setdefault(k, {})[me] = v
        for k in writes:
            self.last_w[k] = (me, v)
            self.readers[k] = {}

    def barrier(self, exclude=()):
        cur = [(("e", e), self.cnt[e]) for e in ENGS if self.cnt[e] > 0]
        cur += [(("d", s), v[1]) for s, v in self.dsems.items() if v[1] > 0 and s not in exclude]
        for e in ENGS:
            w = self.waited[e]
            waits = []
            for sk, v in cur:
                if sk == ("e", e):
                    if e == "tensor":
                        continue
                if w.get(sk, 0) < v:
                    w[sk] = v
                    waits.append((sk, v))
            if waits:
                self.ops[e].append((waits, None, None))
        self.last_w = {k: v for k, v in self.last_w.items() if v[0][0] == "d" and v[0][1] in exclude}
        self.readers = {}
        self.snap = {}

    def emit(self):
        nc = self.nc
        with nc.Block() as block:
            for e in ENGS:
                ops = self.ops[e]
                if not ops:
                    continue
                semobj = self.semobj

                def body(eng, ops=ops):
                    for waits, fn, inc in ops:
                        for sk, v in waits:
                            eng.wait_ge(semobj[sk], v)
                        if fn is not None:
                            ins = getattr(eng, fn[0])(**fn[1])
                            ins.then_inc(semobj[inc[0]], inc[1])
                getattr(block, e)(body)
        self.ops = {e: [] for e in ENGS}


def build_nc(debug=None, stop=None):
    nc = bass.Bass("TRN2", target_bir_lowering=False)

    def din(name, shape):
        return nc.dram_tensor(name, list(shape), F32, kind="ExternalInput").ap()

    x = din("x", [SEQ, D])
    meta = din("meta_tokens", [NMETA, D])
    norm_mix = din("norm_mix", [1, D])
    w_in = din("w_in", [D, 3600])
    conv_qk = din("conv_qk", [4, 1024])
    b_igate = din("b_igate", [1, 4])
    b_fgate_m = din("b_fgate_m", [1, 4])
    m_out_norm = din("m_out_norm", [1, 512])
    b_fgate_f = din("b_fgate_f", [8, 1])
    f_q_norm = din("f_q_norm", [1, 64])
    f_k_norm = din("f_k_norm", [1, 64])
    w_out = din("w_out", [D, D])
    norm_ffn = din("norm_ffn", [1, D])
    peer_query = din("peer_query", [D, 2048])
    peer_keys = din("peer_sub_keys", [2048, 128])
    peer_u = din("peer_u", [16384, D])
    peer_v = din("peer_v", [16384, D])
    out = nc.dram_tensor("out", [SEQ, D], F32, kind="ExternalOutput").ap()
    ut_s = nc.dram_tensor("ut_s", [8, 128, 16384], BF16).ap()
    vb_s = nc.dram_tensor("vb_s", [16384, D], BF16).ap()

    dbg_out = {}
    VEX = ("vprep",)

    es = ExitStack()
    with es:
        P = Prog(nc, es)

        def sb(stack, name, shape, dt=F32):
            return stack.enter_context(nc.sbuf_tensor(name, list(shape), dt))

        def ps(stack, name, dt=F32):
            shape = [128, 512] if dt == F32 else [128, 1024]
            return stack.enter_context(nc.psum_tensor(name, shape, dt))

        def dump(name, ap, key, shape):
            if debug is None or name not in debug:
                return
            t = nc.dram_tensor("dbg_" + name, list(shape), ap.dtype, kind="ExternalOutput").ap()
            dbg_out[name] = t
            P.dma("sync", "dbg", dict(out=t, in_=ap), key if isinstance(key, list) else [key], ["dbg_" + name])

        V = lambda m, kw, r=(), w=(): P.op("vector", m, kw, r, w)
        A = lambda m, kw, r=(), w=(): P.op("scalar", m, kw, r, w)
        G = lambda m, kw, r=(), w=(): P.op("gpsimd", m, kw, r, w)
        T = lambda m, kw, r=(), w=(): P.op("tensor", m, kw, r, w)

        identf = sb(es, "identf", [128, 128])
        identb = sb(es, "identb", [128, 128], BF16)
        epsc = sb(es, "epsc", [128, 1])
        onec = sb(es, "onec", [128, 1])
        lnsc = sb(es, "lnsc", [128, 1])
        gffn_b = sb(es, "gffn_b", [128, D])

        G("iota", dict(out=identf[:], pattern=[[1, 128]], base=0, channel_multiplier=-1,
                           allow_small_or_imprecise_dtypes=True), w=["identf"])
        V("tensor_scalar", dict(out=identb[:], in0=identf[:], scalar1=0.0, scalar2=None, op0=ALU.is_equal), ["identf"], ["identb"])
        iotaf = identf
        G("memset", dict(ap=epsc[:], constant=EPS), w=["epsc"])
        G("memset", dict(ap=onec[:], constant=1.0), w=["onec"])
        G("memset", dict(ap=lnsc[:], constant=float(-0.5 * np.log(128.0))), w=["lnsc"])
        P.dma("sync", "c0", dict(out=gffn_b[:], in_=norm_ffn[0:1, :].partition_broadcast(128)), writes=["gffn_b"])

        def rstd_from_ss(ss_ap, dst_ap, n, r, keys_r, keys_w, tmp_ap):
            A("activation", dict(out=tmp_ap, in_=ss_ap, func=AF.Ln, bias=epsc[0:r, :], scale=1.0 / n), keys_r + ["epsc"], keys_w[:1])
            A("activation", dict(out=dst_ap, in_=tmp_ap, func=AF.Exp, scale=-0.5), keys_w[:1], keys_w[1:])


        import os as _os

        def gen_prep(ub, uts, ptu):
            k = 0
            for eg in range(32):
                b = eg % 2
                P.dma("gpsimd", f"ub{b}", dict(out=ub[b][:], in_=peer_u[512 * eg:512 * eg + 512, :].rearrange("(q p) d -> p q d", p=128)), writes=[f"ub{b}"])
                yield
                for q in range(4):
                    tb = ptu[k % 4]
                    kt = f"ptu{k % 4}"
                    for dc in range(8):
                        T("transpose", dict(out=tb[:, 128 * dc:128 * dc + 128], in_=ub[b][:, q, 128 * dc:128 * dc + 128], identity=identb[:, :]), [f"ub{b}", "identb"], [kt])
                        yield
                    if k % 2 == 0:
                        A("activation", dict(out=uts[b][:, :, 128 * q:128 * q + 128], in_=tb[:, :].rearrange("p (a e) -> p a e", a=8), func=AF.Copy), [kt], [f"uts{b}"])
                    else:
                        V("tensor_copy", dict(out=uts[b][:, :, 128 * q:128 * q + 128], in_=tb[:, :].rearrange("p (a e) -> p a e", a=8)), [kt], [f"uts{b}"])
                    yield
                    k += 1
                P.dma("sync", f"uts{b}", dict(out=ut_s[:, :, 512 * eg:512 * eg + 512].rearrange("dc p e -> p dc e"), in_=uts[b][:]), [f"uts{b}"], ["ut_s"])
                yield

        prep_gen = [None]

        def bg_step(n):
            g_ = prep_gen[0]
            if g_ is None:
                return
            for _ in range(n):
                try:
                    next(g_)
                except StopIteration:
                    prep_gen[0] = None
                    return

        yT = sb(es, "yT", [128, 8, LP], BF16)
        with ExitStack() as ms:
            causf = sb(ms, "causf", [128, 128])
            causb = sb(ms, "causb", [128, 128], BF16)
            negm = sb(ms, "negm", [128, 128])
            onesf = sb(ms, "onesf", [128, 128])
            gmix_b = sb(ms, "gmix_b", [128, D])
            V("tensor_scalar", dict(out=causf[:], in0=iotaf[:], scalar1=0.0, scalar2=None, op0=ALU.is_ge), ["identf"], ["causf"])
            V("tensor_copy", dict(out=causb[:], in_=causf[:]), ["causf"], ["causb"])
            V("tensor_scalar", dict(out=negm[:], in0=iotaf[:], scalar1=0.0, scalar2=NEG, op0=ALU.is_gt, op1=ALU.mult), ["identf"], ["negm"])
            V("tensor_scalar", dict(out=identf[:], in0=iotaf[:], scalar1=0.0, scalar2=None, op0=ALU.is_equal), ["identf", "identb", "causf", "negm"], ["identf"])
            G("memset", dict(ap=onesf[:], constant=1.0), w=["onesf"])
            P.dma("sync", "c0b", dict(out=gmix_b[:], in_=norm_mix[0:1, :].partition_broadcast(128)), writes=["gmix_b"])
            hnT = sb(ms, "hnT", [128, 8, LP], BF16)
            big1 = sb(ms, "big1", [128, 8, LP], BF16)
            wgate = sb(ms, "wgate", [128, 8, 16], BF16)
            P.dma("gpsimd", "c1", dict(out=wgate[:, :, 0:8], in_=w_in[:, 2048:2056].rearrange("(dc p) c -> p dc c", p=128)), writes=["wgate"])
            P.dma("gpsimd", "c1", dict(out=wgate[:, :, 8:16], in_=w_in[:, 3592:3600].rearrange("(dc p) c -> p dc c", p=128)), writes=["wgate"])

            def tile_cols(j):
                return (0, 16) if j == 0 else (16 + 128 * (j - 1), 128)

            pps = ExitStack()
            pps.__enter__()
            if not _os.environ.get("NOPREP"):
                ub_ = [sb(pps, f"ub{i}", [128, 4, D], BF16) for i in range(2)]
                uts_ = [sb(pps, f"uts{i}", [128, 8, 512], BF16) for i in range(2)]
                ptu_ = [ps(pps, f"ptu{i}", BF16) for i in range(4)]
                prep_gen[0] = gen_prep(ub_, uts_, ptu_)
            with ExitStack() as pa:
                xt = [sb(pa, f"xt{i}", [128, D]) for i in range(2)]
                junk = sb(pa, "junkA", [128, D], BF16)
                hn = [sb(pa, f"hn{i}", [128, D], BF16) for i in range(2)]
                ssA = [sb(pa, f"ssA{i}", [128, 3]) for i in range(2)]
                tpA = [ps(pa, f"tpA{i}", BF16) for i in range(2)]
                for j in range(17):
                    c0, r = tile_cols(j)
                    b = j % 2
                    xtb, hnb, ssb, tpb = xt[b], hn[b], ssA[b], tpA[b]
                    kx, kh, ks, kt = f"xt{b}", f"hn{b}", f"ssA{b}", f"tpA{b}"
                    if j == 0:
                        P.dma("sync", kx, dict(out=xtb[0:16, :], in_=meta[:, :]), writes=[kx])
                    else:
                        P.dma("sync", kx, dict(out=xtb[:, :], in_=x[128 * (j - 1):128 * j, :]), writes=[kx])
                    V("scalar_tensor_tensor", dict(out=junk[0:r, :], in0=xtb[0:r, :], scalar=1.0, in1=xtb[0:r, :], op0=ALU.mult, op1=ALU.mult, accum_out=ssb[0:r, 0:1]), [kx], ["junkA", ks + "a"])
                    rstd_from_ss(ssb[0:r, 0:1], ssb[0:r, 2:3], float(D), r, [ks + "a"], [ks + "b", ks + "c"], ssb[0:r, 1:2])
                    V("scalar_tensor_tensor", dict(out=hnb[0:r, :], in0=xtb[0:r, :], scalar=ssb[0:r, 2:3], in1=gmix_b[0:r, :], op0=ALU.mult, op1=ALU.mult), [kx, ks + "c", "gmix_b"], [kh])
                    for dc in range(8):
                        T("transpose", dict(out=tpb[:, dc * 128:dc * 128 + r], in_=hnb[0:r, dc * 128:(dc + 1) * 128], identity=identb[0:r, 0:r]), [kh, "identb"], [kt])
                    tpv = tpb[:, :].rearrange("p (a b) -> p a b", a=8)
                    A("activation", dict(out=hnT[:, :, c0:c0 + r], in_=tpv[:, :, 0:r], func=AF.Copy), [kt], [("hnT", j)])
                    bg_step(int(_os.environ.get("BGA", 4)))
                G("memset", dict(ap=hnT[:, :, L:LP], constant=0.0), w=[("hnT", 17)])
                P.barrier(exclude=VEX)
                P.emit()
            HN_ALL = [("hnT", j) for j in range(18)]
            dump("hnT", hnT[:, :, 0:L], HN_ALL, [128, 8, L])

            if stop == "A":
                P.barrier(exclude=VEX)
                P.emit()
                return nc, dbg_out

            CB = [(0, 512), (512, 512), (1024, 512), (1536, 512), (2048, 64)]
            with ExitStack() as pb:
                wg = [sb(pb, f"wg{i}", [128, 8, 512], BF16) for i in range(2)]
                cw = sb(pb, "cw", [128, 4, 8])
                zc = [sb(pb, f"zc{i}", [128, LM]) for i in range(2)]
                ac = [sb(pb, f"ac{i}", [128, LM]) for i in range(2)]
                pz = [ps(pb, f"pz{i}") for i in range(2)]
                for j in range(4):
                    P.dma("sync", "cw", dict(out=cw[:, j, :], in_=conv_qk[j:j + 1, :].rearrange("o (k c) -> c (o k)", c=128), allow_slow_non_contiguous=True), writes=["cw"])
                it = 0
                for g in range(2):
                    wgb = wg[g]
                    P.dma("gpsimd", f"wg{g}", dict(out=wgb[:], in_=w_in[:, 512 * g:512 * (g + 1)].rearrange("(dc p) c -> p dc c", p=128)), writes=[f"wg{g}"])
                    for ck in range(4):
                        idx = g * 4 + ck
                        b = idx % 2
                        zcb, acb = zc[b], ac[b]
                        for (c0, wdt) in CB:
                            pzb = pz[it % 2]
                            kp = f"pz{it % 2}"
                            it += 1
                            for dc in range(8):
                                T("matmul", dict(out=pzb[:, 0:wdt], lhsT=wgb[:, dc, ck * 128:(ck + 1) * 128], rhs=hnT[:, dc, c0:c0 + wdt], start=(dc == 0), stop=(dc == 7)), [f"wg{g}"] + HN_ALL, [kp])
                            A("activation", dict(out=zcb[:, c0:c0 + wdt], in_=pzb[:, 0:wdt], func=AF.Copy), [kp], [f"zc{b}"])
                            bg_step(int(_os.environ.get("BGB", 30)))
                        V("tensor_scalar", dict(out=acb[:, :], in0=zcb[:, :], scalar1=cw[:, 3, idx:idx + 1], scalar2=None, op0=ALU.mult), [f"zc{b}", "cw"], [f"ac{b}"])
                        for sh in (1, 2, 3):
                            V("scalar_tensor_tensor", dict(out=acb[:, sh:LM], in0=zcb[:, 0:LM - sh], scalar=cw[:, 3 - sh, idx:idx + 1], in1=acb[:, sh:LM], op0=ALU.mult, op1=ALU.add), [f"zc{b}", "cw", f"ac{b}"], [f"ac{b}"])
                        A("activation", dict(out=big1[:, idx, 0:LM], in_=acb[:, :], func=AF.Silu), [f"ac{b}"], [("qk", idx)])
                bg_step(100000)
                P.barrier(exclude=VEX)
                P.emit()
            pps.close()
            QK_ALL = [("qk", i) for i in range(8)]
            dump("qk", big1[:, :, 0:L], QK_ALL, [128, 8, L])
            if stop == "B":
                P.barrier(exclude=VEX)
                P.emit()
                return nc, dbg_out

            with ExitStack() as pc:
                wv = sb(pc, "wv", [128, 8, 512], BF16)
                wo = sb(pc, "wo", [128, 8, 512], BF16)
                P.dma("gpsimd", "wv", dict(out=wv[:], in_=w_in[:, 1024:1536].rearrange("(dc p) c -> p dc c", p=128)), writes=["wv"])
                P.dma("gpsimd", "wo", dict(out=wo[:], in_=w_in[:, 1536:2048].rearrange("(dc p) c -> p dc c", p=128)), writes=["wo"])
                bm_b = sb(pc, "bm_b", [128, 8])
                P.dma("sync", "c2", dict(out=bm_b[:, 0:4], in_=b_igate[0:1, :].partition_broadcast(128)), writes=["bm_b"])
                P.dma("sync", "c2", dict(out=bm_b[:, 4:8], in_=b_fgate_m[0:1, :].partition_broadcast(128)), writes=["bm_b"])
                mg_b = sb(pc, "mg_b", [128, 512])
                P.dma("sync", "c2", dict(out=mg_b[:], in_=m_out_norm[0:1, :].partition_broadcast(128)), writes=["mg_b"])
                CT = sb(pc, "CT", [128, 4, 129])
                CTb = sb(pc, "CTb", [128, 4, 129], BF16)
                mst = sb(pc, "mst", [128, 4])
                G("memset", dict(ap=CT[:], constant=0.0), w=[("CT", h) for h in range(4)])
                G("memset", dict(ap=CTb[:], constant=0.0), w=[("CTb", h) for h in range(4)])
                G("memset", dict(ap=mst[:], constant=0.0), w=["mst"])
                NB = 2
                vx = [sb(pc, f"vx{i}", [64, 4, 129], BF16) for i in range(NB)]
                for i in range(NB):
                    G("memset", dict(ap=vx[i][:], constant=1.0), w=[f"vx{i}"])
                og = [sb(pc, f"og{i}", [64, 512]) for i in range(NB)]
                sm = [sb(pc, f"sm{i}", [128, 64]) for i in range(NB)]
                dg = [sb(pc, f"dg{i}", [64, 4, 64]) for i in range(NB)]
                mk = [sb(pc, f"mk{i}", [64, 4, 64]) for i in range(NB)]
                spT = [sb(pc, f"spT{i}", [64, 64], BF16) for i in range(4)]
                tmpB = [sb(pc, f"tmpB{i}", [64, 129]) for i in range(2)]
                num = [sb(pc, f"num{i}", [64, 129]) for i in range(2)]
                junkc = sb(pc, "junkc", [64, 128], BF16)
                hs = [sb(pc, f"hs{i}", [64, 8]) for i in range(4)]
                ytm = [sb(pc, f"ytm{i}", [64, 4, 128], BF16) for i in range(NB)]
                kw = [sb(pc, f"kw{i}", [64, 128], BF16) for i in range(2)]
                pg = ps(pc, "pg")
                pv = ps(pc, "pv")
                po = ps(pc, "po")
                pqk = ps(pc, "pqk")
                pAB = [ps(pc, f"pAB{i}") for i in range(2)]
                pU = ps(pc, "pU")
                ptr = ps(pc, "ptr", BF16)

                SM = dict(gt=0, e1=8, lfn=12, a=16, ea=20, cm=24, amax=28, M=32, Mend=36, eM=40, w=44, wg=48, dec=52, emt=56, t0=60)
                import os
                CLIM = int(os.environ.get("CLIM", NCH))
                LVL = int(os.environ.get("LVL", 99))
                def chunk_ctx(c):
                    Tn = 64
                    c0 = 64 * c
                    b = c % NB
                    s_ = sm[b]
                    ksm = lambda nm, b=b: (f"sm{b}", nm)
                    col = lambda nm, w=4, s_=s_, r=None: s_[0:(Tn if r is None else r), SM[nm]:SM[nm] + w]
                    hk = HN_ALL
                    return Tn, c0, b, s_, ksm, col, hk

                def gen_pre(c):
                    Tn, c0, b, s_, ksm, col, hk = chunk_ctx(c)
                    if c < 32 and not _os.environ.get("NOPREP"):
                        for r in (2 * c, 2 * c + 1):
                            P.dma("gpsimd", "vprep", dict(out=vb_s[256 * r:256 * r + 256, :], in_=peer_v[256 * r:256 * r + 256, :]), writes=["vb_s"])
                            yield
                    for dc in range(8):
                        T("matmul", dict(out=pg[0:Tn, 0:8], lhsT=hnT[:, dc, c0:c0 + Tn], rhs=wgate[:, dc, 0:8], start=(dc == 0), stop=(dc == 7)), hk + ["wgate"], ["pg"])
                        yield
                    V("tensor_tensor", dict(out=col("gt", 8), in0=pg[0:Tn, 0:8], in1=bm_b[0:Tn, :], op=ALU.add), ["pg", "bm_b"], [ksm("gt")])
                    yield
                    A("activation", dict(out=col("e1"), in_=s_[0:Tn, 4:8], func=AF.Exp, scale=-1.0), [ksm("gt")], [ksm("e1")])
                    yield
                    A("activation", dict(out=col("lfn"), in_=col("e1"), func=AF.Ln, bias=onec[0:Tn, :], scale=1.0), [ksm("e1"), "onec"], [ksm("lfn")])
                    yield
                    T("matmul", dict(out=pg[0:Tn, 8:12], lhsT=causf[0:Tn, 0:Tn], rhs=col("lfn"), start=True, stop=True), [ksm("lfn"), "causf"], ["pg"])
                    yield
                    T("matmul", dict(out=pg[:, 16:20], lhsT=onesf[0:Tn, :], rhs=col("lfn"), start=True, stop=True), [ksm("lfn"), "onesf"], ["pg"])
                    yield
                    V("tensor_tensor", dict(out=col("a"), in0=s_[0:Tn, 0:4], in1=pg[0:Tn, 8:12], op=ALU.add), [ksm("gt"), "pg"], [ksm("a")])
                    yield
                    dgb, mkb = dg[b], mk[b]
                    V("tensor_tensor", dict(out=dgb[0:Tn, :, 0:Tn], in0=identf[0:Tn, 0:Tn].unsqueeze(1).broadcast_to([Tn, 4, Tn]), in1=col("a").unsqueeze(2).broadcast_to([Tn, 4, Tn]), op=ALU.mult), [ksm("a"), "identf"], [f"dg{b}"])
                    yield
                    abc = pg[:, 256:512].rearrange("p (h s) -> p h s", h=4)
                    for h in range(4):
                        T("matmul", dict(out=pg[:, 256 + 64 * h:256 + 64 * h + Tn], lhsT=onesf[0:Tn, :], rhs=dgb[0:Tn, h, 0:Tn], start=True, stop=True), [f"dg{b}", "onesf"], ["pg"])
                        yield
                    V("tensor_tensor", dict(out=mkb[0:Tn, :, 0:Tn], in0=abc[0:Tn, :, 0:Tn], in1=negm[0:Tn, 0:Tn].unsqueeze(1).broadcast_to([Tn, 4, Tn]), op=ALU.add), ["pg", "negm"], [f"mk{b}"])
                    yield
                    V("tensor_reduce", dict(out=col("cm"), in_=mkb[0:Tn, :, 0:Tn], axis=AX.X, op=ALU.max), [f"mk{b}"], [ksm("cm")])
                    yield
                    V("tensor_reduce", dict(out=s_[:, SM["amax"]:SM["amax"] + 4], in_=abc[:, :, 0:Tn], axis=AX.X, op=ALU.max), ["pg"], [ksm("amax")])
                    yield
                    A("activation", dict(out=col("ea"), in_=col("a"), func=AF.Exp, bias=lnsc[0:Tn, :], scale=1.0), [ksm("a"), "lnsc"], [ksm("ea")])
                    yield
                    V("tensor_tensor", dict(out=col("M"), in0=col("cm"), in1=mst[0:Tn, :], op=ALU.max), [ksm("cm"), "mst"], [ksm("M")])
                    yield
                    V("tensor_tensor", dict(out=s_[:, SM["Mend"]:SM["Mend"] + 4], in0=s_[:, SM["amax"]:SM["amax"] + 4], in1=mst[:, :], op=ALU.max), [ksm("amax"), "mst"], [ksm("Mend")])
                    yield
                    A("activation", dict(out=col("eM"), in_=col("M"), func=AF.Exp, scale=-1.0), [ksm("M")], [ksm("eM")])
                    yield
                    V("tensor_tensor", dict(out=col("w"), in0=mst[0:Tn, :], in1=col("M"), op=ALU.subtract), [ksm("M"), "mst"], [ksm("w")])
                    yield
                    A("activation", dict(out=col("w"), in_=col("w"), func=AF.Exp), [ksm("w")], [ksm("w")])
                    yield
                    V("tensor_tensor", dict(out=col("wg"), in0=col("a"), in1=col("Mend"), op=ALU.subtract), [ksm("a"), ksm("Mend")], [ksm("wg")])
                    yield
                    A("activation", dict(out=col("wg"), in_=col("wg"), func=AF.Exp, bias=lnsc[0:Tn, :], scale=1.0), [ksm("wg"), "lnsc"], [ksm("wg")])
                    yield
                    V("tensor_tensor", dict(out=s_[:, SM["dec"]:SM["dec"] + 4], in0=mst[:, :], in1=s_[:, SM["Mend"]:SM["Mend"] + 4], op=ALU.subtract), [ksm("Mend"), "mst"], [ksm("dec")])
                    yield
                    A("activation", dict(out=s_[:, SM["dec"]:SM["dec"] + 4], in_=s_[:, SM["dec"]:SM["dec"] + 4], func=AF.Exp), [ksm("dec")], [ksm("dec")])
                    yield
                    V("tensor_tensor", dict(out=col("emt"), in0=pg[0:Tn, 8:12], in1=col("M"), op=ALU.subtract), [ksm("M"), "pg"], [ksm("emt")])
                    yield
                    A("activation", dict(out=col("emt"), in_=col("emt"), func=AF.Exp), [ksm("emt")], [ksm("emt")])
                    yield
                    V("tensor_tensor", dict(out=mst[:, :], in0=s_[:, SM["Mend"]:SM["Mend"] + 4], in1=pg[:, 16:20], op=ALU.subtract), [ksm("Mend"), "pg", "mst"], ["mst"])
                    yield
                    vxb, ogb = vx[b], og[b]
                    for dc in range(8):
                        T("matmul", dict(out=pv[0:Tn, :], lhsT=hnT[:, dc, c0:c0 + Tn], rhs=wv[:, dc, :], start=(dc == 0), stop=(dc == 7)), hk + ["wv"], ["pv"])
                        yield
                    A("activation", dict(out=vxb[0:Tn, :, 0:128], in_=pv[0:Tn, :].rearrange("p (h d) -> p h d", h=4), func=AF.Copy), ["pv"], [f"vx{b}"])
                    yield
                    for dc in range(8):
                        T("matmul", dict(out=po[0:Tn, :], lhsT=hnT[:, dc, c0:c0 + Tn], rhs=wo[:, dc, :], start=(dc == 0), stop=(dc == 7)), hk + ["wo"], ["po"])
                        yield
                    A("activation", dict(out=ogb[0:Tn, :], in_=po[0:Tn, :], func=AF.Exp, scale=-1.0), ["po"], [f"og{b}"])
                    yield
                    V("tensor_scalar", dict(out=ogb[0:Tn, :], in0=ogb[0:Tn, :], scalar1=1.0, scalar2=None, op0=ALU.add), [f"og{b}"], [f"og{b}"])
                    yield
                    V("reciprocal", dict(out=ogb[0:Tn, :], in_=ogb[0:Tn, :]), [f"og{b}"], [f"og{b}"])
                    yield
                    V("tensor_tensor", dict(out=ogb[0:Tn, :], in0=ogb[0:Tn, :], in1=mg_b[0:Tn, :], op=ALU.mult), [f"og{b}", "mg_b"], [f"og{b}"])
                    yield
                    ytb = ytm[b]

                def gen_head(c, h):
                    Tn, c0, b, s_, ksm, col, hk = chunk_ctx(c)
                    vxb, ogb, ytb = vx[b], og[b], ytm[b]
                    hb = h % 2
                    qTc = big1[:, h, c0:c0 + Tn]
                    kTc = big1[:, 4 + h, c0:c0 + Tn]
                    sp = spT[h]
                    pab = pAB[hb]
                    hsb = hs[h]
                    khs = lambda nm, h=h: (f"hs{h}", nm)
                    T("matmul", dict(out=pqk[0:Tn, 64 * h:64 * h + Tn], lhsT=kTc, rhs=qTc, start=True, stop=True), [("qk", h), ("qk", 4 + h)], ["pqk"])
                    yield
                    T("matmul", dict(out=pab[0:Tn, 256:385], lhsT=qTc, rhs=CTb[:, h, :], start=True, stop=True), [("qk", h), ("CTb", h)], [f"pAB{hb}"])
                    yield
                    V("scalar_tensor_tensor", dict(out=sp[0:Tn, 0:Tn], in0=pqk[0:Tn, 64 * h:64 * h + Tn], scalar=col("ea")[:, h:h + 1], in1=causf[0:Tn, 0:Tn], op0=ALU.mult, op1=ALU.mult), ["pqk", ksm("ea"), "causf"], [f"spT{h}"])
                    yield
                    T("matmul", dict(out=pab[0:Tn, 0:129], lhsT=sp[0:Tn, 0:Tn], rhs=vxb[0:Tn, h, :], start=True, stop=True), [f"spT{h}", f"vx{b}"], [f"pAB{hb}"])
                    yield
                    tB, nm = tmpB[hb], num[hb]
                    A("activation", dict(out=tB[0:Tn, :], in_=pab[0:Tn, 256:385], func=AF.Copy, scale=col("w")[:, h:h + 1]), [f"pAB{hb}", ksm("w")], [f"tmpB{hb}"])
                    yield
                    V("scalar_tensor_tensor", dict(out=nm[0:Tn, :], in0=pab[0:Tn, 0:129], scalar=col("eM")[:, h:h + 1], in1=tB[0:Tn, :], op0=ALU.mult, op1=ALU.add), [f"pAB{hb}", f"tmpB{hb}", ksm("eM")], [f"num{hb}"])
                    yield
                    A("activation", dict(out=hsb[0:Tn, 7:8], in_=nm[0:Tn, 128:129], func=AF.Abs), [f"num{hb}"], [khs("abs")])
                    yield
                    V("tensor_scalar", dict(out=hsb[0:Tn, 0:1], in0=hsb[0:Tn, 7:8], scalar1=col("emt")[:, h:h + 1], scalar2=None, op0=ALU.max), [khs("abs"), ksm("emt")], [khs("den")])
                    yield
                    V("reciprocal", dict(out=hsb[0:Tn, 1:2], in_=hsb[0:Tn, 0:1]), [khs("den")], [khs("rden")])
                    yield
                    V("scalar_tensor_tensor", dict(out=junkc[0:Tn, :], in0=nm[0:Tn, 0:128], scalar=1.0, in1=nm[0:Tn, 0:128], op0=ALU.mult, op1=ALU.mult, accum_out=hsb[0:Tn, 2:3]), [f"num{hb}"], ["junkc", khs("ss")])
                    yield
                    V("tensor_scalar", dict(out=hsb[0:Tn, 3:4], in0=hsb[0:Tn, 2:3], scalar1=hsb[0:Tn, 1:2], scalar2=hsb[0:Tn, 1:2], op0=ALU.mult, op1=ALU.mult), [khs("ss"), khs("rden")], [khs("t1")])
                    yield
                    rstd_from_ss(hsb[0:Tn, 3:4], hsb[0:Tn, 5:6], 128.0, Tn, [khs("t1")], [khs("ln"), khs("rstd")], hsb[0:Tn, 4:5])
                    yield
                    V("tensor_tensor", dict(out=hsb[0:Tn, 6:7], in0=hsb[0:Tn, 5:6], in1=hsb[0:Tn, 1:2], op=ALU.mult), [khs("rstd"), khs("rden")], [khs("sc")])
                    yield
                    V("scalar_tensor_tensor", dict(out=ytb[0:Tn, h, :], in0=nm[0:Tn, 0:128], scalar=hsb[0:Tn, 6:7], in1=ogb[0:Tn, 128 * h:128 * (h + 1)], op0=ALU.mult, op1=ALU.mult), [f"num{hb}", khs("sc"), f"og{b}"], [(f"ytm{b}", h)])
                    yield
                    T("transpose", dict(out=ptr[:, 64 * h:64 * h + Tn], in_=ytb[0:Tn, h, :], identity=identb[0:Tn, 0:Tn]), [(f"ytm{b}", h), "identb"], ["ptr"])
                    yield
                    A("activation", dict(out=yT[:, h, c0:c0 + Tn], in_=ptr[:, 64 * h:64 * h + Tn], func=AF.Copy), ["ptr"], [("yT", h, c)])
                    yield

                def gen_state(c, h):
                    Tn, c0, b, s_, ksm, col, hk = chunk_ctx(c)
                    vxb = vx[b]
                    hb = h % 2
                    kTc = big1[:, 4 + h, c0:c0 + Tn]
                    yield
                    yield
                    kwb = kw[hb]
                    T("transpose", dict(out=ptr[0:Tn, 512 + 128 * hb:512 + 128 * hb + 128], in_=kTc, identity=identb[:, :]), [("qk", 4 + h), "identb"], ["ptr"])
                    yield
                    A("activation", dict(out=kwb[0:Tn, :], in_=ptr[0:Tn, 512 + 128 * hb:512 + 128 * hb + 128], func=AF.Copy, scale=col("wg")[:, h:h + 1]), ["ptr", ksm("wg")], [f"kw{hb}"])
                    yield
                    T("matmul", dict(out=pU[:, 256 * hb:256 * hb + 129], lhsT=kwb[0:Tn, :], rhs=vxb[0:Tn, h, :], start=True, stop=True), [f"kw{hb}", f"vx{b}"], ["pU"])
                    yield
                    V("scalar_tensor_tensor", dict(out=CT[:, h, :], in0=CT[:, h, :], scalar=s_[:, SM["dec"] + h:SM["dec"] + h + 1], in1=pU[:, 256 * hb:256 * hb + 129], op0=ALU.mult, op1=ALU.add), [("CT", h), ksm("dec"), "pU"], [("CT", h)])
                    yield
                    A("activation", dict(out=CTb[:, h, :], in_=CT[:, h, :], func=AF.Copy), [("CT", h)], [("CTb", h)])
                    yield


                def run_rr(gens, bg=()):
                    gens = list(gens)
                    bg = list(bg)
                    while gens:
                        alive = []
                        for g_ in gens:
                            try:
                                next(g_)
                                alive.append(g_)
                            except StopIteration:
                                pass
                        gens = alive
                        for g_ in list(bg):
                            try:
                                for _ in range(BGS):
                                    next(g_)
                            except StopIteration:
                                bg.remove(g_)
                    return bg

                NCC = min(NCH, CLIM)
                BGS = int(os.environ.get("BGS", 2))
                run_rr([gen_pre(0)])
                for c in range(NCC):
                    bg = [gen_pre(c + 1)] if c + 1 < NCC else []
                    bg = run_rr([gen_head(c, 0), gen_head(c, 1), gen_state(c, 0), gen_state(c, 1)], bg)
                    bg = run_rr([gen_head(c, 2), gen_head(c, 3), gen_state(c, 2), gen_state(c, 3)], bg)
                    run_rr(bg)
                P.barrier(exclude=VEX)
                P.emit()
            dump("yTm", yT[:, 0:4, 0:L], [], [128, 4, L])
            if stop == "C":
                P.barrier(exclude=VEX)
                P.emit()
                return nc, dbg_out

            NBLK = 17
            kTe = sb(ms, "kTe", [70, 8, LP], BF16)
            fvx = sb(ms, "fvx", [128, NBLK, 8, 65], BF16)
            prow = sb(ms, "prow", [48, LP], BF16)
            qTe = big1
            with ExitStack() as pd:
                gq_b = sb(pd, "gq_b", [128, 64])
                gk_b = sb(pd, "gk_b", [128, 64])
                nbff = sb(pd, "nbff", [8, 1])
                P.dma("sync", "c3", dict(out=gq_b[:], in_=f_q_norm[0:1, :].partition_broadcast(128)), writes=["gq_b"])
                P.dma("sync", "c3", dict(out=gk_b[:], in_=f_k_norm[0:1, :].partition_broadcast(128)), writes=["gk_b"])
                P.dma("sync", "c3", dict(out=nbff[:], in_=b_fgate_f[:, :]), writes=["nbff"])
                bffb = sb(pd, "bffb", [128, 8])
                P.dma("sync", "c3", dict(out=bffb[:], in_=b_fgate_f.rearrange("h o -> o h").partition_broadcast(128)), writes=["bffb"])
                V("tensor_scalar", dict(out=gq_b[:], in0=gq_b[:], scalar1=0.125, scalar2=None, op0=ALU.mult), ["gq_b"], ["gq_b"])
                V("tensor_scalar", dict(out=nbff[:], in0=nbff[:], scalar1=-1.0, scalar2=None, op0=ALU.mult), ["nbff"], ["nbff"])
                G("memset", dict(ap=fvx[:], constant=1.0), w=["fvx"])
                G("memset", dict(ap=qTe[64:70, :, :], constant=1.0), w=["qTe_ext"])
                G("memset", dict(ap=kTe[64:70, :, :], constant=1.0), w=["kTe_ext"])
                wf = [sb(pd, f"wf{i}", [128, 8, 512], BF16) for i in range(2)]
                qs = [sb(pd, f"qs{i}", [128, 512]) for i in range(2)]
                sq = sb(pd, "sqD", [128, 512])
                qtm = [sb(pd, f"qtm{i}", [128, 512], BF16) for i in range(2)]
                ssd = [sb(pd, f"ssd{i}", [128, 24]) for i in range(2)]
                pq = [ps(pd, f"pq{i}") for i in range(2)]
                ptq = [ps(pd, f"ptq{i}", BF16) for i in range(2)]
                pgf = ps(pd, "pgf")
                lfd = sb(pd, "lfd", [128, NBLK, 8])
                cnd = sb(pd, "cnd", [128, NBLK, 8])
                tot = sb(pd, "tot", [128, NBLK + 1, 8])
                for i in range(NBLK):
                    c0 = 128 * i
                    for dc in range(8):
                        T("matmul", dict(out=pgf[:, 8 * i:8 * i + 8], lhsT=hnT[:, dc, c0:c0 + 128], rhs=wgate[:, dc, 8:16], start=(dc == 0), stop=(dc == 7)), HN_ALL + ["wgate"], ["pgf"])
                pgv = pgf[:, 0:8 * NBLK].rearrange("p (b h) -> p b h", h=8)
                V("tensor_tensor", dict(out=lfd[:], in0=pgv, in1=bffb[:, :].unsqueeze(1).broadcast_to([128, NBLK, 8]), op=ALU.add), ["pgf", "bffb"], ["lfd"])
                A("activation", dict(out=lfd[:], in_=lfd[:], func=AF.Exp, scale=-1.0), ["lfd"], ["lfd"])
                A("activation", dict(out=lfd[:], in_=lfd[:], func=AF.Ln, bias=onec[:, :], scale=1.0), ["lfd", "onec"], ["lfd"])
                lf2 = lfd[:, :, :].rearrange("p b h -> p (b h)")
                T("matmul", dict(out=pgf[:, 256:256 + 8 * NBLK], lhsT=causf[:, :], rhs=lf2, start=True, stop=True), ["lfd", "causf"], ["pgf"])
                V("tensor_copy", dict(out=cnd[:, :, :].rearrange("p b h -> p (b h)"), in_=pgf[:, 256:256 + 8 * NBLK]), ["pgf"], ["cnd"])
                T("matmul", dict(out=pgf[:, 256:256 + 8 * NBLK], lhsT=onesf[:, :], rhs=lf2, start=True, stop=True), ["lfd", "onesf", "cnd"], ["pgf"])
                G("memset", dict(ap=tot[:, 0, :], constant=0.0), w=["tot"])
                for i in range(NBLK):
                    V("tensor_tensor", dict(out=tot[:, i + 1, :], in0=tot[:, i, :], in1=pgf[:, 256 + 8 * i:256 + 8 * i + 8], op=ALU.add), ["tot", "pgf"], ["tot"])
                V("tensor_tensor", dict(out=cnd[:], in0=cnd[:], in1=tot[:, 0:NBLK, :], op=ALU.add), ["cnd", "tot"], ["cnd"])
                parts = sb(pd, "parts", [128, NBLK, 48], BF16)
                r1 = sb(pd, "r1d", [128, NBLK, 8])
                pv3 = lambda a, b_: parts[:, :, a:b_]
                V("tensor_copy", dict(out=pv3(0, 8), in_=cnd[:]), ["cnd"], ["parts0"])
                V("tensor_tensor", dict(out=r1[:], in0=cnd[:], in1=pv3(0, 8), op=ALU.subtract), ["cnd", "parts0"], ["r1d"])
                V("tensor_copy", dict(out=pv3(8, 16), in_=r1[:]), ["r1d"], ["parts1"])
                V("tensor_tensor", dict(out=r1[:], in0=r1[:], in1=pv3(8, 16), op=ALU.subtract), ["r1d", "parts1"], ["r1d"])
                V("tensor_copy", dict(out=pv3(16, 24), in_=r1[:]), ["r1d"], ["parts2"])
                V("tensor_scalar", dict(out=pv3(24, 48), in0=pv3(0, 24), scalar1=-1.0, scalar2=None, op0=ALU.mult), ["parts0", "parts1", "parts2"], ["parts3"])
                PK = ["parts0", "parts1", "parts2", "parts3"]
                for i in range(NBLK):
                    tpb = ptq[i % 2]
                    T("transpose", dict(out=tpb[0:48, 0:128], in_=parts[:, i, :], identity=identb[:, :]), PK + ["identb"], [f"ptq{i % 2}"])
                    A("activation", dict(out=prow[:, 128 * i:128 * i + 128], in_=tpb[0:48, 0:128], func=AF.Copy), [f"ptq{i % 2}"], ["prow"])
                for h in range(8):
                    for j in range(3):
                        P.dma("sync", "ext", dict(out=kTe[67 + j:68 + j, h, :], in_=prow[8 * j + h:8 * j + h + 1, :]), ["prow", "kTe_ext"], [("kTe_ext", h, j)])
                        P.dma("sync", "ext", dict(out=qTe[64 + j:65 + j, h, :], in_=prow[24 + 8 * j + h:24 + 8 * j + h + 1, :]), ["prow", "qTe_ext"], [("qTe_ext", h, j)])
                wsrc = [(2056, "q"), (2568, "k"), (3080, "v")]
                sq2 = [sq, sb(pd, "sqD2", [128, 512])]

                def gen_blk(wfb, kw_, kind, i, b):
                    c0 = 128 * i
                    pqb, qsb, qtb, ssb, tpb, sqb = pq[b], qs[b], qtm[b], ssd[b], ptq[b], sq2[b]
                    for dc in range(8):
                        T("matmul", dict(out=pqb[:, :], lhsT=hnT[:, dc, c0:c0 + 128], rhs=wfb[:, dc, :], start=(dc == 0), stop=(dc == 7)), HN_ALL + [kw_], [f"pq{b}"])
                        yield
                    if kind == "v":
                        A("activation", dict(out=fvx[:, i, :, 0:64], in_=pqb[:, :].rearrange("p (h d) -> p h d", h=8), func=AF.Copy), [f"pq{b}", "fvx"], [("fvx", i)])
                        yield
                        return
                    gb_ = gq_b if kind == "q" else gk_b
                    A("activation", dict(out=qsb[:, :], in_=pqb[:, :], func=AF.Copy), [f"pq{b}"], [f"qs{b}"])
                    yield
                    V("tensor_tensor", dict(out=sqb[:, :], in0=qsb[:, :], in1=qsb[:, :], op=ALU.mult), [f"qs{b}"], [f"sqD{b}"])
                    yield
                    V("tensor_reduce", dict(out=ssb[:, 0:8], in_=sqb[:, :].rearrange("p (h d) -> p h d", h=8), axis=AX.X, op=ALU.add), [f"sqD{b}"], [f"ssd{b}a"])
                    yield
                    rstd_from_ss(ssb[:, 0:8], ssb[:, 16:24], 64.0, 128, [f"ssd{b}a"], [f"ssd{b}b", f"ssd{b}c"], ssb[:, 8:16])
                    yield
                    q3 = qsb[:, :].rearrange("p (h d) -> p h d", h=8)
                    V("tensor_tensor", dict(out=q3, in0=q3, in1=ssb[:, 16:24].unsqueeze(2).broadcast_to([128, 8, 64]), op=ALU.mult), [f"qs{b}", f"ssd{b}c"], [f"qs{b}"])
                    yield
                    V("tensor_tensor", dict(out=qtb[:, :].rearrange("p (h d) -> p h d", h=8), in0=q3, in1=gb_[:, :].unsqueeze(1).broadcast_to([128, 8, 64]), op=ALU.mult), [f"qs{b}", "gq_b", "gk_b"], [f"qtm{b}"])
                    yield
                    for h in range(8):
                        T("transpose", dict(out=tpb[0:64, 128 * h:128 * h + 128], in_=qtb[:, 64 * h:64 * h + 64], identity=identb[:, :]), [f"qtm{b}", "identb"], [f"ptq{b}"])
                        yield
                    dst = qTe if kind == "q" else kTe
                    A("activation", dict(out=dst[0:64, :, c0:c0 + 128], in_=tpb[0:64, :].rearrange("p (h t) -> p h t", h=8), func=AF.Copy), [f"ptq{b}"], [(kind + "T", i)])
                    yield

                def run_rr2(gens):
                    gens = list(gens)
                    while gens:
                        alive = []
                        for g_ in gens:
                            try:
                                next(g_)
                                alive.append(g_)
                            except StopIteration:
                                pass
                        gens = alive

                for wi, (wc0, kind) in enumerate(wsrc):
                    wfb = wf[wi % 2]
                    kw_ = f"wf{wi % 2}"
                    P.dma("gpsimd", kw_, dict(out=wfb[:], in_=w_in[:, wc0:wc0 + 512].rearrange("(dc p) c -> p dc c", p=128)), writes=[kw_])
                    for i in range(0, NBLK, 2):
                        gl = [gen_blk(wfb, kw_, kind, i, 0)]
                        if i + 1 < NBLK:
                            gl.append(gen_blk(wfb, kw_, kind, i + 1, 1))
                        run_rr2(gl)
                P.barrier(exclude=VEX)
                P.emit()
            dump("qTe", qTe[0:70, :, 0:L], [], [70, 8, L])
            dump("kTe", kTe[0:70, :, 0:L], [], [70, 8, L])
            if stop == "D1":
                P.barrier(exclude=VEX)
                P.emit()
                return nc, dbg_out
            with ExitStack() as pd2:
                pT = [sb(pd2, f"pT{i}", [128, 512], BF16) for i in range(3)]
                ytf = [sb(pd2, f"ytf{i}", [128, 4, 512], BF16) for i in range(2)]
                rinv = [sb(pd2, f"rinv{i}", [128, 4]) for i in range(2)]
                lg = [ps(pd2, f"lg{i}") for i in range(2)]
                pacc = [ps(pd2, f"pacc{i}") for i in range(4)]
                ptr2 = [ps(pd2, f"ptr2{i}", BF16) for i in range(2)]
                it = 0
                for TS in range(5):
                    jl = [j for j in range(4 * TS, min(4 * TS + 4, NBLK))]
                    nj = len(jl)
                    t0 = 512 * TS
                    WT = 128 * nj
                    ytb = ytf[TS % 2]
                    for h in range(8):
                        rb = rinv[h % 2]
                        nI = jl[-1] + 1

                        def qk(i, itv):
                            b2 = itv % 2
                            ts_ = max(t0, 128 * i)
                            Wd = t0 + WT - ts_
                            T("matmul", dict(out=lg[b2][:, 0:Wd], lhsT=kTe[0:70, h, 128 * i:128 * i + 128], rhs=qTe[0:70, h, ts_:ts_ + Wd], start=True, stop=True), [], [f"lg{b2}"])

                        qk(0, it)
                        for i in range(nI):
                            b3 = it % 3
                            b2 = it % 2
                            ts_ = max(t0, 128 * i)
                            Wd = t0 + WT - ts_
                            A("activation", dict(out=pT[b3][:, 0:Wd], in_=lg[b2][:, 0:Wd], func=AF.Exp), [f"lg{b2}"], [f"pT{b3}"])
                            if 128 * i >= t0:
                                G("tensor_tensor", dict(out=pT[b3][:, 0:128], in0=pT[b3][:, 0:128], in1=causb[:, :], op=ALU.mult), [f"pT{b3}", "causb"], [f"pT{b3}"])
                            if i + 1 < nI:
                                qk(i + 1, it + 1)
                            for j in jl:
                                if j < i:
                                    continue
                                jj = j - 4 * TS
                                co = 128 * j - ts_
                                T("matmul", dict(out=pacc[jj][:, 0:65], lhsT=pT[b3][:, co:co + 128], rhs=fvx[:, i, h, :], start=(i == 0), stop=(i == j)), [f"pT{b3}"], [f"pacc{jj}"])
                            it += 1
                        for jj in range(nj):
                            V("reciprocal", dict(out=rb[:, jj:jj + 1], in_=pacc[jj][:, 64:65]), [f"pacc{jj}"], [(f"rinv{h % 2}", jj)])
                            V("tensor_scalar", dict(out=ytb[:, jj, 64 * h:64 * h + 64], in0=pacc[jj][:, 0:64], scalar1=rb[:, jj:jj + 1], scalar2=None, op0=ALU.mult), [f"pacc{jj}", (f"rinv{h % 2}", jj)], [(f"ytf{TS % 2}", h)])
                    for jj, j in enumerate(jl):
                        tb = ptr2[jj % 2]
                        for pr in range(4):
                            T("transpose", dict(out=tb[:, 128 * pr:128 * pr + 128], in_=ytb[:, jj, 128 * pr:128 * pr + 128], identity=identb[:, :]), [(f"ytf{TS % 2}", hh) for hh in range(8)] + ["identb"], [f"ptr2{jj % 2}"])
                        A("activation", dict(out=yT[:, 4:8, 128 * j:128 * j + 128], in_=tb[:, 0:512].rearrange("p (a t) -> p a t", a=4), func=AF.Copy), [f"ptr2{jj % 2}"], [("yTf", j)])
                P.barrier(exclude=VEX)
                P.emit()
            dump("yT", yT[:, :, 0:L], [], [128, 8, L])
            if stop == "D":
                P.barrier(exclude=VEX)
                P.emit()
                return nc, dbg_out


        with ExitStack() as pf:
            wo16 = sb(pf, "wo16", [128, 8, D], BF16)
            wq16 = sb(pf, "wq16", [128, 8, 2048], BF16)
            keysT = sb(pf, "keysT", [128, 16, 128])
            P.dma("gpsimd", "wo16", dict(out=wo16[:], in_=w_out.rearrange("(cc p) d -> p cc d", p=128)), writes=["wo16"])
            for hf in range(2):
                P.dma("gpsimd", "wq16", dict(out=wq16[:, :, 1024 * hf:1024 * hf + 1024], in_=peer_query[:, 1024 * hf:1024 * hf + 1024].rearrange("(dc p) c -> p dc c", p=128)), writes=["wq16"])
            pacc = [ps(pf, f"paccF{i}") for i in range(2)]
            ptx = ps(pf, "ptx", BF16)
            pS = ps(pf, "pS")
            pa = [ps(pf, f"pa{i}") for i in range(2)]
            pgt = [ps(pf, f"pgt{i}") for i in range(2)]
            with ExitStack() as pk:
                knat = sb(pk, "knat", [128, 16, 128])
                P.dma("sync", "knat", dict(out=knat[:], in_=peer_keys.rearrange("(hp n) c -> n hp c", n=128)), writes=["knat"])
                for hp in range(16):
                    T("transpose", dict(out=pS[:, 128 * (hp % 4):128 * (hp % 4) + 128], in_=knat[:, hp, :], identity=identf[:, :]), ["knat", "identf"], ["pS"])
                    if hp % 4 == 3:
                        A("activation", dict(out=keysT[:, hp - 3:hp + 1, :], in_=pS[:, :].rearrange("p (a n) -> p a n", a=4), func=AF.Copy), ["pS"], ["keysT"])
                P.barrier(exclude=VEX)
                P.emit()
            xt2 = sb(pf, "xt2", [128, D])
            h2t = sb(pf, "h2t", [128, D])
            xn16 = sb(pf, "xn16", [128, D], BF16)
            xnT = sb(pf, "xnT", [128, 8, 128], BF16)
            scr16 = sb(pf, "scr16", [128, 2048])
            qT_sb = scr16[:, 0:2048].rearrange("p (a t) -> p a t", a=16)
            apre = scr16[:, :].rearrange("p (s e) -> p s e", s=4)
            S_sb = sb(pf, "S_sb", [128, 16, 128])
            v1 = sb(pf, "v1", [128, 16, 16])
            wk = [sb(pf, f"wk{i}", [128, 128]) for i in range(4)]
            cand = [sb(pf, f"cand{i}", [128, 256]) for i in range(2)]
            wk2 = [sb(pf, f"wk2{i}", [128, 256]) for i in range(2)]
            ct = sb(pf, "ct", [128, 8, 16])
            sv = sb(pf, "sv", [128, 64])
            dd = sb(pf, "dd", [128, 8, 16])
            S1t = sb(pf, "S1t", [128, 8, 128])
            UT = [sb(pf, f"UT{i}", [128, 8, 512], BF16) for i in range(2)]
            Vt = [sb(pf, f"Vt{i}", [128, 4, D], BF16) for i in range(2)]
            gA = sb(pf, "gA", [128, 4, 512], BF16)
            tmps = [[sb(pf, f"tmp{p_}_{i}", [128, 512], BF16) for i in range(8)] for p_ in range(2)]
            NZ = 6
            zt = [sb(pf, f"zt{i}", [128, 4, 128]) for i in range(NZ)]
            ez = [sb(pf, f"ez{i}", [128, 512], BF16) for i in range(NZ)]
            wT = [sb(pf, f"wT{i}", [128, 4, 128], BF16) for i in range(2)]
            ADDENG = os.environ.get("ADDENG", "PPPAPPPA")
            FH = [h for h in range(8) if ADDENG[h] == "F"]
            S2e = sb(pf, "S2e", [128, max(1, len(FH)), 128 if FH else 1])
            EH = [h for h in range(8) if ADDENG[h] == "E"]
            SBe = sb(pf, "SBe", [128, max(1, len(EH)), 128 if EH else 1])
            ezf = [sb(pf, f"ezf{i}", [128, 512 if EH else 1]) for i in range(2)]
            DH = [h for h in range(8) if ADDENG[h] == "D"]
            E1d = sb(pf, "E1d", [128, max(1, len(DH)), 128 if DH else 1])
            E2d = sb(pf, "E2d", [128, max(1, len(DH)), 128 if DH else 1])
            Rn = sb(pf, "Rn", [128, max(1, len(FH)), 128 if FH else 1])
            import os
            NT = int(os.environ.get("NTILE", 16))
            zi = 0
            ei = [0]
            for j in range(NT):
                c0 = 16 + 128 * j
                if j == 0:
                    P.dma("sync", "xt2", dict(out=xt2[:, :], in_=x[0:128, :]), writes=["xt2"])
                for hf in range(2):
                    for cc in range(8):
                        T("matmul", dict(out=pacc[hf][:, :], lhsT=yT[:, cc, c0:c0 + 128], rhs=wo16[:, cc, 512 * hf:512 * hf + 512], start=(cc == 0), stop=(cc == 7)), ["wo16"], [f"paccF{hf}"])
                    V("tensor_tensor", dict(out=h2t[:, 512 * hf:512 * hf + 512], in0=pacc[hf][:, :], in1=xt2[:, 512 * hf:512 * hf + 512], op=ALU.add), [f"paccF{hf}", "xt2"], [("h2t", hf)])
                H2 = [("h2t", 0), ("h2t", 1)]
                if j + 1 < NT:
                    P.dma("sync", "xt2", dict(out=xt2[:, :], in_=x[128 * (j + 1):128 * (j + 2), :]), ["xt2"], ["xt2"])
                if debug is not None and "h2" in debug:
                    if j == 0:
                        dbg_h2 = nc.dram_tensor("dbg_h2", [128, 16, D], F32, kind="ExternalOutput").ap()
                        dbg_out["h2"] = dbg_h2
                    P.dma("sync", "dbgh2", dict(out=dbg_h2[:, j, :], in_=h2t[:, :]), H2, ["dbgh2"])
                V("scalar_tensor_tensor", dict(out=xn16[:, :], in0=h2t[:, :], scalar=1.0, in1=h2t[:, :], op0=ALU.mult, op1=ALU.mult, accum_out=sv[:, 0:1]), H2, ["xn16", ("sv", "ss")])
                rstd_from_ss(sv[:, 0:1], sv[:, 2:3], float(D), 128, [("sv", "ss")], [("sv", "ln"), ("sv", "rstd")], sv[:, 1:2])
                V("scalar_tensor_tensor", dict(out=xn16[:, :], in0=h2t[:, :], scalar=sv[:, 2:3], in1=gffn_b[:, :], op0=ALU.mult, op1=ALU.mult), H2 + [("sv", "rstd"), "gffn_b"], ["xn16"])
                for dc in range(8):
                    T("transpose", dict(out=ptx[:, 128 * dc:128 * dc + 128], in_=xn16[:, 128 * dc:128 * dc + 128], identity=identb[:, :]), ["xn16", "identb"], ["ptx"])
                A("activation", dict(out=xnT[:, :, :], in_=ptx[:, :].rearrange("p (a t) -> p a t", a=8), func=AF.Copy), ["ptx"], ["xnT"])
                rot = [(pS, "pS"), (pa[0], "pa0"), (pa[1], "pa1")]
                for g4 in range(4):
                    pb_, pk_ = rot[g4 % 3]
                    for cq in range(4):
                        cc = 4 * g4 + cq
                        for dc in range(8):
                            T("matmul", dict(out=pb_[:, 128 * cq:128 * cq + 128], lhsT=wq16[:, dc, 128 * cc:128 * cc + 128], rhs=xnT[:, dc, :], start=(dc == 0), stop=(dc == 7)), ["wq16", "xnT"], [pk_])
                    A("activation", dict(out=qT_sb[:, 4 * g4:4 * g4 + 4, :], in_=pb_[:, :].rearrange("p (a t) -> p a t", a=4), func=AF.Copy), [pk_], [("scr", g4)])
                for g4 in range(4):
                    pb_, pk_ = rot[(g4 + 1) % 3]
                    for hq in range(4):
                        hp = 4 * g4 + hq
                        T("matmul", dict(out=pb_[:, 128 * hq:128 * hq + 128], lhsT=qT_sb[:, hp, :], rhs=keysT[:, hp, :], start=True, stop=True), [("scr", g4), "keysT"], [pk_])
                    A("activation", dict(out=S_sb[:, 4 * g4:4 * g4 + 4, :], in_=pb_[:, :].rearrange("p (a t) -> p a t", a=4), func=AF.Copy), [pk_], [("S_sb", g4)])
                SK = [("S_sb", g) for g in range(4)]
                for hp0 in range(0, 16, 4):
                    grp = range(hp0, hp0 + 4)
                    for hp in grp:
                        V("max", dict(out=v1[:, hp, 0:8], in_=S_sb[:, hp, :]), SK, [("v1", hp, 0)])
                    for hp in grp:
                        V("match_replace", dict(out=wk[hp % 4][:, :], in_to_replace=v1[:, hp, 0:8], in_values=S_sb[:, hp, :], imm_value=NEG), SK + [("v1", hp, 0)], [f"wk{hp % 4}"])
                    for hp in grp:
                        V("max", dict(out=v1[:, hp, 8:16], in_=wk[hp % 4][:, :]), [f"wk{hp % 4}"], [("v1", hp, 1)])
                for h0 in range(0, 8, 2):
                    grp = range(h0, h0 + 2)
                    for h in grp:
                        V("tensor_tensor", dict(out=cand[h % 2][:, :].rearrange("p (a b) -> p a b", a=16), in0=v1[:, 2 * h, :].unsqueeze(2).broadcast_to([128, 16, 16]), in1=v1[:, 2 * h + 1, :].unsqueeze(1).broadcast_to([128, 16, 16]), op=ALU.add),
                          [("v1", 2 * h, 0), ("v1", 2 * h, 1), ("v1", 2 * h + 1, 0), ("v1", 2 * h + 1, 1)], [f"cand{h % 2}"])
                    for h in grp:
                        V("max", dict(out=ct[:, h, 0:8], in_=cand[h % 2][:, :]), [f"cand{h % 2}"], [("ct", h, 0)])
                    for h in grp:
                        V("match_replace", dict(out=wk2[h % 2][:, :], in_to_replace=ct[:, h, 0:8], in_values=cand[h % 2][:, :], imm_value=NEG), [f"cand{h % 2}", ("ct", h, 0)], [f"wk2{h % 2}"])
                    for h in grp:
                        V("max", dict(out=ct[:, h, 8:16], in_=wk2[h % 2][:, :]), [f"wk2{h % 2}"], [("ct", h, 1)])
                CTK = [("ct", h, k_) for h in range(8) for k_ in range(2)]
                tau = ct[:, :, 15]
                mx_ = ct[:, :, 0]
                V("tensor_tensor", dict(out=dd[:, :, :], in0=ct[:, :, :], in1=ct[:, :, 0:1].broadcast_to([128, 8, 16]), op=ALU.subtract), CTK, ["dd"])
                A("activation", dict(out=dd[:, :, :], in_=dd[:, :, :], func=AF.Exp), ["dd"], ["dd"])
                V("tensor_reduce", dict(out=sv[:, 8:16], in_=dd[:, :, :], axis=AX.X, op=ALU.add), ["dd"], [("sv", "Z")])
                A("activation", dict(out=sv[:, 16:24], in_=sv[:, 8:16], func=AF.Ln), [("sv", "Z")], [("sv", "lnZ")])
                V("tensor_tensor", dict(out=sv[:, 24:32], in0=sv[:, 16:24], in1=mx_, op=ALU.add), [("sv", "lnZ")] + CTK, [("sv", "cst")])
                V("tensor_tensor", dict(out=sv[:, 32:40], in0=tau, in1=sv[:, 24:32], op=ALU.subtract), [("sv", "cst")] + CTK, [("sv", "bias")])
                S4 = S_sb[:, :, :].rearrange("p (h two) n -> p h two n", two=2)
                V("tensor_tensor", dict(out=S1t[:, :, :], in0=S4[:, :, 0, :], in1=ct[:, :, 15:16].broadcast_to([128, 8, 128]), op=ALU.subtract), SK + CTK, ["S1t"])
                V("tensor_scalar", dict(out=S1t[:, :, :], in0=S1t[:, :, :], scalar1=8.0e-6, scalar2=None, op0=ALU.add), ["S1t"], ["S1t"])
                for k_, h in enumerate(FH):
                    V("tensor_scalar", dict(out=S2e[:, k_, :], in0=S_sb[:, 2 * h + 1, :], scalar1=sv[:, 32 + h:33 + h], scalar2=None, op0=ALU.add), SK + [("sv", "bias")], ["S2e"])
                    V("tensor_scalar", dict(out=Rn[:, k_, :], in0=S1t[:, h, :], scalar1=-1.0, scalar2=None, op0=ALU.mult), ["S1t"], ["Rn"])
                if EH:
                    A("activation", dict(out=sv[:, 40:48], in_=sv[:, 32:40], func=AF.Exp), [("sv", "bias")], [("sv", "thr")])
                for k_, h in enumerate(EH):
                    V("tensor_scalar", dict(out=SBe[:, k_, :], in0=S1t[:, h, :], scalar1=sv[:, 32 + h:33 + h], scalar2=None, op0=ALU.add), ["S1t", ("sv", "bias")], ["SBe"])
                for k_, h in enumerate(DH):
                    A("activation", dict(out=E1d[:, k_, :], in_=S1t[:, h, :], func=AF.Exp), ["S1t"], ["E1d"])
                    A("activation", dict(out=E2d[:, k_, :], in_=S_sb[:, 2 * h + 1, :], func=AF.Exp, bias=sv[:, 32 + h:33 + h], scale=1.0), SK + [("sv", "bias")], ["E2d"])
                NIB = 32
                NDUM = int(os.environ.get("NDUM", 0))

                def loadUT(ibp):
                    b_ = ibp % 2
                    P.dma("sync", f"UT{b_}", dict(out=UT[b_][:], in_=ut_s[:, :, 512 * ibp:512 * ibp + 512].rearrange("dc p e -> p dc e")), writes=[f"UT{b_}"])

                def loadVt(ib):
                    b_ = ib % 2
                    P.dma("sync", f"Vt{b_}", dict(out=Vt[b_][:], in_=vb_s[512 * ib:512 * ib + 512, :].rearrange("(q p) d -> p q d", p=128)), ["vb_s"], [f"Vt{b_}"])

                def computeA(ibp):
                    b_ = ibp % 2
                    for q in range(4):
                        for dc in range(8):
                            T("matmul", dict(out=pa[b_][:, 128 * q:128 * q + 128], lhsT=UT[b_][:, dc, 128 * q:128 * q + 128], rhs=xnT[:, dc, :], start=(dc == 0), stop=(dc == 7)), [f"UT{b_}", "xnT"], [f"pa{b_}"])

                def evacA(ibp):
                    b_ = ibp % 2
                    A("activation", dict(out=apre[:, ibp % 4, :], in_=pa[b_][:, :], func=AF.Copy), [f"pa{b_}"], [("scr", ibp % 4)])

                def gelu_burst(ibs):
                    for ibp in ibs:
                        A("activation", dict(out=gA[:, ibp % 4, :], in_=apre[:, ibp % 4, :], func=AF.Gelu), [("scr", ibp % 4)], [("gA", ibp % 4)])

                def stageG(ib):
                    nonlocal zi
                    p_ = ib % 2
                    for h in range(8):
                        zb = zi % NZ
                        zi += 1
                        ae = ADDENG[h]
                        if ae == "E":
                            k_ = EH.index(h)
                            eb = ei[0] % 2
                            ei[0] += 1
                            for q in range(4):
                                A("activation", dict(out=ezf[eb][:, 128 * q:128 * q + 128], in_=S_sb[:, 2 * h + 1, :], func=AF.Exp, bias=SBe[:, k_, 4 * ib + q:4 * ib + q + 1], scale=1.0), SK + ["SBe"], [f"ezf{eb}"])
                            V("scalar_tensor_tensor", dict(out=tmps[p_][h][:, :], in0=ezf[eb][:, :], scalar=sv[:, 40 + h:41 + h], in1=ezf[eb][:, :], op0=ALU.is_ge, op1=ALU.mult), [f"ezf{eb}", ("sv", "thr")], [f"tmp{p_}_{h}"])
                            continue
                        if ae == "F":
                            k_ = FH.index(h)
                            for q in range(4):
                                A("activation", dict(out=ez[zb][:, 128 * q:128 * q + 128], in_=S2e[:, k_, :], func=AF.Exp, bias=S1t[:, h, 4 * ib + q:4 * ib + q + 1], scale=1.0), ["S2e", "S1t"], [f"ez{zb}"])
                            for q in range(4):
                                V("scalar_tensor_tensor", dict(out=tmps[p_][h][:, 128 * q:128 * q + 128], in0=S_sb[:, 2 * h + 1, :], scalar=Rn[:, k_, 4 * ib + q:4 * ib + q + 1], in1=ez[zb][:, 128 * q:128 * q + 128], op0=ALU.is_ge, op1=ALU.mult), SK + ["Rn", f"ez{zb}"], [f"tmp{p_}_{h}"])
                            continue
                        if ae == "A":
                            for q in range(4):
                                A("activation", dict(out=zt[zb][:, q, :], in_=S_sb[:, 2 * h + 1, :], func=AF.Identity, bias=S1t[:, h, 4 * ib + q:4 * ib + q + 1], scale=1.0), SK + ["S1t"], [f"zt{zb}"])
                        else:
                            addeng = G if ae in ("P", "D") else V
                            addeng("tensor_tensor", dict(out=zt[zb][:, :, :], in0=S_sb[:, 2 * h + 1, :].unsqueeze(1).broadcast_to([128, 4, 128]), in1=S1t[:, h, 4 * ib:4 * ib + 4].unsqueeze(2).broadcast_to([128, 4, 128]), op=ALU.add), SK + ["S1t"], [f"zt{zb}"])
                        if ae == "D":
                            k_ = DH.index(h)
                            V("tensor_tensor", dict(out=ez[zb][:, :].rearrange("p (a n) -> p a n", a=4), in0=E2d[:, k_, :].unsqueeze(1).broadcast_to([128, 4, 128]), in1=E1d[:, k_, 4 * ib:4 * ib + 4].unsqueeze(2).broadcast_to([128, 4, 128]), op=ALU.mult), ["E1d", "E2d"], [f"ez{zb}"])
                        else:
                            A("activation", dict(out=ez[zb][:, :], in_=zt[zb][:, :, :].rearrange("p a n -> p (a n)"), func=AF.Exp, bias=sv[:, 32 + h:33 + h], scale=1.0), [f"zt{zb}", ("sv", "bias")], [f"ez{zb}"])
                        V("scalar_tensor_tensor", dict(out=tmps[p_][h][:, :], in0=zt[zb][:, :, :].rearrange("p a n -> p (a n)"), scalar=0.0, in1=ez[zb][:, :], op0=ALU.is_ge, op1=ALU.mult), [f"zt{zb}", f"ez{zb}"], [f"tmp{p_}_{h}"])

                def stageT(ib):
                    p_ = ib % 2
                    for _d in range(NDUM):
                        T("matmul", dict(out=pS[:, :], lhsT=identb[:, :], rhs=gA[:, _d % 4, :], start=True, stop=True), [], ["pSdummy"])
                    for q in range(4):
                        for h in range(8):
                            T("matmul", dict(out=pgt[p_][:, 128 * q:128 * q + 128], lhsT=tmps[p_][h][:, 128 * q:128 * q + 128], rhs=identb[:, :], start=(h == 0), stop=(h == 7)), [f"tmp{p_}_{h}", "identb"], [f"pgt{p_}"])

                def stageW(ib):
                    p_ = ib % 2
                    V("tensor_tensor", dict(out=wT[p_][:, :, :].rearrange("p a t -> p (a t)"), in0=pgt[p_][:, :], in1=gA[:, ib % 4, :], op=ALU.mult), [f"pgt{p_}", ("gA", ib % 4)], [f"wT{p_}"])
                    for q in range(4):
                        for hf in range(2):
                            T("matmul", dict(out=pacc[hf][:, :], lhsT=wT[p_][:, q, :], rhs=Vt[p_][:, q, 512 * hf:512 * hf + 512], start=(ib == 0 and q == 0), stop=(ib == NIB - 1 and q == 3)), [f"wT{p_}", f"Vt{p_}"], [f"paccF{hf}"])

                loadUT(0)
                loadUT(1)
                computeA(0)
                evacA(0)
                loadUT(2)
                computeA(1)
                evacA(1)
                for k in range(NIB + 2):
                    if k + 3 < NIB:
                        loadUT(k + 3)
                    if 0 <= k - 1 < NIB:
                        loadVt(k - 1)
                    if k % 4 == 2:
                        gelu_burst(range(k - 2, min(k + 2, NIB)))
                    if 0 <= k - 1 < NIB:
                        stageT(k - 1)
                    if 0 <= k - 2 < NIB:
                        stageW(k - 2)
                    if k + 2 < NIB:
                        computeA(k + 2)
                    if k < NIB:
                        stageG(k)
                    if k + 2 < NIB:
                        evacA(k + 2)
                SCR = [("scr", s_) for s_ in range(4)]
                for hf in range(2):
                    V("tensor_tensor", dict(out=scr16[:, 512 * hf:512 * hf + 512], in0=pacc[hf][:, :], in1=h2t[:, 512 * hf:512 * hf + 512], op=ALU.add), [f"paccF{hf}"] + H2, SCR)
                P.dma("sync", "ost", dict(out=out[128 * j:128 * j + 128, :], in_=scr16[:, 0:1024]), SCR, SCR + [("out", j)])
            P.barrier()
            P.emit()
        P.barrier()
        P.emit()
    return nc, dbg_out

_CACHE = {}


def kernel(**inputs):
    n = 8
    if "nc" not in _CACHE:
        _CACHE["nc"] = build_nc()[0]
    nc = _CACHE["nc"]
    f = lambda a: np.ascontiguousarray(np.asarray(a, dtype=np.float32))
    x = f(inputs["x"])
    shared = {
        "meta_tokens": f(inputs["meta_tokens"]),
        "norm_mix": f(inputs["norm_mix"]).reshape(1, D),
        "w_in": f(inputs["w_in"]).reshape(D, 3600),
        "conv_qk": f(inputs["conv_qk"]).reshape(4, 1024),
        "b_igate": f(inputs["b_igate"]).reshape(1, 4),
        "b_fgate_m": f(inputs["b_fgate_m"]).reshape(1, 4),
        "m_out_norm": f(inputs["m_out_norm"]).reshape(1, 512),
        "b_fgate_f": f(inputs["b_fgate_f"]).reshape(8, 1),
        "f_q_norm": f(inputs["f_q_norm"]).reshape(1, 64),
        "f_k_norm": f(inputs["f_k_norm"]).reshape(1, 64),
        "w_out": f(inputs["w_out"]).reshape(D, D),
        "norm_ffn": f(inputs["norm_ffn"]).reshape(1, D),
        "peer_query": f(inputs["peer_query"]).reshape(D, 2048),
        "peer_sub_keys": f(inputs["peer_sub_keys"]).reshape(2048, 128),
        "peer_u": f(inputs["peer_u"]).reshape(16384, D),
        "peer_v": f(inputs["peer_v"]).reshape(16384, D),
    }
    in_maps = [dict(shared, x=x[b]) for b in range(n)]
    res = run_bass_kernel_spmd(nc, in_maps, core_ids=list(range(n)))
    return np.stack([np.asarray(res.results[b]["out"], dtype=np.float32) for b in range(n)], axis=0)
```
